# Optimizing a Trainium2 kernel written in Bass

```python
import math
import jax
import jax.numpy as jnp
from jax import lax
import numpy as np

D_MODEL = 1024
BATCH = 8
SEQ = 2048
DEPTH = 2

GRID_W = 64
CTX_LEN = 256
EPS = 1e-6
N_DIR = 2
CONV_K = 3

SSD_HEADS = 6
SSD_HEAD_DIM = 64
SSD_GROUPS = 2
SSD_HPG = SSD_HEADS // SSD_GROUPS
SSD_STATE = 64
SSD_CHUNK = 128
SSD_WIDTH = SSD_HEADS * SSD_HEAD_DIM

GDN_HEADS = 4
GDN_HEAD_DIM = 64
GDN_CHUNK = 64
GDN_WIDTH = GDN_HEADS * GDN_HEAD_DIM

ATTN_Q_HEADS = 6
ATTN_KV_HEADS = 2
ATTN_GROUP = ATTN_Q_HEADS // ATTN_KV_HEADS
ATTN_HEAD_DIM = 64
ATTN_WIDTH = ATTN_Q_HEADS * ATTN_HEAD_DIM
ATTN_KV_WIDTH = ATTN_KV_HEADS * ATTN_HEAD_DIM
ATTN_BLOCK = 128
ROPE_THETA = 10000.0

D_MIX = SSD_WIDTH + GDN_WIDTH + ATTN_WIDTH

CONV_SPLITS = (SSD_WIDTH, SSD_GROUPS * SSD_STATE, SSD_GROUPS * SSD_STATE, GDN_WIDTH, GDN_WIDTH, GDN_WIDTH)
CONV_DIM = sum(CONV_SPLITS)
PROJ_SPLITS = (CONV_DIM, SSD_WIDTH, N_DIR * SSD_HEADS, GDN_WIDTH, N_DIR * GDN_HEADS, N_DIR * GDN_HEADS,
               ATTN_WIDTH, ATTN_KV_WIDTH, ATTN_KV_WIDTH, ATTN_WIDTH)
PROJ_DIM = sum(PROJ_SPLITS)

kernel_name = "hybrid_ssd_gdn_attn_dit_block"


def _split(x, sizes):
    idx = np.cumsum(sizes)[:-1].tolist()
    return jnp.split(x, idx, axis=-1)


def _rms(x, w):
    xf = x.astype(jnp.float32)
    y = xf * lax.rsqrt(jnp.mean(xf * xf, axis=-1, keepdims=True) + EPS)
    return (y * w.astype(jnp.float32)).astype(x.dtype)


def _l2norm(x):
    return x * lax.rsqrt(jnp.sum(x * x, axis=-1, keepdims=True) + EPS)


def _dwconv_centred(x, w, b):
    C = x.shape[-1]
    pad = CONV_K // 2
    y = lax.conv_general_dilated(x, w.astype(x.dtype)[:, None, :], window_strides=(1,),
                                 padding=[(pad, pad)], dimension_numbers=('NWC', 'WIO', 'NWC'),
                                 feature_group_count=C)
    return y + b.astype(x.dtype)


def _project(h, w_in, conv_w, conv_b):
    proj = h @ w_in
    (conv_in, ssd_z, ssd_dt, gdn_z, gdn_a, gdn_b, attn_q, attn_k, attn_v, attn_z) = _split(proj, PROJ_SPLITS)
    conv_out = jax.nn.silu(_dwconv_centred(conv_in, conv_w, conv_b))
    ssd_x, ssd_B, ssd_C, gdn_q, gdn_k, gdn_v = _split(conv_out, CONV_SPLITS)
    return dict(ssd_x=ssd_x, ssd_B=ssd_B, ssd_C=ssd_C, ssd_z=ssd_z, ssd_dt=ssd_dt,
                gdn_q=gdn_q, gdn_k=gdn_k, gdn_v=gdn_v, gdn_z=gdn_z, gdn_a=gdn_a, gdn_b=gdn_b,
                attn_q=attn_q, attn_k=attn_k, attn_v=attn_v, attn_z=attn_z)


def _ssd_scan(x, dt, Bm, Cm, A, s0):
    Bsz, L, G, Hg, P = x.shape
    N = Bm.shape[-1]
    Q = SSD_CHUNK
    nc = L // Q
    x = x.reshape(Bsz, nc, Q, G, Hg, P)
    dt = dt.reshape(Bsz, nc, Q, G, Hg)
    Bm = Bm.reshape(Bsz, nc, Q, G, N)
    Cm = Cm.reshape(Bsz, nc, Q, G, N)
    cum = jnp.cumsum(dt * A, axis=2)
    causal = jnp.tril(jnp.ones((Q, Q), bool))[:, :, None, None]
    diff = cum[:, :, :, None] - cum[:, :, None, :]
    Lmat = jnp.exp(jnp.where(causal, diff, -jnp.inf))
    scores = jnp.einsum('bcign,bcjgn->bcijg', Cm, Bm)
    W = scores[..., None] * Lmat * dt[:, :, None]
    y_diag = jnp.einsum('bcijgh,bcjghp->bcighp', W, x)
    decay_to_end = jnp.exp(cum[:, :, -1:] - cum)
    chunk_states = jnp.einsum('bcjgn,bcjgh,bcjghp->bcghpn', Bm, decay_to_end * dt, x)
    chunk_decay = jnp.exp(cum[:, :, -1])

    def step(s, inp):
        st, dec = inp
        return s * dec[..., None, None] + st, s

    s_final, s_in = lax.scan(step, s0, (jnp.moveaxis(chunk_states, 1, 0), jnp.moveaxis(chunk_decay, 1, 0)))
    s_in = jnp.moveaxis(s_in, 0, 1)
    y_off = jnp.einsum('bcign,bcghpn,bcigh->bcighp', Cm, s_in, jnp.exp(cum))
    return (y_diag + y_off).reshape(Bsz, L, G, Hg, P), s_final


def _ssd_bidir(p, A_log, dt_bias, init):
    f32 = jnp.float32
    Bsz, L, _ = p['ssd_x'].shape
    xs = p['ssd_x'].astype(f32).reshape(Bsz, L, SSD_GROUPS, SSD_HPG, SSD_HEAD_DIM)
    Bm = p['ssd_B'].astype(f32).reshape(Bsz, L, SSD_GROUPS, SSD_STATE)
    Cm = p['ssd_C'].astype(f32).reshape(Bsz, L, SSD_GROUPS, SSD_STATE)
    dt_raw = p['ssd_dt'].astype(f32).reshape(Bsz, L, N_DIR, SSD_GROUPS, SSD_HPG)
    dt = jax.nn.softplus(dt_raw + dt_bias.astype(f32).reshape(N_DIR, SSD_GROUPS, SSD_HPG))
    A = -jnp.exp(A_log.astype(f32)).reshape(N_DIR, SSD_GROUPS, SSD_HPG)
    if init is None:
        init = jnp.zeros((N_DIR, Bsz, SSD_GROUPS, SSD_HPG, SSD_HEAD_DIM, SSD_STATE), f32)
    ys, finals = [], []
    for d in range(N_DIR):
        args = (xs, dt[:, :, d], Bm, Cm)
        if d == 1:
            args = tuple(jnp.flip(a, axis=1) for a in args)
        y, s = _ssd_scan(*args, A[d], init[d])
        ys.append(jnp.flip(y, axis=1) if d == 1 else y)
        finals.append(s)
    return ys[0] + ys[1], jnp.stack(finals)


def _ssd_out(y, p, D_skip, norm_w):
    Bsz, L = y.shape[:2]
    xs = p['ssd_x'].astype(jnp.float32).reshape(Bsz, L, SSD_GROUPS, SSD_HPG, SSD_HEAD_DIM)
    y = y + D_skip.astype(jnp.float32).reshape(SSD_GROUPS, SSD_HPG)[..., None] * xs
    y = y.reshape(Bsz, L, SSD_WIDTH).astype(p['ssd_z'].dtype)
    return _rms(y * jax.nn.silu(p['ssd_z']), norm_w)


def _gdn_scan(q, k, v, g, beta, s0):
    Bsz, L, H, Dk = q.shape
    Dv = v.shape[-1]
    C = GDN_CHUNK
    nc = L // C

    def chunks(a):
        return jnp.moveaxis(a.reshape((Bsz, nc, C, H) + a.shape[3:]), 3, 1)

    q = chunks(q * (Dk ** -0.5))
    k = chunks(k)
    v = chunks(v)
    g = chunks(g)
    beta = chunks(beta)
    gc = jnp.cumsum(g, axis=-1)
    lower = jnp.tril(jnp.ones((C, C), bool))
    strict = jnp.tril(jnp.ones((C, C), bool), -1)
    decay = jnp.exp(jnp.where(lower, gc[..., :, None] - gc[..., None, :], -jnp.inf))
    k_beta = k * beta[..., None]
    v_beta = v * beta[..., None]
    tri = jnp.where(strict, jnp.einsum('bhnid,bhnjd->bhnij', k_beta, k) * decay, 0.0) + jnp.eye(C, dtype=q.dtype)
    u = lax.linalg.triangular_solve(tri, v_beta, left_side=True, lower=True, unit_diagonal=True)
    w = lax.linalg.triangular_solve(tri, k_beta * jnp.exp(gc)[..., None], left_side=True, lower=True,
                                    unit_diagonal=True)
    intra = jnp.where(lower, jnp.einsum('bhnid,bhnjd->bhnij', q, k) * decay, 0.0)
    q_dec = q * jnp.exp(gc)[..., None]
    k_dec = k * jnp.exp(gc[..., -1:] - gc)[..., None]
    chunk_decay = jnp.exp(gc[..., -1])

    def step(S, inp):
        qd, kd, u_i, w_i, a_i, d_i = inp
        v_new = u_i - jnp.einsum('bhcd,bhde->bhce', w_i, S)
        o = jnp.einsum('bhcd,bhde->bhce', qd, S) + jnp.einsum('bhij,bhje->bhie', a_i, v_new)
        S = S * d_i[..., None, None] + jnp.einsum('bhcd,bhce->bhde', kd, v_new)
        return S, o

    xs = tuple(jnp.moveaxis(a, 2, 0) for a in (q_dec, k_dec, u, w, intra, chunk_decay))
    s_final, o = lax.scan(step, s0, xs)
    o = jnp.moveaxis(jnp.moveaxis(o, 0, 2), 1, 3).reshape(Bsz, L, H, Dv)
    return o, s_final


def _gdn_bidir(p, A_log, dt_bias, init):
    f32 = jnp.float32
    Bsz, L, _ = p['gdn_q'].shape
    q = _l2norm(p['gdn_q'].astype(f32).reshape(Bsz, L, GDN_HEADS, GDN_HEAD_DIM))
    k = _l2norm(p['gdn_k'].astype(f32).reshape(Bsz, L, GDN_HEADS, GDN_HEAD_DIM))
    v = p['gdn_v'].astype(f32).reshape(Bsz, L, GDN_HEADS, GDN_HEAD_DIM)
    a = p['gdn_a'].astype(f32).reshape(Bsz, L, N_DIR, GDN_HEADS)
    b = p['gdn_b'].astype(f32).reshape(Bsz, L, N_DIR, GDN_HEADS)
    g = -jnp.exp(A_log.astype(f32)) * jax.nn.softplus(a + dt_bias.astype(f32))
    beta = jax.nn.sigmoid(b)
    if init is None:
        init = jnp.zeros((N_DIR, Bsz, GDN_HEADS, GDN_HEAD_DIM, GDN_HEAD_DIM), f32)
    os, finals = [], []
    for d in range(N_DIR):
        args = (q, k, v, g[:, :, d], beta[:, :, d])
        if d == 1:
            args = tuple(jnp.flip(t, axis=1) for t in args)
        o, s = _gdn_scan(*args, init[d])
        os.append(jnp.flip(o, axis=1) if d == 1 else o)
        finals.append(s)
    return os[0] + os[1], jnp.stack(finals)


def _gdn_out(o, p, norm_w):
    Bsz, L = o.shape[:2]
    o = _rms(o, norm_w).reshape(Bsz, L, GDN_WIDTH).astype(p['gdn_z'].dtype)
    return o * jax.nn.silu(p['gdn_z'])


def _rope_tables(pos_row, pos_col):
    n_freq = ATTN_HEAD_DIM // 4
    freqs = ROPE_THETA ** (-jnp.arange(n_freq, dtype=jnp.float32) / n_freq)
    ang_r = pos_row.astype(jnp.float32)[:, None] * freqs
    ang_c = pos_col.astype(jnp.float32)[:, None] * freqs
    return (jnp.cos(ang_r), jnp.sin(ang_r), jnp.cos(ang_c), jnp.sin(ang_c))


def _rope_axis(x, cos, sin):
    F = x.shape[-1] // 2
    x1, x2 = x[..., :F], x[..., F:]
    cos = cos.astype(x.dtype)[:, None]
    sin = sin.astype(x.dtype)[:, None]
    return jnp.concatenate([x1 * cos - x2 * sin, x1 * sin + x2 * cos], axis=-1)


def _rope2d(x, rope):
    cr, sr, cc, sc = rope
    half = x.shape[-1] // 2
    return jnp.concatenate([_rope_axis(x[..., :half], cr, sr), _rope_axis(x[..., half:], cc, sc)], axis=-1)


def _qkv(p, q_norm_w, k_norm_w, rope):
    Bsz, L, _ = p['attn_q'].shape
    q = _rms(p['attn_q'].reshape(Bsz, L, ATTN_Q_HEADS, ATTN_HEAD_DIM), q_norm_w)
    k = _rms(p['attn_k'].reshape(Bsz, L, ATTN_KV_HEADS, ATTN_HEAD_DIM), k_norm_w)
    v = p['attn_v'].reshape(Bsz, L, ATTN_KV_HEADS, ATTN_HEAD_DIM)
    if rope is not None:
        q = _rope2d(q, rope)
        k = _rope2d(k, rope)
    return q.reshape(Bsz, L, ATTN_KV_HEADS, ATTN_GROUP, ATTN_HEAD_DIM), k, v


def _attend(q, k, v):
    Bsz, Lq = q.shape[:2]
    nb = Lq // ATTN_BLOCK
    qb = jnp.moveaxis(q.reshape((Bsz, nb, ATTN_BLOCK) + q.shape[2:]), 1, 0)
    scale = ATTN_HEAD_DIM ** -0.5

    def block(qi):
        s = jnp.einsum('bqhgd,bkhd->bhgqk', qi, k).astype(jnp.float32) * scale
        pr = jax.nn.softmax(s, axis=-1).astype(v.dtype)
        return jnp.einsum('bhgqk,bkhd->bqhgd', pr, v)

    o = lax.map(block, qb)
    return jnp.moveaxis(o, 0, 1).reshape(Bsz, Lq, ATTN_WIDTH)


def _layer(x, xc, c, c_ctx, rope, norm_w, w_mod, b_mod, w_in, conv_w, conv_b, ssd_A_log, ssd_dt_bias,
           ssd_D, ssd_norm_w, gdn_A_log, gdn_dt_bias, gdn_norm_w, q_norm_w, k_norm_w, w_out, update_ctx):
    shift, scale, gate = jnp.split(jax.nn.silu(c) @ w_mod + b_mod, 3, axis=-1)
    shift_c, scale_c, gate_c = jnp.split(jax.nn.silu(c_ctx) @ w_mod + b_mod, 3, axis=-1)
    h = _rms(x, norm_w) * (1.0 + scale[:, None]) + shift[:, None]
    hc = _rms(xc, norm_w) * (1.0 + scale_c) + shift_c
    pl = _project(h, w_in, conv_w, conv_b)
    pc = _project(hc, w_in, conv_w, conv_b)

    ssd_yc, ssd_state = _ssd_bidir(pc, ssd_A_log, ssd_dt_bias, None)
    ssd_yl, _ = _ssd_bidir(pl, ssd_A_log, ssd_dt_bias, ssd_state)
    gdn_oc, gdn_state = _gdn_bidir(pc, gdn_A_log, gdn_dt_bias, None)
    gdn_ol, _ = _gdn_bidir(pl, gdn_A_log, gdn_dt_bias, gdn_state)
    qc, kc, vc = _qkv(pc, q_norm_w, k_norm_w, None)
    ql, kl, vl = _qkv(pl, q_norm_w, k_norm_w, rope)
    attn_l = _attend(ql, jnp.concatenate([kc, kl], axis=1), jnp.concatenate([vc, vl], axis=1))

    mix_l = jnp.concatenate([_ssd_out(ssd_yl, pl, ssd_D, ssd_norm_w),
                             _gdn_out(gdn_ol, pl, gdn_norm_w),
                             attn_l * jax.nn.silu(pl['attn_z'])], axis=-1)
    x = x + gate[:, None] * (mix_l @ w_out)
    if update_ctx:
        attn_c = _attend(qc, kc, vc)
        mix_c = jnp.concatenate([_ssd_out(ssd_yc, pc, ssd_D, ssd_norm_w),
                                 _gdn_out(gdn_oc, pc, gdn_norm_w),
                                 attn_c * jax.nn.silu(pc['attn_z'])], axis=-1)
        xc = xc + gate_c * (mix_c @ w_out)
    return x, xc


def setup_inputs(seed: int = 0) -> dict:
    key = jax.random.key(seed)
    ks = jax.random.split(key, 24)
    f32 = jnp.float32

    def nrm(k, shape, s):
        return jax.random.normal(k, shape, f32) * s

    def dt_bias_init(k, shape):
        dt = jnp.exp(jax.random.uniform(k, shape, f32, math.log(1e-3), math.log(1e-1)))
        return dt + jnp.log(-jnp.expm1(-dt))

    return {
        'x': nrm(ks[0], (BATCH, SEQ, D_MODEL), 1.0),
        'c': nrm(ks[1], (BATCH, D_MODEL), 1.0),
        'ctx': nrm(ks[2], (BATCH, CTX_LEN, D_MODEL), 1.0),
        'c_ctx': nrm(ks[3], (D_MODEL,), 1.0),
        'norm_w': 1.0 + nrm(ks[4], (DEPTH, D_MODEL), 0.02),
        'w_mod': nrm(ks[5], (DEPTH, D_MODEL, 3 * D_MODEL), D_MODEL ** -0.5),
        'b_mod': nrm(ks[6], (DEPTH, 3 * D_MODEL), 0.01),
        'w_in': nrm(ks[7], (DEPTH, D_MODEL, PROJ_DIM), D_MODEL ** -0.5),
        'conv_w': nrm(ks[8], (DEPTH, CONV_K, CONV_DIM), CONV_K ** -0.5),
        'conv_b': nrm(ks[9], (DEPTH, CONV_DIM), 0.01),
        'ssd_A_log': jnp.log(jax.random.uniform(ks[10], (DEPTH, N_DIR, SSD_HEADS), f32, 1.0, 16.0)),
        'ssd_dt_bias': dt_bias_init(ks[11], (DEPTH, N_DIR, SSD_HEADS)),
        'ssd_D': 1.0 + nrm(ks[12], (DEPTH, SSD_HEADS), 0.1),
        'ssd_norm_w': 1.0 + nrm(ks[13], (DEPTH, SSD_WIDTH), 0.02),
        'gdn_A_log': jnp.log(jax.random.uniform(ks[14], (DEPTH, N_DIR, GDN_HEADS), f32, 1.0, 16.0)),
        'gdn_dt_bias': dt_bias_init(ks[15], (DEPTH, N_DIR, GDN_HEADS)),
        'gdn_norm_w': 1.0 + nrm(ks[16], (DEPTH, GDN_HEAD_DIM), 0.02),
        'q_norm_w': 1.0 + nrm(ks[17], (DEPTH, ATTN_HEAD_DIM), 0.02),
        'k_norm_w': 1.0 + nrm(ks[18], (DEPTH, ATTN_HEAD_DIM), 0.02),
        'w_out': nrm(ks[19], (DEPTH, D_MIX, D_MODEL), D_MIX ** -0.5),
    }


def reference(x, c, ctx, c_ctx, norm_w, w_mod, b_mod, w_in, conv_w, conv_b, ssd_A_log, ssd_dt_bias, ssd_D,
              ssd_norm_w, gdn_A_log, gdn_dt_bias, gdn_norm_w, q_norm_w, k_norm_w, w_out):
    L = x.shape[1]
    ROWS = L // GRID_W
    pos_row = jnp.repeat(jnp.arange(ROWS, dtype=jnp.int32), GRID_W)
    pos_col = jnp.tile(jnp.arange(GRID_W, dtype=jnp.int32), ROWS)
    rope = _rope_tables(pos_row, pos_col)
    h, hc = x, ctx
    for i in range(DEPTH):
        h, hc = _layer(h, hc, c, c_ctx, rope, norm_w[i], w_mod[i], b_mod[i], w_in[i], conv_w[i], conv_b[i],
                       ssd_A_log[i], ssd_dt_bias[i], ssd_D[i], ssd_norm_w[i], gdn_A_log[i], gdn_dt_bias[i],
                       gdn_norm_w[i], q_norm_w[i], k_norm_w[i], w_out[i], i < DEPTH - 1)
    return h
```

```python
import math
import numpy as np
import ml_dtypes
import concourse.bass as bass
import concourse.mybir as mybir
from concourse.bass_utils import run_bass_kernel_spmd

F32 = mybir.dt.float32
BF16 = mybir.dt.bfloat16
AF = mybir.ActivationFunctionType
ALU = mybir.AluOpType
AX = mybir.AxisListType
ENGS = ['pe', 'act', 'dve', 'pool', 'sp']
ISZ = {F32: 4, BF16: 2}

T = 2304
NT = 18
D = 1024
EPS = 1e-6
BIG = 30000.0


class Op:
    __slots__ = ('eng', 'fn', 'is_dma', 'eidx', 'gidx', 'waits', 'signal', 'sem', 'val', 'prevdma')


class Prog:
    NDMA = 24

    def __init__(self, nc):
        self.nc = nc
        self.ops = []
        self.eops = {e: [] for e in ENGS}
        self.track = {}
        self.waited = {e: {} for e in ENGS}
        self.dma_waited = {e: set() for e in ENGS}
        self.ndma = 0
        self.dma_last = {}
        self.addr = {}
        self.pool_ok = False
        self.fence_scr = None
        self.nsw = 0
        self.strict = True

    def region(self, ap):
        t = ap.tensor
        name = ap.name
        pairs = ap.ap
        off = int(ap.offset)
        sp = str(ap.space)
        if sp in ('SB', 'PSUM'):
            shp = tuple(t.shape)
            fsz = 1
            for s in shp[1:]:
                fsz *= s
            p0 = off // fsz
            f0 = off % fsz
            pst, pc = pairs[0]
            p1 = p0 + (pc if pst > 0 else 1)
            ext = 0
            for st, c in pairs[1:]:
                ext += abs(st) * (c - 1)
            isz = ISZ.get(ap.dtype, 4)
            if sp == 'SB':
                base = self.addr[name]
                return ('SB', p0, p1, base + f0 * isz, base + (f0 + ext + 1) * isz)
            b0 = (f0 * isz) // 2048
            b1 = ((f0 + ext + 1) * isz + 2047) // 2048
            return (name, (p0 // 32) * 32, ((p1 + 31) // 32) * 32, b0 * 2048, b1 * 2048)
        ext = 0
        for st, c in pairs:
            ext += abs(st) * (c - 1)
        return (name, 0, 1, off, off + ext + 1)

    def add(self, eng, fn, reads, writes, is_dma=False):
        if eng == 'pool' and not is_dma and not self.pool_ok:
            eng = 'dve'
        op = Op()
        op.eng = eng
        op.fn = fn
        op.is_dma = is_dma
        op.signal = False
        op.gidx = len(self.ops)
        op.eidx = len(self.eops[eng])
        op.sem = None
        op.val = 0
        op.prevdma = None
        deps = []
        rregs = [self.region(a) for a in reads]
        wregs = [self.region(a) for a in writes]

        def ov(a, b):
            return a[1] < b[2] and b[1] < a[2] and a[3] < b[4] and b[3] < a[4]

        def cov(a, b):
            return a[1] <= b[1] and a[2] >= b[2] and a[3] <= b[3] and a[4] >= b[4]

        for r in rregs:
            tr = self.track.get(r[0])
            if tr is None:
                continue
            for (wr, wop) in tr[0]:
                if ov(r, wr):
                    deps.append((wop, 'RAW'))
        for w in wregs:
            tr = self.track.get(w[0])
            if tr is None:
                continue
            for (wr, wop) in tr[0]:
                if ov(w, wr):
                    deps.append((wop, 'WAW'))
            for (rr, rop) in tr[1]:
                if ov(w, rr):
                    deps.append((rop, 'WAR'))
        for w in wregs:
            tr = self.track.setdefault(w[0], [[], []])
            tr[0] = [(wr, wop) for (wr, wop) in tr[0] if not cov(w, wr)]
            tr[1] = [(rr, rop) for (rr, rop) in tr[1] if not cov(w, rr)]
            tr[0].append((w, op))
        for r in rregs:
            tr = self.track.setdefault(r[0], [[], []])
            if not is_dma:
                tr[1] = [(rr, rop) for (rr, rop) in tr[1]
                         if not (rop.eng == eng and not rop.is_dma and cov(r, rr))]
            tr[1].append((r, op))
        need = {}
        for (p, kind) in deps:
            if p is op:
                continue
            if p.is_dma:
                if p.gidx in self.dma_waited[eng]:
                    continue
                need[('dma', p.gidx)] = p
            else:
                if p.eng == eng:
                    if eng == 'pe':
                        continue
                    if kind != 'RAW' and not is_dma and not self.strict:
                        continue
                if eng != p.eng and kind == 'RAW' and p.eng in ('dve', 'act') and self.fence_scr is not None:
                    lst = self.eops[p.eng]
                    if p.eidx + 1 >= len(lst):
                        self._fence(p.eng)
                    p = lst[p.eidx + 1]
                if p.eidx <= self.waited[eng].get(p.eng, -1):
                    continue
                cur = need.get(p.eng)
                if cur is None or p.eidx > cur.eidx:
                    need[p.eng] = p
        for k, p in need.items():
            p.signal = True
            if p.is_dma:
                self.dma_waited[eng].add(p.gidx)
            else:
                self.waited[eng][p.eng] = p.eidx
        op.waits = list(need.values())
        if is_dma and eng == 'pool':
            op.sem = ('sw', self.nsw)
            self.nsw += 1
            op.signal = True
        elif is_dma:
            slot = self.ndma % self.NDMA
            self.ndma += 1
            prev = self.dma_last.get(slot)
            if prev is not None and prev.gidx not in self.dma_waited[eng]:
                op.prevdma = prev
                self.dma_waited[eng].add(prev.gidx)
            self.dma_last[slot] = op
            op.sem = slot
            op.signal = True
        op.gidx = len(self.ops)
        op.eidx = len(self.eops[eng])
        self.ops.append(op)
        self.eops[eng].append(op)
        return op

    def _fence(self, eng):
        scr = self.fence_scr
        if eng == 'dve':
            return self.add('dve', lambda e: e.memset(scr[:, 0:1], 0.0), [], [scr[:, 0:1]])
        return self.add('act', lambda e: e.memzero(scr[:, 2:3]), [], [scr[:, 2:3]])

    def mm(self, out, lhsT, rhs, start=True, stop=True):
        rd = [lhsT, rhs] + ([] if start else [out])
        return self.add('pe', lambda e: e.matmul(out, lhsT, rhs, start=start, stop=stop), rd, [out])

    def act(self, out, in_, func, bias=None, scale=1.0, accum_out=None):
        rd = [in_]
        kw = {}
        if bias is not None:
            kw['bias'] = bias
            if not isinstance(bias, (int, float)):
                rd.append(bias)
        if not isinstance(scale, (int, float)):
            rd.append(scale)
        wr = [out]
        if accum_out is not None:
            kw['accum_out'] = accum_out
            wr.append(accum_out)
        return self.add('act', lambda e: e.activation(out, in_, func, scale=scale, **kw), rd, wr)

    def tt(self, eng, out, in0, in1, op):
        return self.add(eng, lambda e: e.tensor_tensor(out, in0, in1, op), [in0, in1], [out])

    def ts(self, eng, out, in0, s1, s2, op0, op1=None):
        rd = [in0]
        if not isinstance(s1, (int, float)):
            rd.append(s1)
        if s2 is not None and not isinstance(s2, (int, float)):
            rd.append(s2)
        if op1 is None:
            return self.add(eng, lambda e: e.tensor_scalar(out, in0, s1, None, op0), rd, [out])
        return self.add(eng, lambda e: e.tensor_scalar(out, in0, s1, s2, op0, op1), rd, [out])

    def stt(self, out, in0, scalar, in1, op0, op1):
        rd = [in0, in1]
        if not isinstance(scalar, (int, float)):
            rd.append(scalar)
        return self.add('dve', lambda e: e.scalar_tensor_tensor(out, in0, scalar, in1, op0, op1), rd, [out])

    def copy(self, eng, out, in_):
        if eng == 'act':
            return self.add(eng, lambda e: e.copy(out, in_), [in_], [out])
        return self.add(eng, lambda e: e.tensor_copy(out, in_), [in_], [out])

    def memset(self, eng, out, val):
        return self.add(eng, lambda e: e.memset(out, val), [], [out])

    def reduce(self, eng, out, in_, op=None):
        op = ALU.add if op is None else op
        return self.add(eng, lambda e: e.tensor_reduce(out, in_, AX.X, op), [in_], [out])

    def recip(self, out, in_):
        return self.add('dve', lambda e: e.reciprocal(out, in_), [in_], [out])

    def dma(self, out, in_, q='sp'):
        return self.add(q, lambda e: e.dma_start(out=out, in_=in_), [in_], [out], is_dma=True)

    def finish(self, aps, q='sp'):
        return self.add(q, None, list(aps), [])

    def emit(self):
        nc = self.nc
        cnt = {e: 0 for e in ENGS}
        dcnt = {}
        for op in self.ops:
            if op.is_dma and isinstance(op.sem, tuple):
                op.val = 16
            elif op.is_dma:
                dcnt[op.sem] = dcnt.get(op.sem, 0) + 16
                op.val = dcnt[op.sem]
            elif op.signal:
                cnt[op.eng] += 1
                op.val = cnt[op.eng]
        self.stats = {e: (len(self.eops[e]), cnt[e]) for e in ENGS}
        esem = {e: nc.alloc_semaphore('s_' + e) for e in ENGS}
        dsem = [nc.alloc_semaphore('s_dma%d' % i) for i in range(self.NDMA)]
        swsem = [nc.alloc_semaphore('s_sw%d' % i) for i in range(self.nsw)]

        def semof(p):
            if p.is_dma:
                return swsem[p.sem[1]] if isinstance(p.sem, tuple) else dsem[p.sem]
            return esem[p.eng]

        def run(ename, e):
            for op in self.eops[ename]:
                if op.prevdma is not None:
                    e.wait_ge(dsem[op.prevdma.sem], op.prevdma.val)
                for p in op.waits:
                    e.wait_ge(semof(p), p.val)
                if op.fn is None:
                    continue
                ins = op.fn(e)
                if op.is_dma:
                    ins.then_inc(semof(op), 16)
                elif op.signal:
                    ins.then_inc(esem[op.eng], 1)

        with nc.Block() as block:
            @block.tensor
            def _(e):
                run('pe', e)

            @block.scalar
            def _(e):
                run('act', e)

            @block.vector
            def _(e):
                run('dve', e)

            @block.gpsimd
            def _(e):
                run('pool', e)

            @block.sync
            def _(e):
                run('sp', e)
                for slot, v in dcnt.items():
                    e.wait_ge(dsem[slot], v)
                for sm in swsem:
                    e.wait_ge(sm, 16)


def bcm(ap, n):
    s = list(ap.shape)
    return ap.unsqueeze(len(s)).broadcast_to(s + [n])


def bch(ap, n):
    s = list(ap.shape)
    return ap.unsqueeze(1).broadcast_to([s[0], n] + s[1:])


C_SSDZ, C_SSDDT, C_GDNZ, C_GDNA, C_GDNB = 1408, 1792, 1804, 2060, 2068
C_AQ, C_AK, C_AV, C_AZ = 2076, 2460, 2588, 2716

(F_ID, F_TRIF, F_TRIB, F_TBF, F_TBB, F_ONES, F_NONES) = range(7)
NF_ = 7
(B_ID, B_NMF, B_NMB, B_NBF, B_NBB, B_SBF, B_SBB, B_BLK, B_TRIF, B_TRIB, B_TBF, B_TBB, B_ONES, B_NONES) = range(14)
B_LV = 14
NB_ = 20


def host_consts():
    t = np.arange(128)
    tri_f = (t[:, None] <= t[None, :]).astype(np.float32)
    tri_b = (t[:, None] >= t[None, :]).astype(np.float32)
    blk = ((t[:, None] // 64) == (t[None, :] // 64)).astype(np.float32)
    m = np.zeros((128, NF_, 128), np.float32)
    m[:, F_ID] = np.eye(128)
    m[:, F_TRIF] = tri_f
    m[:, F_TRIB] = tri_b
    m[:, F_TBF] = tri_f * blk
    m[:, F_TBB] = tri_b * blk
    m[:, F_ONES] = 1.0
    m[:, F_NONES] = -1.0
    mb = np.zeros((128, NB_, 128), np.float32)
    mb[:, B_ID] = np.eye(128)
    mb[:, B_NMF] = (tri_f - 1) * BIG
    mb[:, B_NMB] = (tri_b - 1) * BIG
    mb[:, B_NBF] = (tri_f * blk - 1) * BIG
    mb[:, B_NBB] = (tri_b * blk - 1) * BIG
    mb[:, B_SBF] = (t[:, None] < t[None, :]) * blk
    mb[:, B_SBB] = (t[:, None] > t[None, :]) * blk
    mb[:, B_BLK] = blk
    mb[:, B_TRIF] = tri_f
    mb[:, B_TRIB] = tri_b
    mb[:, B_TBF] = tri_f * blk
    mb[:, B_TBB] = tri_b * blk
    mb[:, B_ONES] = 1.0
    mb[:, B_NONES] = -1.0
    for sl in range(6):
        mb[:, B_LV + sl] = ((t[:, None] >> (sl + 1)) == (t[None, :] >> (sl + 1))) & ((t[:, None] >> sl) != (t[None, :] >> sl))
    nf = 16
    freqs = (10000.0 ** (-np.arange(nf, dtype=np.float32) / nf)).astype(np.float32)
    pos = np.arange(2048)
    ar = (pos // 64).astype(np.float32)[:, None] * freqs
    ac = (pos % 64).astype(np.float32)[:, None] * freqs
    cr, sr, cc, sc = np.cos(ar), np.sin(ar), np.cos(ac), np.sin(ac)
    cosf = np.concatenate([cr, cr, cc, cc], 1).astype(np.float32)
    sinf = np.concatenate([-sr, sr, -sc, sc], 1).astype(np.float32)
    rope = np.stack([cosf, sinf], 1).reshape(16, 128, 2, 64).transpose(1, 0, 2, 3)
    cm = np.zeros((128, 2), np.float32)
    cm[:64, 0] = 1
    cm[64:, 1] = 1
    return np.ascontiguousarray(m), np.ascontiguousarray(mb), np.ascontiguousarray(rope), cm


def build(n_layers=2, stop=None, dbg=()):
    nc = bass.Bass("TRN2", target_bir_lowering=False)
    P = Prog(nc)
    ins = {}

    def din(name, shape, dt=F32):
        ins[name] = nc.dram_tensor(name, list(shape), dt, kind="ExternalInput").ap()
        return ins[name]

    x_d = din("x", [2048, D])
    ctx_d = din("ctx", [256, D])
    cc_d = din("cc", [128, 8, 2])
    wmod_d = din("w_mod", [2, D, 3072])
    bmod_d = din("b_mod", [2, 3072])
    win_d = din("w_in", [2, D, 3100])
    wout_d = din("w_out", [2, D, D])
    normwc_d = din("normwc", [128, 2, 8])
    bmodc_d = din("bmodc", [128, 2, 16])
    convwc_d = din("convwc", [128, 2, 11, 3])
    convbc_d = din("convbc", [128, 2, 11])
    rows_d = din("rows", [2, 1024])
    constm_d = din("constm", [128, NF_, 128])
    constb_d = din("constb", [128, NB_, 128])
    rope_d = din("rope", [128, 16, 2, 64])
    cm_d = din("cm", [128, 2])
    out_d = nc.dram_tensor("out", [2048, D], F32, kind="ExternalOutput").ap()
    x1s_d = nc.dram_tensor("x1s", [2048, D], F32, kind="Internal").ap()
    dbg_out = {}

    from contextlib import ExitStack
    with ExitStack() as es:
        def sb(name, shape, dt):
            h = es.enter_context(nc.sbuf_tensor(name, list(shape), dt))
            P.addr[name] = int(nc.lookup_mloc(h).addr)
            return h

        PS = es.enter_context(nc.psum_tensor("PS", [128, 8, 512], F32))
        bank_ctr = [0]

        NROT = [8]

        def nb():
            b = bank_ctr[0] % NROT[0]
            bank_ctr[0] += 1
            return PS[:, b, :]

        HT = sb("HT", [128, 8, T], BF16)
        MIX = sb("MIX", [128, NT, 1024], BF16)
        XC = sb("XC", [128, 2, D], F32)
        CMF = sb("CMF", [128, NF_, 128], F32)
        CMB = sb("CMB", [128, NB_, 128], BF16)
        WS = [sb("WS0", [128, 8, 512], BF16), sb("WS1", [128, 8, 512], BF16)]
        SM = sb("SM", [128, 512], F32)
        DTR = sb("DTR", [128, NT, 28], F32)
        SDT = sb("SDT", [128, NT, 12], F32)
        SDA = sb("SDA", [128, NT, 12], F32)
        GG = sb("GG", [128, NT, 8], F32)
        GB = sb("GB", [128, NT, 8], F32)
        SDAH = sb("SDAH", [128, NT, 12], BF16)
        SDAL = sb("SDAL", [128, NT, 12], BF16)
        GGH = sb("GGH", [128, NT, 8], BF16)
        GGL = sb("GGL", [128, NT, 8], BF16)
        ROWS = sb("ROWS", [128, 1024], F32)
        MODC = sb("MODC", [128, 16, 2], F32)
        GC = sb("GC", [128, 8, 2], F32)
        SILC = sb("SILC", [128, 8, 2], BF16)
        SILB = sb("SILB", [128, 2, 8, 128], BF16)
        CCF = sb("CCF", [128, 8, 2], F32)
        NWC = sb("NWC", [128, 2, 8], F32)
        BMC = sb("BMC", [128, 2, 16], F32)
        CVW = sb("CVW", [128, 2, 11, 3], F32)
        CVB = sb("CVB", [128, 2, 11], F32)
        CMK = sb("CMK", [128, 2], F32)
        ARENA = sb("ARENA", [128, 22080], F32)
        arena_off = [0]

        def arena_reset():
            arena_off[0] = 0

        def al(shape, dt):
            n = 1
            for s in shape[1:]:
                n *= s
            nbytes = n * ISZ[dt]
            nwords = (nbytes + 3) // 4
            o = arena_off[0]
            assert o + nwords <= 22080, ("arena overflow", o, nwords)
            arena_off[0] = o + nwords
            v = ARENA[:, o:o + nwords]
            if dt != F32:
                v = v.bitcast(dt)[:, 0:n]
            if len(shape) == 2:
                return v
            names = ' '.join('a%d' % i for i in range(len(shape) - 1))
            kw = {'a%d' % i: shape[i + 1] for i in range(len(shape) - 2)}
            return v.rearrange("p (%s) -> p %s" % (names, names), **kw)

        def cmf(i):
            return CMF[:, i, :]

        def cmb(i):
            return CMB[:, i, :]

        IDB = cmb(B_ID)
        IDF = cmf(F_ID)
        ONESB = cmb(B_ONES)
        NONESB = cmb(B_NONES)

        def dump(name, ap):
            if name not in dbg:
                return
            d = nc.dram_tensor("dbg_" + name, list(ap.shape), ap.dtype, kind="ExternalOutput").ap()
            dbg_out[name] = d
            P.dma(d, ap)

        P.dma(CMF[:], constm_d)
        P.dma(CMB[:], constb_d, q='pool')
        P.dma(CCF[:], cc_d)
        P.dma(NWC[:], normwc_d)
        P.dma(BMC[:], bmodc_d)
        P.dma(CVW[:], convwc_d)
        P.dma(CVB[:], convbc_d)
        P.dma(CMK[:], cm_d)
        P.dma(XC[:], ctx_d.rearrange("(t p) d -> p t d", p=128))
        EPSC = SM[:, 0:1]
        P.memset('dve', EPSC, EPS)
        P.fence_scr = SM[:, 208:216]
        P.act(SILC[:], CCF[:], AF.Silu)
        for v in range(2):
            P.copy('dve', SILB[:, v, :, :], bcm(SILC[:, :, v], 128))

        def softplus(out, in_, tmp1, tmp2, eng='dve'):
            P.act(tmp1, in_, AF.Abs)
            P.act(tmp1, tmp1, AF.Exp, scale=-1.0)
            P.ts(eng, tmp1, tmp1, 1.0, None, ALU.add)
            P.act(tmp1, tmp1, AF.Ln)
            P.ts(eng, tmp2, in_, 0.0, None, ALU.max)
            P.tt(eng, out, tmp1, tmp2, ALU.add)

        import os as _os
        stop_in = stop
        stop_l = int(_os.environ.get('STOP_L', '0'))
        for l in range(n_layers):
            stop = stop_in if l == stop_l else None
            last = (l == n_layers - 1)
            xsrc = x_d if l == 0 else x1s_d
            xdst = out_d if last else x1s_d
            arena_reset()
            P.dma(ROWS[:], rows_d[l].partition_broadcast(128))
            R_DTB = ROWS[:, 0:12]
            R_AL = ROWS[:, 12:24]
            R_D = ROWS[:, 24:30]
            R_GAL = ROWS[:, 32:40]
            R_GDB = ROWS[:, 40:48]
            R_GNW = ROWS[:, 64:128]
            R_QNW = ROWS[:, 128:192]
            R_KNW = ROWS[:, 192:256]
            R_SNW = ROWS[:, 256:640]
            RA = SM[:, 8:20]
            RGA = SM[:, 20:28]
            P.act(RA, R_AL, AF.Exp)
            P.ts('dve', RA, RA, -1.0, None, ALU.mult)
            P.act(RGA, R_GAL, AF.Exp)
            P.ts('dve', RGA, RGA, -1.0, None, ALU.mult)

            GATE = al([128, 2, 1024], F32)
            mark_gate = arena_off[0]
            BG = al([128, 1024], F32)
            P.dma(BG, bmod_d[l, 2048:3072].partition_broadcast(128))
            wmod_v = wmod_d[l].rearrange("(k p) n -> p k n", p=128)
            for blk in range(6):
                ws = WS[blk % 2]
                P.dma(ws[:, :, :], wmod_v[:, :, blk * 512:(blk + 1) * 512], q='pool')
                if blk < 4:
                    pb = nb()
                    for j in range(4):
                        for k in range(8):
                            P.mm(pb[:, j * 2:(j + 1) * 2], ws[:, k, j * 128:(j + 1) * 128], SILC[:, k, :],
                                 start=(k == 0), stop=(k == 7))
                    P.tt('dve', MODC[:, blk * 4:(blk + 1) * 4, :],
                         pb[:, 0:8].rearrange("p (j v) -> p j v", v=2),
                         bcm(BMC[:, l, blk * 4:(blk + 1) * 4], 2), ALU.add)
                else:
                    for v in range(2):
                        pb = nb()
                        for k in range(8):
                            P.mm(pb[:, :], SILB[:, v, k, :], ws[:, k, :], start=(k == 0), stop=(k == 7))
                        P.tt('dve', GATE[:, v, (blk - 4) * 512:(blk - 3) * 512], pb[:, :],
                             BG[:, (blk - 4) * 512:(blk - 3) * 512], ALU.add)
            P.ts('dve', GC[:], MODC[:, 8:16, :], 1.0, None, ALU.add)
            P.tt('dve', GC[:], GC[:], bcm(NWC[:, l, :], 2), ALU.mult)
            dump("L%d_gc" % l, GC[:])
            dump("L%d_modc" % l, MODC[:])
            dump("L%d_gate" % l, GATE)

            arena_off[0] = mark_gate
            mark = arena_off[0]
            XIN = [al([128, 4, D], F32), al([128, 4, D], F32)]
            XN = [al([128, 4, D], BF16), al([128, 4, D], BF16)]
            JUNK = al([128, D], BF16)
            SS = SM[:, 32:50]
            RS = SM[:, 64:82]
            groups = [([0, 1], 1)] + [([2 + 4 * g + j for j in range(4)], 0) for g in range(4)]
            for gi, (tiles, v) in enumerate(groups):
                n = len(tiles)
                t0 = tiles[0]
                slot = gi % 2
                if v == 1:
                    xin = XC
                else:
                    xin = XIN[slot]
                    r0 = (t0 - 2) * 128
                    P.dma(xin[:, :, :], xsrc[r0:r0 + 512, :].rearrange("(t p) d -> p t d", p=128))
                for j, t in enumerate(tiles):
                    P.act(JUNK, xin[:, j, :], AF.Square, accum_out=SS[:, t:t + 1])
                P.ts('dve', RS[:, t0:t0 + n], SS[:, t0:t0 + n], 1.0 / D, EPS, ALU.mult, ALU.add)
                P.act(RS[:, t0:t0 + n], RS[:, t0:t0 + n], AF.Sqrt)
                P.recip(RS[:, t0:t0 + n], RS[:, t0:t0 + n])
                for j, t in enumerate(tiles):
                    P.ts('dve' if j % 2 == 0 else 'pool', XN[slot][:, j, :], xin[:, j, :], RS[:, t:t + 1], None, ALU.mult)
                for k in range(8):
                    pb = nb()
                    for j in range(n):
                        P.mm(pb[:, j * 128:(j + 1) * 128], XN[slot][:, j, k * 128:(k + 1) * 128], IDB)
                    dst = HT[:, k, t0 * 128:(t0 + n) * 128]
                    if k % 2 == 0:
                        P.ts('dve', dst, pb[:, 0:n * 128], GC[:, k, v:v + 1], MODC[:, k, v:v + 1], ALU.mult, ALU.add)
                    else:
                        P.act(dst, pb[:, 0:n * 128], AF.Identity, bias=MODC[:, k, v:v + 1], scale=GC[:, k, v:v + 1])
            dump("L%d_ht" % l, HT[:])
            arena_off[0] = mark
            if stop == 'p1':
                break

            win_v = win_d[l].rearrange("(k p) n -> p k n", p=128)
            WSM = al([128, 8, 28], BF16)
            P.dma(WSM[:, :, 0:12], win_v[:, :, C_SSDDT:C_SSDDT + 12], q='pool')
            P.dma(WSM[:, :, 12:28], win_v[:, :, C_GDNA:C_GDNA + 16], q='pool')
            for g0 in (0, 16):
                tiles = list(range(g0, min(g0 + 16, NT)))
                pb = nb()
                for j, t in enumerate(tiles):
                    for k in range(8):
                        P.mm(pb[:, j * 28:(j + 1) * 28], HT[:, k, t * 128:(t + 1) * 128], WSM[:, k, :],
                             start=(k == 0), stop=(k == 7))
                n = len(tiles)
                P.copy('dve', DTR[:, g0:g0 + n, :], pb[:, 0:n * 28].rearrange("p (t c) -> p t c", c=28))
            TMPA = al([128, NT, 12], F32)
            TMPB = al([128, NT, 12], F32)
            P.tt('dve', SDT[:], DTR[:, :, 0:12], bch(R_DTB, NT), ALU.add)
            softplus(SDT[:], SDT[:], TMPA, TMPB)
            P.tt('dve', SDA[:], SDT[:], bch(RA, NT), ALU.mult)
            P.copy('dve', SDAH[:], SDA[:])
            P.tt('dve', SDAL[:], SDA[:], SDAH[:], ALU.subtract)
            P.tt('dve', GG[:], DTR[:, :, 12:20], bch(R_GDB, NT), ALU.add)
            softplus(GG[:], GG[:], TMPA[:, :, 0:8], TMPB[:, :, 0:8])
            P.tt('dve', GG[:], GG[:], bch(RGA, NT), ALU.mult)
            P.act(GB[:], DTR[:, :, 20:28], AF.Sigmoid)
            P.copy('dve', GGH[:], GG[:])
            P.tt('dve', GGL[:], GG[:], GGH[:], ALU.subtract)
            dump("L%d_sdt" % l, SDT[:])
            dump("L%d_gg" % l, GG[:])
            dump("L%d_gb" % l, GB[:])
            dump("L%d_dtr" % l, DTR[:])
            if stop == 'dt':
                break

            mark_mix = arena_off[0]
            PREB = al([128, T + 4], BF16)
            DG = [al([128, 3, 128], BF16), al([128, 3, 128], BF16)]
            P.memset('pool', PREB[:, 0:1], 0.0)
            P.memset('pool', PREB[:, 257:259], 0.0)
            P.memset('pool', PREB[:, T + 3:T + 4], 0.0)
            TBK = [(0, 256)] + [(256 + 512 * i, 512) for i in range(4)]
            if stop == 'c0':
                dump("L%d_preb" % l, PREB)
                break
            cctr = [0]
            wsslot = [0]
            wsbase = [0]

            def conv_proj(ch, dest):
                dg = DG[cctr[0] % 2]
                cctr[0] += 1
                grp = {0: (0, 4), 4: (4, 5), 5: (5, 9), 9: (9, 11)}
                if ch in grp:
                    c0, c1 = grp[ch]
                    wsslot[0] = (wsslot[0] + 1) % 2
                    wsbase[0] = c0
                    P.dma(WS[wsslot[0]][:, :, 0:(c1 - c0) * 128], win_v[:, :, c0 * 128:c1 * 128], q='pool')
                ws = WS[wsslot[0]][:, :, (ch - wsbase[0]) * 128:(ch - wsbase[0] + 1) * 128]
                for k in range(3):
                    P.ts('pool', dg[:, k, :], IDF, CVW[:, l, ch, k:k + 1], None, ALU.mult)
                for bi, (t0, n) in enumerate(TBK):
                    pb = nb()
                    for k in range(8):
                        P.mm(pb[:, 0:n], ws[:, k, 0:128], HT[:, k, t0:t0 + n], start=(k == 0), stop=(k == 7))
                    po = t0 + 1 if t0 < 256 else t0 + 3
                    if bi % 2 == 0:
                        P.copy('dve', PREB[:, po:po + n], pb[:, 0:n])
                    else:
                        P.copy('act', PREB[:, po:po + n], pb[:, 0:n])
                if stop == 'c1':
                    return
                for bi, (t0, n) in enumerate(TBK):
                    po = t0 + 1 if t0 < 256 else t0 + 3
                    pb2 = nb()
                    for k in range(3):
                        P.mm(pb2[:, 0:n], dg[:, k, :], PREB[:, po - 1 + k:po - 1 + k + n], start=(k == 0), stop=(k == 2))
                    P.act(dest[:, t0:t0 + n], pb2[:, 0:n], AF.Silu, bias=CVB[:, l, ch:ch + 1])

            def to_tok(src, CTt, c0):
                for gi, g0 in enumerate(range(0, NT, 4)):
                    tiles = list(range(g0, min(g0 + 4, NT)))
                    n = len(tiles)
                    pb = nb()
                    for j, t in enumerate(tiles):
                        P.mm(pb[:, j * 128:(j + 1) * 128], src[:, t * 128:(t + 1) * 128], IDB)
                    P.copy('dve' if gi % 2 == 0 else 'act', CTt[:, g0:g0 + n, c0:c0 + 128],
                           pb[:, 0:n * 128].rearrange("p (t c) -> p t c", c=128))

            mark_ssd = arena_off[0]
            CTS = al([128, NT, 512], BF16)
            CFB = al([128, T], BF16)
            CFC = al([128, T], BF16)
            XF = [al([128, T], BF16), al([128, T], BF16)]
            if stop in ('c1', 'c2'):
                conv_proj(0, XF[0])
                dump("L%d_preb" % l, PREB)
                dump("L%d_xf" % l, XF[0])
                if stop == 'c2':
                    to_tok(XF[0], CTS, 0)
                    dump("L%d_cts" % l, CTS)
                break
            for ch in range(3):
                conv_proj(ch, XF[ch % 2])
                to_tok(XF[ch % 2], CTS, ch * 128)
            conv_proj(3, CFB)
            to_tok(CFB, CTS, 384)
            conv_proj(4, CFC)
            dump("L%d_cts" % l, CTS)
            dump("L%d_cfc" % l, CFC)
            if stop == 'conv':
                break
            RALH = [al([128, 6, 128], BF16), al([128, 6, 128], BF16)]
            RALL = [al([128, 6, 128], BF16), al([128, 6, 128], BF16)]
            EE = [al([128, 6, 128], F32), al([128, 6, 128], F32)]
            WTT = [al([128, 6, 128], BF16), al([128, 6, 128], BF16)]
            XS = [al([128, 6, 64], BF16), al([128, 6, 64], BF16)]
            TMPY = al([128, 6, 64], F32)
            TY2 = al([128, 384], F32)
            TY3 = al([128, 384], F32)
            SST = [al([128, 384], F32), al([128, 384], F32)]
            SBF = [al([128, 384], BF16), al([128, 384], BF16)]
            NCUM = SM[:, 96:102]
            ECUM = SM[:, 104:110]
            SD = SM[:, 112:118]
            CD = SM[:, 120:126]
            DFULL = al([128, 6, 64], F32)
            P.copy('dve', DFULL, bcm(R_D, 64))
            sctr = [0]
            for d in range(2):
                P.memset('dve', SST[d], 0.0)
                P.memset('pool', SBF[d], 0.0)
                order = list(range(NT)) if d == 0 else [1, 0] + list(range(NT - 1, 1, -1))
                import os
                order = order[:int(os.environ.get('SSD_N', '99'))]
                TRI = cmb(B_TRIF if d == 0 else B_TRIB)
                NMK = cmb(B_NMF if d == 0 else B_NMB)
                lastc = 127 if d == 0 else 0
                for t in order:
                    par = sctr[0] % 2
                    sctr[0] += 1
                    tok = slice(t * 128, (t + 1) * 128)
                    need_y = not (last and t < 2)
                    a6h = SDAH[:, t, d * 6:(d + 1) * 6]
                    a6l = SDAL[:, t, d * 6:(d + 1) * 6]
                    dt6 = SDT[:, t, d * 6:(d + 1) * 6]
                    rah, ral, ee, wt, xs = RALH[par], RALL[par], EE[par], WTT[par], XS[par]
                    P.tt('pool', rah, bch(TRI, 6), bcm(a6h, 128), ALU.mult)
                    P.tt('pool', ral, bch(TRI, 6), bcm(a6l, 128), ALU.mult)
                    pcol = nb()
                    P.mm(pcol[:, 0:6], TRI, a6h, start=True, stop=False)
                    P.mm(pcol[:, 0:6], TRI, a6l, start=False, stop=True)
                    P.ts('dve', NCUM, pcol[:, 0:6], -1.0, None, ALU.mult)
                    P.act(ECUM, pcol[:, 0:6], AF.Exp)
                    if stop == 's1':
                        break
                    pA = nb()
                    pB = nb()
                    dsts = []
                    for h in range(6):
                        dst = (pA if h < 4 else pB)[:, (h % 4) * 128:(h % 4 + 1) * 128]
                        dsts.append(dst)
                        P.mm(dst, ONESB, rah[:, h, :], start=True, stop=False)
                        P.mm(dst, ONESB, ral[:, h, :], start=False, stop=False)
                        P.mm(dst, IDB, NMK, start=False, stop=True)
                    for h in range(6):
                        P.act(ee[:, h, :], dsts[h], AF.Exp, bias=NCUM[:, h:h + 1])
                    if stop == 's2':
                        break
                    P.mm(pcol[:, 8:14], ONESB, a6h, start=True, stop=False)
                    P.mm(pcol[:, 8:14], ONESB, a6l, start=False, stop=True)
                    P.tt('dve', SD, pcol[:, 8:14], NCUM, ALU.add)
                    P.act(SD, SD, AF.Exp)
                    P.tt('dve', SD, SD, dt6, ALU.mult)
                    P.act(CD, pcol[:, 8:14], AF.Exp)
                    if stop == 's3b':
                        break
                    P.tt('pool', xs, CTS[:, t, 0:384].rearrange("p (h f) -> p h f", h=6), bcm(SD, 64), ALU.mult)
                    if stop == 's3':
                        break
                    if need_y:
                        psc = [nb(), nb()]
                        for g in range(2):
                            P.mm(psc[g][:, 0:128], CFB[g * 64:(g + 1) * 64, tok], CFC[g * 64:(g + 1) * 64, tok])
                        for h in range(6):
                            g = h // 3
                            P.stt(wt[:, h, :], psc[g][:, 0:128], dt6[:, h:h + 1], ee[:, h, :], ALU.mult, ALU.mult)
                        if stop == 'y1':
                            break
                        py = nb()
                        poff = [nb(), nb()]
                        for h in range(6):
                            P.mm(py[:, h * 64:(h + 1) * 64], wt[:, h, :], CTS[:, t, h * 64:(h + 1) * 64])
                        for g in range(2):
                            P.mm(poff[g][:, 0:192], CFC[g * 64:(g + 1) * 64, tok],
                                 SBF[d][g * 64:(g + 1) * 64, g * 192:(g + 1) * 192])
                        if stop == 'y2':
                            break
                        for g in range(2):
                            P.tt('dve', TMPY[:, 3 * g:3 * g + 3, :], poff[g][:, 0:192].rearrange("p (h f) -> p h f", h=3),
                                 bcm(ECUM[:, 3 * g:3 * g + 3], 64), ALU.mult)
                        P.tt('dve', TY2, py[:, 0:384], TMPY.rearrange("p h f -> p (h f)"), ALU.add)
                        if d == 0:
                            P.tt('pool', TY3, CTS[:, t, 0:384], DFULL.rearrange("p h f -> p (h f)"), ALU.mult)
                            P.tt('pool', MIX[:, t, 0:384], TY2, TY3, ALU.add)
                        else:
                            P.tt('pool', MIX[:, t, 0:384], TY2, MIX[:, t, 0:384], ALU.add)
                    if stop == 's4':
                        break
                    pst = nb()
                    P.mm(pst[:, 0:384], CTS[:, t, 384:512], xs.rearrange("p h f -> p (h f)"))
                    P.tt('pool', SST[d].rearrange("p (h f) -> p h f", h=6), SST[d].rearrange("p (h f) -> p h f", h=6),
                         bcm(CD, 64), ALU.mult)
                    P.tt('dve', SST[d], SST[d], pst[:, 0:384], ALU.add)
                    P.copy('pool', SBF[d], SST[d])
                    if stop == 's5':
                        break
            dump("L%d_mixs" % l, MIX[:, :, 0:384])
            dump("L%d_mix" % l, MIX[:])
            arena_off[0] = mark_ssd
            if stop in ('ssd', 's1', 's2', 's3', 's4', 's5', 's3a', 's3b', 'y1', 'y2'):
                break

            def nb2():
                if bank_ctr[0] % 2 == 1:
                    bank_ctr[0] += 1
                b = bank_ctr[0] % NROT[0]
                bank_ctr[0] += 2
                return PS[:, b:b + 2, :]

            hpi = lambda h: (h % 2) * 2 + h // 2
            GBP = al([128, NT, 8], F32)
            GHP = al([128, NT, 8], BF16)
            GLP = al([128, NT, 8], BF16)
            for dd in range(2):
                for h in range(4):
                    P.copy('dve', GBP[:, :, dd * 4 + hpi(h)], GB[:, :, dd * 4 + h])
                    P.copy('dve', GHP[:, :, dd * 4 + hpi(h)], GGH[:, :, dd * 4 + h])
                    P.copy('dve', GLP[:, :, dd * 4 + hpi(h)], GGL[:, :, dd * 4 + h])
            CFQK = al([128, 4, T], BF16)
            CTG = al([128, NT, 512], BF16)
            mark_gcore = arena_off[0]
            XFG = [al([128, T], BF16), al([128, T], BF16)]
            SQ = al([128, 512], BF16)
            RN = al([128, 512], F32)
            for ci in range(4):
                xf = XFG[ci % 2]
                conv_proj(5 + ci, xf)
                for (t0, n) in TBK:
                    P.tt('dve', SQ[:, 0:n], xf[:, t0:t0 + n], xf[:, t0:t0 + n], ALU.mult)
                    pb = nb()
                    P.mm(pb[:, 0:n], cmb(B_BLK), SQ[:, 0:n])
                    P.act(RN[:, 0:n], pb[:, 0:n], AF.Sqrt, bias=EPSC)
                    P.recip(RN[:, 0:n], RN[:, 0:n])
                    if ci < 2:
                        P.stt(CFQK[:, ci, t0:t0 + n], xf[:, t0:t0 + n], 0.125, RN[:, 0:n], ALU.mult, ALU.mult)
                    else:
                        P.tt('dve', CFQK[:, ci, t0:t0 + n], xf[:, t0:t0 + n], RN[:, 0:n], ALU.mult)
                if ci >= 2:
                    to_tok(CFQK[:, ci, :], CTG, (ci - 2) * 128)
            for ci in range(2):
                xf = XFG[ci % 2]
                conv_proj(9 + ci, xf)
                to_tok(xf, CTG, 256 + ci * 128)
            dump("L%d_cfqk" % l, CFQK)
            dump("L%d_ctg" % l, CTG)
            if stop == 'gconv':
                break
            arena_off[0] = mark_gcore
            RGH = al([128, 4, 128], BF16)
            RGL = al([128, 4, 128], BF16)
            GMH = al([128, 4, 2], BF16)
            GML = al([128, 4, 2], BF16)
            EG = al([128, 4, 128], F32)
            ES = al([128, 4, 128], F32)
            T1 = al([128, 4, 128], F32)
            XX = [al([128, 4, 128], BF16), al([128, 4, 128], BF16)]
            XXT = [al([128, 4, 128], BF16), al([128, 4, 128], BF16)]
            WW = [al([128, 4, 128], BF16), al([128, 4, 128], BF16)]
            WWT = [al([128, 4, 128], BF16), al([128, 4, 128], BF16)]
            CTSd = [al([128, 4, 128], BF16), al([128, 4, 128], BF16)]
            CSd = [al([128, 4, 128], BF16), al([128, 4, 128], BF16)]
            YY = al([128, 4, 128], BF16)
            YYT = al([128, 4, 128], BF16)
            ITT = al([128, 4, 128], BF16)
            UU = al([128, 256], F32)
            KG = al([128, 4, 64], BF16)
            KD = al([128, 4, 64], BF16)
            WTG = al([128, 2, 128], BF16)
            VN = al([128, 256], BF16)
            OA = al([128, 256], F32)
            OA2 = al([128, 256], F32)
            SG = [al([128, 128], F32), al([128, 128], F32)]
            SGB = [al([128, 128], BF16), al([128, 128], BF16)]
            NGC = SM[:, 128:132]
            EGC = SM[:, 136:140]
            F1 = SM[:, 144:148]
            CDF = SM[:, 152:156].rearrange("p (c hh) -> p c hh", c=2)
            for d in range(2):
                P.memset('dve', SG[d], 0.0)
                P.memset('dve', SGB[d], 0.0)
                order = list(range(NT)) if d == 0 else [1, 0] + list(range(NT - 1, 1, -1))
                import os
                order = order[:int(os.environ.get('GDN_N', '99'))]
                TB = cmb(B_TBF if d == 0 else B_TBB)
                NMK = cmb(B_NBF if d == 0 else B_NBB)
                SMK = cmb(B_SBF if d == 0 else B_SBB)
                for t in order:
                    tok = slice(t * 128, (t + 1) * 128)
                    need_o = not (last and t < 2)
                    gh = GHP[:, t, d * 4:(d + 1) * 4]
                    gl = GLP[:, t, d * 4:(d + 1) * 4]
                    bp = GBP[:, t, d * 4:(d + 1) * 4]
                    P.tt('dve', RGH, bch(TB, 4), bcm(gh, 128), ALU.mult)
                    P.tt('dve', RGL, bch(TB, 4), bcm(gl, 128), ALU.mult)
                    P.tt('dve', GMH, bcm(gh, 2), bch(CMK[:], 4), ALU.mult)
                    P.tt('dve', GML, bcm(gl, 2), bch(CMK[:], 4), ALU.mult)
                    pcol = nb()
                    P.mm(pcol[:, 0:4], TB, gh, start=True, stop=False)
                    P.mm(pcol[:, 0:4], TB, gl, start=False, stop=True)
                    P.mm(pcol[:, 8:16], ONESB, GMH.rearrange("p h c -> p (h c)"), start=True, stop=False)
                    P.mm(pcol[:, 8:16], ONESB, GML.rearrange("p h c -> p (h c)"), start=False, stop=True)
                    P.ts('dve', NGC, pcol[:, 0:4], -1.0, None, ALU.mult)
                    P.act(EGC, pcol[:, 0:4], AF.Exp)
                    pcv = pcol[:, 8:16].rearrange("p (hl hh c) -> p hl hh c", hl=2, hh=2)
                    for hl in range(2):
                        for c in range(2):
                            P.act(CDF[hl * 64:(hl + 1) * 64, c, :], pcv[hl * 64:(hl + 1) * 64, hl, :, c], AF.Exp)
                    pd = nb()
                    for hp in range(4):
                        dst = pd[:, hp * 128:(hp + 1) * 128]
                        P.mm(dst, ONESB, RGH[:, hp, :], start=True, stop=False)
                        P.mm(dst, ONESB, RGL[:, hp, :], start=False, stop=False)
                        P.mm(dst, IDB, NMK, start=False, stop=True)
                    for hp in range(4):
                        P.act(EG[:, hp, :], pd[:, hp * 128:(hp + 1) * 128], AF.Exp, bias=NGC[:, hp:hp + 1])
                    pkk = nb2()
                    pqk = nb2()
                    for h in range(4):
                        hl, hh = h % 2, h // 2
                        kf = CFQK[hl * 64:(hl + 1) * 64, 2 + hh, tok]
                        qf = CFQK[hl * 64:(hl + 1) * 64, hh, tok]
                        P.mm(pkk[:, hl, hh * 128:(hh + 1) * 128], kf, kf)
                        P.mm(pqk[:, hl, hh * 128:(hh + 1) * 128], kf, qf)
                    P.tt('dve', ES, EG, bch(SMK, 4), ALU.mult)
                    for hl in range(2):
                        P.tt('dve', T1[:, hl * 2:(hl + 1) * 2, :], pkk[:, hl, 0:256].rearrange("p (b i) -> p b i", b=2),
                             bcm(bp[:, hl * 2:(hl + 1) * 2], 128), ALU.mult)
                    P.tt('dve', XX[0], T1, ES, ALU.mult)
                    for hl in range(2):
                        P.tt('dve', T1[:, hl * 2:(hl + 1) * 2, :], pqk[:, hl, 0:256].rearrange("p (b i) -> p b i", b=2),
                             bcm(bp[:, hl * 2:(hl + 1) * 2], 128), ALU.mult)
                    P.tt('dve', ITT, T1, EG, ALU.mult)
                    pt = nb()
                    for hp in range(4):
                        P.mm(pt[:, hp * 128:(hp + 1) * 128], XX[0][:, hp, :], IDB)
                    P.copy('act', XXT[0].rearrange("p h i -> p (h i)"), pt[:, :])
                    MPm, LPm = XX[0], XXT[0]
                    Wc = [WW[0], WW[1]]
                    Wtc = [WWT[0], WWT[1]]
                    m0 = cmb(B_LV)
                    P.tt('dve', T1, LPm, bch(m0, 4), ALU.mult)
                    P.stt(Wc[0], T1, -1.0, bch(IDB, 4), ALU.mult, ALU.add)
                    P.tt('dve', T1, MPm, bch(m0, 4), ALU.mult)
                    P.stt(Wtc[0], T1, -1.0, bch(IDB, 4), ALU.mult, ALU.add)
                    cur = 0
                    for lev in range(1, 6):
                        ml = cmb(B_LV + lev)
                        CTS_, CS_ = CTSd[lev % 2], CSd[lev % 2]
                        P.tt('dve', CTS_, MPm, bch(ml, 4), ALU.mult)
                        P.tt('dve', CS_, LPm, bch(ml, 4), ALU.mult)
                        p1 = nb()
                        for hp in range(4):
                            P.mm(p1[:, hp * 128:(hp + 1) * 128], CTS_[:, hp, :], Wc[cur][:, hp, :])
                        P.copy('act', YY.rearrange("p h i -> p (h i)"), p1[:, :])
                        p2 = nb()
                        for hp in range(4):
                            P.mm(p2[:, hp * 128:(hp + 1) * 128], CS_[:, hp, :], Wtc[cur][:, hp, :])
                        P.copy('act', YYT.rearrange("p h i -> p (h i)"), p2[:, :])
                        p3 = nb()
                        for hp in range(4):
                            P.mm(p3[:, hp * 128:(hp + 1) * 128], Wtc[cur][:, hp, :], YY[:, hp, :])
                        P.tt('dve', Wc[1 - cur], Wc[cur], p3.rearrange("p (h i) -> p h i", h=4), ALU.subtract)
                        p4 = nb()
                        for hp in range(4):
                            P.mm(p4[:, hp * 128:(hp + 1) * 128], Wc[cur][:, hp, :], YYT[:, hp, :])
                        P.tt('dve', Wtc[1 - cur], Wtc[cur], p4.rearrange("p (h i) -> p h i", h=4), ALU.subtract)
                        cur = 1 - cur
                    PM = Wtc[cur]
                    pu = nb()
                    for h in range(4):
                        hp = hpi(h)
                        P.mm(pu[:, hp * 64:(hp + 1) * 64], PM[:, hp, :], CTG[:, t, 256 + h * 64:256 + (h + 1) * 64])
                    P.copy('dve', UU, pu[:, 0:256])
                    kv = CTG[:, t, 0:256].rearrange("p (hh hl f) -> p hh hl f", hh=2, hl=2)
                    for hl in range(2):
                        P.tt('dve', KG[:, hl * 2:(hl + 1) * 2, :], kv[:, :, hl, :], bcm(EGC[:, hl * 2:(hl + 1) * 2], 64), ALU.mult)
                    pw = nb()
                    for hp in range(4):
                        hl, hh = hp // 2, hp % 2
                        P.mm(pw[hl * 64:(hl + 1) * 64, hh * 128:(hh + 1) * 128], KG[:, hp, :], PM[:, hp, :])
                    P.copy('act', WTG.rearrange("p h i -> p (h i)"), pw[:, 0:256])
                    for c in range(2):
                        rows = slice(c * 64, (c + 1) * 64)
                        lastcol = c * 64 + (63 if d == 0 else 0)
                        P.tt('dve', F1[rows, :], EG[rows, :, lastcol], bp[rows, :], ALU.mult)
                    for hl in range(2):
                        P.tt('dve', KD[:, hl * 2:(hl + 1) * 2, :], kv[:, :, hl, :], bcm(F1[:, hl * 2:(hl + 1) * 2], 64), ALU.mult)
                    for c in ([0, 1] if d == 0 else [1, 0]):
                        rows = slice(c * 64, (c + 1) * 64)
                        ctok = slice(t * 128 + c * 64, t * 128 + (c + 1) * 64)
                        pr = nb2()
                        for hp in range(4):
                            hl, hh = hp // 2, hp % 2
                            sblk = SGB[d][hl * 64:(hl + 1) * 64, hh * 64:(hh + 1) * 64]
                            P.mm(pr[rows, hl, hh * 64:(hh + 1) * 64], WTG[hl * 64:(hl + 1) * 64, hh, c * 64:(c + 1) * 64], sblk)
                            if need_o:
                                P.mm(pr[rows, hl, 128 + hh * 64:128 + (hh + 1) * 64], CFQK[hl * 64:(hl + 1) * 64, hh, ctok], sblk)
                        P.tt('dve', VN[rows, :].rearrange("p (a b) -> p a b", a=2), UU[rows, :].rearrange("p (a b) -> p a b", a=2),
                             pr[rows, :, 0:128], ALU.subtract)
                        if need_o:
                            for hl in range(2):
                                P.tt('dve', OA[rows, hl * 128:(hl + 1) * 128].rearrange("p (a b) -> p a b", a=2),
                                     pr[rows, hl, 128:256].rearrange("p (a b) -> p a b", a=2),
                                     bcm(EGC[rows, hl * 2:(hl + 1) * 2], 64), ALU.mult)
                            po = nb()
                            for hp in range(4):
                                P.mm(po[rows, hp * 64:(hp + 1) * 64], ITT[rows, hp, c * 64:(c + 1) * 64], VN[rows, hp * 64:(hp + 1) * 64])
                            P.tt('dve', OA2[rows, :], OA[rows, :], po[rows, 0:256], ALU.add)
                            mv = MIX[rows, t, 384:640].rearrange("p (hh hl f) -> p hh hl f", hh=2, hl=2)
                            for hl in range(2):
                                src = OA2[rows, hl * 128:(hl + 1) * 128].rearrange("p (a b) -> p a b", a=2)
                                if d == 0:
                                    P.copy('dve', mv[:, :, hl, :], src)
                                else:
                                    P.tt('dve', mv[:, :, hl, :], mv[:, :, hl, :], src, ALU.add)
                        psu = nb()
                        for hp in range(4):
                            hl, hh = hp // 2, hp % 2
                            P.mm(psu[hl * 64:(hl + 1) * 64, hh * 64:(hh + 1) * 64], KD[rows, hp, :], VN[rows, hp * 64:(hp + 1) * 64])
                        P.tt('dve', SG[d].rearrange("p (a b) -> p a b", a=2), SG[d].rearrange("p (a b) -> p a b", a=2),
                             bcm(CDF[:, c, :], 64), ALU.mult)
                        P.tt('dve', SG[d], SG[d], psu[:, 0:128], ALU.add)
                        P.copy('act', SGB[d], SG[d])
            dump("L%d_mixg" % l, MIX[:, :, 0:640])
            arena_off[0] = mark_ssd
            if stop == 'gdn':
                break

            arena_off[0] = mark_mix
            ROPE = al([128, 16, 2, 64], F32)
            P.dma(ROPE, rope_d)
            QKT = al([128, 4, T], BF16)
            VA = al([128, NT, 2, 65], BF16)
            NWB = al([128, 512], F32)
            SQF = al([128, 512], F32)
            QN = al([128, 512], F32)
            T1R = al([128, 512], F32)
            T2R = al([128, 512], F32)
            SINR = al([128, 512], F32)
            DST = al([128, 512], BF16)
            SSQ = SM[:, 160:168]
            P.dma(WS[0][:, :, 0:512], win_v[:, :, C_AQ:C_AQ + 512], q='pool')
            P.dma(WS[1][:, :, 0:128], win_v[:, :, C_AV:C_AV + 128], q='pool')
            P.ts('dve', NWB[:, 0:384].rearrange("p (h f) -> p h f", h=6), bch(R_QNW, 6), 0.125, None, ALU.mult)
            P.copy('dve', NWB[:, 384:512].rearrange("p (h f) -> p h f", h=2), bch(R_KNW, 2))
            P.memset('dve', VA[:, :, :, 64:65], 1.0)
            for t in range(NT):
                tok = slice(t * 128, (t + 1) * 128)
                pa = nb()
                pv = nb()
                for k in range(8):
                    P.mm(pa[:, 0:512], HT[:, k, tok], WS[0][:, k, 0:512], start=(k == 0), stop=(k == 7))
                for k in range(8):
                    P.mm(pv[:, 0:128], HT[:, k, tok], WS[1][:, k, 0:128], start=(k == 0), stop=(k == 7))
                P.copy('act', VA[:, t, :, 0:64], pv[:, 0:128].rearrange("p (g f) -> p g f", g=2))
                P.act(SQF, pa[:, 0:512], AF.Square)
                P.reduce('dve', SSQ, SQF.rearrange("p (h f) -> p h f", h=8))
                P.ts('dve', SSQ, SSQ, 1.0 / 64, EPS, ALU.mult, ALU.add)
                P.act(SSQ, SSQ, AF.Sqrt)
                P.recip(SSQ, SSQ)
                P.tt('dve', QN.rearrange("p (h f) -> p h f", h=8), pa[:, 0:512].rearrange("p (h f) -> p h f", h=8),
                     bcm(SSQ, 64), ALU.mult)
                P.tt('dve', QN, QN, NWB, ALU.mult)
                if t >= 2:
                    P.tt('dve', T1R.rearrange("p (h f) -> p h f", h=8), QN.rearrange("p (h f) -> p h f", h=8),
                         bch(ROPE[:, t - 2, 0, :], 8), ALU.mult)
                    P.copy('act', SINR.rearrange("p (h f) -> p h f", h=8), bch(ROPE[:, t - 2, 1, :], 8))
                    qv = QN.rearrange("p (ha s f) -> p ha s f", s=2, f=16)
                    sv = SINR.rearrange("p (ha s f) -> p ha s f", s=2, f=16)
                    tv = T2R.rearrange("p (ha s f) -> p ha s f", s=2, f=16)
                    P.tt('dve', tv[:, :, 0, :], qv[:, :, 1, :], sv[:, :, 0, :], ALU.mult)
                    P.tt('dve', tv[:, :, 1, :], qv[:, :, 0, :], sv[:, :, 1, :], ALU.mult)
                    P.tt('dve', T1R, T1R, T2R, ALU.add)
                    srcq = T1R
                else:
                    srcq = QN
                P.copy('act', DST[:, 0:128], srcq[:, 384:512])
                dq = DST[:, 128:512].rearrange("p (a g f) -> p a g f", a=3, g=2)
                for g in range(2):
                    P.copy('dve', dq[:, :, g, :], srcq[:, g * 192:(g + 1) * 192].rearrange("p (a f) -> p a f", a=3))
                ptq = nb()
                for j in range(4):
                    P.mm(ptq[:, j * 128:(j + 1) * 128], DST[:, j * 128:(j + 1) * 128], IDB)
                P.copy('act', QKT[:, :, tok], ptq.rearrange("p (j i) -> p j i", j=4))
            dump("L%d_qkt" % l, QKT)
            dump("L%d_va" % l, VA)
            if stop == 'aprep':
                break
            PT = [al([128, 512], BF16), al([128, 512], BF16), al([128, 512], BF16)]
            AO = al([128, 4, 384], F32)
            REC = SM[:, 176:180]
            qblocks = [(256 + 512 * i, 512, list(range(NT))) for i in range(4)]
            if not last:
                qblocks = [(0, 256, [0, 1])] + qblocks
            import os
            qblocks = qblocks[:int(os.environ.get('ATT_N', '99'))]
            pctr = 0
            NROT[0] = 6
            actr = 0
            for (q0, nq, ktiles) in qblocks:
                nj = nq // 128
                for g in range(2):
                    for a in range(3):
                        h = 3 * g + a
                        pacc = PS[:, 6 + actr % 2, :]
                        actr += 1
                        for kt in ktiles:
                            ps = nb()
                            P.mm(ps[:, 0:nq], QKT[g * 64:(g + 1) * 64, 0, kt * 128:(kt + 1) * 128],
                                 QKT[g * 64:(g + 1) * 64, 1 + a, q0:q0 + nq])
                            pt = PT[pctr % 3]
                            pctr += 1
                            P.act(pt[:, 0:nq], ps[:, 0:nq], AF.Exp)
                            for j in range(nj):
                                P.mm(pacc[:, j * 65:(j + 1) * 65], pt[:, j * 128:(j + 1) * 128], VA[:, kt, g, :],
                                     start=(kt == ktiles[0] and j == 0), stop=(kt == ktiles[-1] and j == nj - 1))
                        pv3 = pacc[:, 0:nj * 65].rearrange("p (j c) -> p j c", c=65)
                        P.recip(REC[:, 0:nj], pv3[:, :, 64])
                        P.tt('dve', AO[:, 0:nj, h * 64:(h + 1) * 64], pv3[:, :, 0:64], bcm(REC[:, 0:nj], 64), ALU.mult)
                tq = q0 // 128
                P.copy('act', MIX[:, tq:tq + nj, 640:1024], AO[:, 0:nj, :])
            NROT[0] = 8
            dump("L%d_mixa" % l, MIX[:])
            if stop == 'attn':
                break

            arena_off[0] = mark_gate
            WZ = al([128, 8, 1024], BF16)
            WO = al([128, 8, 1024], BF16)
            P.dma(WZ[:, :, 0:384], win_v[:, :, C_SSDZ:C_SSDZ + 384], q='pool')
            P.dma(WZ[:, :, 384:640], win_v[:, :, C_GDNZ:C_GDNZ + 256], q='pool')
            P.dma(WZ[:, :, 640:1024], win_v[:, :, C_AZ:C_AZ + 384], q='pool')
            P.dma(WO[:, :, :], wout_d[l].rearrange("(k p) n -> p k n", p=128), q='pool')
            XTL = [al([128, D], F32), al([128, D], F32)]
            ZS = al([128, 1024], F32)
            G1 = al([128, 1024], F32)
            MB = al([128, 1024], BF16)
            MT = al([128, 8, 128], BF16)
            UPD = al([128, 1024], F32)
            JK = al([128, 384], F32)
            FS = SM[:, 192:200]
            ftiles = list(range(NT)) if not last else list(range(2, NT))
            for t in ftiles:
                tok = slice(t * 128, (t + 1) * 128)
                v = 1 if t < 2 else 0
                if t >= 2:
                    xt = XTL[t % 2]
                    P.dma(xt, xsrc[(t - 2) * 128:(t - 1) * 128, :])
                else:
                    xt = XC[:, t, :]
                pz = [nb(), nb()]
                for nbk in range(2):
                    for k in range(8):
                        P.mm(pz[nbk][:, :], HT[:, k, tok], WZ[:, k, nbk * 512:(nbk + 1) * 512], start=(k == 0), stop=(k == 7))
                P.act(ZS[:, 0:512], pz[0][:, :], AF.Silu)
                P.act(ZS[:, 512:1024], pz[1][:, :], AF.Silu)
                P.tt('dve', G1[:, 0:384], MIX[:, t, 0:384], ZS[:, 0:384], ALU.mult)
                P.act(JK, G1[:, 0:384], AF.Square, accum_out=FS[:, 0:1])
                P.ts('dve', FS[:, 1:2], FS[:, 0:1], 1.0 / 384, EPS, ALU.mult, ALU.add)
                P.act(FS[:, 1:2], FS[:, 1:2], AF.Sqrt)
                P.recip(FS[:, 1:2], FS[:, 1:2])
                P.stt(MB[:, 0:384], G1[:, 0:384], FS[:, 1:2], R_SNW, ALU.mult, ALU.mult)
                P.act(JK[:, 0:256], MIX[:, t, 384:640], AF.Square)
                P.reduce('dve', FS[:, 4:8], JK[:, 0:256].rearrange("p (h f) -> p h f", h=4))
                P.ts('dve', FS[:, 4:8], FS[:, 4:8], 1.0 / 64, EPS, ALU.mult, ALU.add)
                P.act(FS[:, 4:8], FS[:, 4:8], AF.Sqrt)
                P.recip(FS[:, 4:8], FS[:, 4:8])
                g3 = G1[:, 384:640].rearrange("p (h f) -> p h f", h=4)
                P.tt('dve', g3, MIX[:, t, 384:640].rearrange("p (h f) -> p h f", h=4), bcm(FS[:, 4:8], 64), ALU.mult)
                P.tt('dve', g3, g3, bch(R_GNW, 4), ALU.mult)
                P.tt('dve', MB[:, 384:640], G1[:, 384:640], ZS[:, 384:640], ALU.mult)
                P.tt('dve', MB[:, 640:1024], MIX[:, t, 640:1024], ZS[:, 640:1024], ALU.mult)
                for half in range(2):
                    ptm = nb()
                    for j in range(4):
                        kc = half * 4 + j
                        P.mm(ptm[:, j * 128:(j + 1) * 128], MB[:, kc * 128:(kc + 1) * 128], IDB)
                    P.copy('act' if half else 'dve', MT[:, half * 4:(half + 1) * 4, :], ptm.rearrange("p (j i) -> p j i", j=4))
                for nbk in range(2):
                    po = nb()
                    for kc in range(8):
                        P.mm(po[:, :], MT[:, kc, :], WO[:, kc, nbk * 512:(nbk + 1) * 512], start=(kc == 0), stop=(kc == 7))
                    P.tt('dve', UPD[:, nbk * 512:(nbk + 1) * 512], po[:, :], GATE[:, v, nbk * 512:(nbk + 1) * 512], ALU.mult)
                if t >= 2:
                    P.tt('dve', xt, xt, UPD, ALU.add)
                    P.dma(xdst[(t - 2) * 128:(t - 1) * 128, :], xt)
                else:
                    P.tt('dve', XC[:, t, :], XC[:, t, :], UPD, ALU.add)
            dump("L%d_wz" % l, WZ)
            dump("L%d_wo" % l, WO)
            if stop == 'L0':
                dump("L%d_xc" % l, XC[:])
                break

        dump("final_mix", MIX[:])
        P.finish([out_d] + list(dbg_out.values()))
        P.emit()
    return nc, dbg_out


def prep_inputs(inputs):
    f = lambda a: np.ascontiguousarray(np.asarray(a, dtype=np.float32))
    constm, constb, rope, cm = host_consts()
    c = f(inputs['c'])
    c_ctx = f(inputs['c_ctx'])
    norm_w = f(inputs['norm_w'])
    b_mod = f(inputs['b_mod'])
    conv_w = f(inputs['conv_w'])
    conv_b = f(inputs['conv_b'])
    rows = np.zeros((2, 1024), np.float32)
    rows[:, 0:12] = f(inputs['ssd_dt_bias']).reshape(2, 12)
    rows[:, 12:24] = f(inputs['ssd_A_log']).reshape(2, 12)
    rows[:, 24:30] = f(inputs['ssd_D'])
    rows[:, 32:40] = f(inputs['gdn_A_log']).reshape(2, 8)
    rows[:, 40:48] = f(inputs['gdn_dt_bias']).reshape(2, 8)
    rows[:, 64:128] = f(inputs['gdn_norm_w'])
    rows[:, 128:192] = f(inputs['q_norm_w'])
    rows[:, 192:256] = f(inputs['k_norm_w'])
    rows[:, 256:640] = f(inputs['ssd_norm_w'])
    shared = {
        'w_mod': f(inputs['w_mod']), 'b_mod': b_mod, 'w_in': f(inputs['w_in']), 'w_out': f(inputs['w_out']),
        'normwc': np.ascontiguousarray(norm_w.reshape(2, 8, 128).transpose(2, 0, 1)),
        'bmodc': np.ascontiguousarray(b_mod[:, 0:2048].reshape(2, 16, 128).transpose(2, 0, 1)),
        'convwc': np.ascontiguousarray(conv_w.reshape(2, 3, 11, 128).transpose(3, 0, 2, 1)),
        'convbc': np.ascontiguousarray(conv_b.reshape(2, 11, 128).transpose(2, 0, 1)),
        'rows': rows, 'constm': constm, 'constb': constb, 'rope': rope, 'cm': cm,
    }
    x = f(inputs['x'])
    ctx = f(inputs['ctx'])
    maps = []
    for b in range(x.shape[0]):
        m = dict(shared)
        m['x'] = x[b]
        m['ctx'] = ctx[b]
        cc = np.stack([c[b].reshape(8, 128).T, c_ctx.reshape(8, 128).T], axis=-1)
        m['cc'] = np.ascontiguousarray(cc)
        maps.append(m)
    return maps


def kernel(**inputs):
    maps = prep_inputs(inputs)
    nc, _ = build()
    res = run_bass_kernel_spmd(nc, maps, core_ids=list(range(len(maps))))
    return np.stack([np.asarray(r["out"], dtype=np.float32) for r in res.results], axis=0)
```

```python
import math
import numpy as np
import ml_dtypes
import concourse.bass as bass
import concourse.mybir as mybir
from concourse.bass_utils import run_bass_kernel_spmd

F32 = mybir.dt.float32
BF16 = mybir.dt.bfloat16
AF = mybir.ActivationFunctionType
ALU = mybir.AluOpType
AX = mybir.AxisListType
ENGS = ['pe', 'act', 'dve', 'pool', 'sp']
ISZ = {F32: 4, BF16: 2}

T = 2304
NT = 18
D = 1024
EPS = 1e-6
BIG = 30000.0


class Op:
    __slots__ = ('eng', 'fn', 'is_dma', 'eidx', 'gidx', 'waits', 'signal', 'sem', 'val', 'prevdma')


class Prog:
    NDMA = 24

    def __init__(self, nc):
        self.nc = nc
        self.ops = []
        self.eops = {e: [] for e in ENGS}
        self.track = {}
        self.waited = {e: {} for e in ENGS}
        self.dma_waited = {e: set() for e in ENGS}
        self.ndma = 0
        self.dma_last = {}
        self.addr = {}
        self.pool_ok = False
        self.fence_scr = None
        self.nsw = 0
        self.strict = False

    def region(self, ap):
        t = ap.tensor
        name = ap.name
        pairs = ap.ap
        off = int(ap.offset)
        sp = str(ap.space)
        if sp in ('SB', 'PSUM'):
            shp = tuple(t.shape)
            fsz = 1
            for s in shp[1:]:
                fsz *= s
            p0 = off // fsz
            f0 = off % fsz
            pst, pc = pairs[0]
            p1 = p0 + (pc if pst > 0 else 1)
            ext = 0
            for st, c in pairs[1:]:
                ext += abs(st) * (c - 1)
            isz = ISZ.get(ap.dtype, 4)
            if sp == 'SB':
                base = self.addr[name]
                return ('SB', p0, p1, base + f0 * isz, base + (f0 + ext + 1) * isz)
            b0 = (f0 * isz) // 2048
            b1 = ((f0 + ext + 1) * isz + 2047) // 2048
            return (name, (p0 // 32) * 32, ((p1 + 31) // 32) * 32, b0 * 2048, b1 * 2048)
        ext = 0
        for st, c in pairs:
            ext += abs(st) * (c - 1)
        return (name, 0, 1, off, off + ext + 1)

    def add(self, eng, fn, reads, writes, is_dma=False):
        if eng == 'pool' and not is_dma and not self.pool_ok:
            eng = 'dve'
        op = Op()
        op.eng = eng
        op.fn = fn
        op.is_dma = is_dma
        op.signal = False
        op.gidx = len(self.ops)
        op.eidx = len(self.eops[eng])
        op.sem = None
        op.val = 0
        op.prevdma = None
        deps = []
        rregs = [self.region(a) for a in reads]
        wregs = [self.region(a) for a in writes]

        def ov(a, b):
            return a[1] < b[2] and b[1] < a[2] and a[3] < b[4] and b[3] < a[4]

        def cov(a, b):
            return a[1] <= b[1] and a[2] >= b[2] and a[3] <= b[3] and a[4] >= b[4]

        for r in rregs:
            tr = self.track.get(r[0])
            if tr is None:
                continue
            for (wr, wop) in tr[0]:
                if ov(r, wr):
                    deps.append((wop, 'RAW'))
        for w in wregs:
            tr = self.track.get(w[0])
            if tr is None:
                continue
            for (wr, wop) in tr[0]:
                if ov(w, wr):
                    deps.append((wop, 'WAW'))
            for (rr, rop) in tr[1]:
                if ov(w, rr):
                    deps.append((rop, 'WAR'))
        for w in wregs:
            tr = self.track.setdefault(w[0], [[], []])
            tr[0] = [(wr, wop) for (wr, wop) in tr[0] if not cov(w, wr)]
            tr[1] = [(rr, rop) for (rr, rop) in tr[1] if not cov(w, rr)]
            tr[0].append((w, op))
        for r in rregs:
            tr = self.track.setdefault(r[0], [[], []])
            if not is_dma:
                tr[1] = [(rr, rop) for (rr, rop) in tr[1]
                         if not (rop.eng == eng and not rop.is_dma and cov(r, rr))]
            tr[1].append((r, op))
        need = {}
        for (p, kind) in deps:
            if p is op:
                continue
            if p.is_dma:
                if p.gidx in self.dma_waited[eng]:
                    continue
                need[('dma', p.gidx)] = p
            else:
                if p.eng == eng:
                    if eng == 'pe':
                        continue
                    if kind != 'RAW' and not is_dma and not self.strict:
                        continue
                if eng != p.eng and kind == 'RAW' and p.eng in ('dve', 'act') and self.fence_scr is not None:
                    lst = self.eops[p.eng]
                    if p.eidx + 1 >= len(lst):
                        self._fence(p.eng)
                    p = lst[p.eidx + 1]
                if p.eidx <= self.waited[eng].get(p.eng, -1):
                    continue
                cur = need.get(p.eng)
                if cur is None or p.eidx > cur.eidx:
                    need[p.eng] = p
        for k, p in need.items():
            p.signal = True
            if p.is_dma:
                self.dma_waited[eng].add(p.gidx)
            else:
                self.waited[eng][p.eng] = p.eidx
        op.waits = list(need.values())
        if is_dma and eng == 'pool':
            op.sem = ('sw', self.nsw)
            self.nsw += 1
            op.signal = True
        elif is_dma:
            slot = self.ndma % self.NDMA
            self.ndma += 1
            prev = self.dma_last.get(slot)
            if prev is not None and prev.gidx not in self.dma_waited[eng]:
                op.prevdma = prev
                self.dma_waited[eng].add(prev.gidx)
            self.dma_last[slot] = op
            op.sem = slot
            op.signal = True
        op.gidx = len(self.ops)
        op.eidx = len(self.eops[eng])
        self.ops.append(op)
        self.eops[eng].append(op)
        return op

    def _fence(self, eng):
        scr = self.fence_scr
        if eng == 'dve':
            return self.add('dve', lambda e: e.memset(scr[:, 0:1], 0.0), [], [scr[:, 0:1]])
        return self.add('act', lambda e: e.memzero(scr[:, 2:3]), [], [scr[:, 2:3]])

    def mm(self, out, lhsT, rhs, start=True, stop=True):
        rd = [lhsT, rhs] + ([] if start else [out])
        return self.add('pe', lambda e: e.matmul(out, lhsT, rhs, start=start, stop=stop), rd, [out])

    def act(self, out, in_, func, bias=None, scale=1.0, accum_out=None):
        rd = [in_]
        kw = {}
        if bias is not None:
            kw['bias'] = bias
            if not isinstance(bias, (int, float)):
                rd.append(bias)
        if not isinstance(scale, (int, float)):
            rd.append(scale)
        wr = [out]
        if accum_out is not None:
            kw['accum_out'] = accum_out
            wr.append(accum_out)
        return self.add('act', lambda e: e.activation(out, in_, func, scale=scale, **kw), rd, wr)

    def tt(self, eng, out, in0, in1, op):
        return self.add(eng, lambda e: e.tensor_tensor(out, in0, in1, op), [in0, in1], [out])

    def ts(self, eng, out, in0, s1, s2, op0, op1=None):
        rd = [in0]
        if not isinstance(s1, (int, float)):
            rd.append(s1)
        if s2 is not None and not isinstance(s2, (int, float)):
            rd.append(s2)
        if op1 is None:
            return self.add(eng, lambda e: e.tensor_scalar(out, in0, s1, None, op0), rd, [out])
        return self.add(eng, lambda e: e.tensor_scalar(out, in0, s1, s2, op0, op1), rd, [out])

    def stt(self, out, in0, scalar, in1, op0, op1):
        rd = [in0, in1]
        if not isinstance(scalar, (int, float)):
            rd.append(scalar)
        return self.add('dve', lambda e: e.scalar_tensor_tensor(out, in0, scalar, in1, op0, op1), rd, [out])

    def copy(self, eng, out, in_):
        if eng == 'act':
            return self.add(eng, lambda e: e.copy(out, in_), [in_], [out])
        return self.add(eng, lambda e: e.tensor_copy(out, in_), [in_], [out])

    def memset(self, eng, out, val):
        return self.add(eng, lambda e: e.memset(out, val), [], [out])

    def reduce(self, eng, out, in_, op=None):
        op = ALU.add if op is None else op
        return self.add(eng, lambda e: e.tensor_reduce(out, in_, AX.X, op), [in_], [out])

    def recip(self, out, in_):
        return self.add('dve', lambda e: e.reciprocal(out, in_), [in_], [out])

    def dma(self, out, in_, q='sp'):
        return self.add(q, lambda e: e.dma_start(out=out, in_=in_), [in_], [out], is_dma=True)

    def finish(self, aps, q='sp'):
        return self.add(q, None, list(aps), [])

    def emit(self):
        nc = self.nc
        cnt = {e: 0 for e in ENGS}
        dcnt = {}
        for op in self.ops:
            if op.is_dma and isinstance(op.sem, tuple):
                op.val = 16
            elif op.is_dma:
                dcnt[op.sem] = dcnt.get(op.sem, 0) + 16
                op.val = dcnt[op.sem]
            elif op.signal:
                cnt[op.eng] += 1
                op.val = cnt[op.eng]
        self.stats = {e: (len(self.eops[e]), cnt[e]) for e in ENGS}
        esem = {e: nc.alloc_semaphore('s_' + e) for e in ENGS}
        dsem = [nc.alloc_semaphore('s_dma%d' % i) for i in range(self.NDMA)]
        swsem = [nc.alloc_semaphore('s_sw%d' % i) for i in range(self.nsw)]

        def semof(p):
            if p.is_dma:
                return swsem[p.sem[1]] if isinstance(p.sem, tuple) else dsem[p.sem]
            return esem[p.eng]

        def run(ename, e):
            for op in self.eops[ename]:
                if op.prevdma is not None:
                    e.wait_ge(dsem[op.prevdma.sem], op.prevdma.val)
                for p in op.waits:
                    e.wait_ge(semof(p), p.val)
                if op.fn is None:
                    continue
                ins = op.fn(e)
                if op.is_dma:
                    ins.then_inc(semof(op), 16)
                elif op.signal:
                    ins.then_inc(esem[op.eng], 1)

        with nc.Block() as block:
            @block.tensor
            def _(e):
                run('pe', e)

            @block.scalar
            def _(e):
                run('act', e)

            @block.vector
            def _(e):
                run('dve', e)

            @block.gpsimd
            def _(e):
                run('pool', e)

            @block.sync
            def _(e):
                run('sp', e)
                for slot, v in dcnt.items():
                    e.wait_ge(dsem[slot], v)
                for sm in swsem:
                    e.wait_ge(sm, 16)


def bcm(ap, n):
    s = list(ap.shape)
    return ap.unsqueeze(len(s)).broadcast_to(s + [n])


def bch(ap, n):
    s = list(ap.shape)
    return ap.unsqueeze(1).broadcast_to([s[0], n] + s[1:])


C_SSDZ, C_SSDDT, C_GDNZ, C_GDNA, C_GDNB = 1408, 1792, 1804, 2060, 2068
C_AQ, C_AK, C_AV, C_AZ = 2076, 2460, 2588, 2716

(F_ID, F_TRIF, F_TRIB, F_TBF, F_TBB, F_ONES, F_NONES) = range(7)
NF_ = 7
(B_ID, B_NMF, B_NMB, B_NBF, B_NBB, B_SBF, B_SBB, B_BLK, B_TRIF, B_TRIB, B_TBF, B_TBB, B_ONES, B_NONES) = range(14)
B_LV = 14
NB_ = 20


def host_consts():
    t = np.arange(128)
    tri_f = (t[:, None] <= t[None, :]).astype(np.float32)
    tri_b = (t[:, None] >= t[None, :]).astype(np.float32)
    blk = ((t[:, None] // 64) == (t[None, :] // 64)).astype(np.float32)
    m = np.zeros((128, NF_, 128), np.float32)
    m[:, F_ID] = np.eye(128)
    m[:, F_TRIF] = tri_f
    m[:, F_TRIB] = tri_b
    m[:, F_TBF] = tri_f * blk
    m[:, F_TBB] = tri_b * blk
    m[:, F_ONES] = 1.0
    m[:, F_NONES] = -1.0
    mb = np.zeros((128, NB_, 128), np.float32)
    mb[:, B_ID] = np.eye(128)
    mb[:, B_NMF] = (tri_f - 1) * BIG
    mb[:, B_NMB] = (tri_b - 1) * BIG
    mb[:, B_NBF] = (tri_f * blk - 1) * BIG
    mb[:, B_NBB] = (tri_b * blk - 1) * BIG
    mb[:, B_SBF] = (t[:, None] < t[None, :]) * blk
    mb[:, B_SBB] = (t[:, None] > t[None, :]) * blk
    mb[:, B_BLK] = blk
    mb[:, B_TRIF] = tri_f
    mb[:, B_TRIB] = tri_b
    mb[:, B_TBF] = tri_f * blk
    mb[:, B_TBB] = tri_b * blk
    mb[:, B_ONES] = 1.0
    mb[:, B_NONES] = -1.0
    for sl in range(6):
        mb[:, B_LV + sl] = ((t[:, None] >> (sl + 1)) == (t[None, :] >> (sl + 1))) & ((t[:, None] >> sl) != (t[None, :] >> sl))
    nf = 16
    freqs = (10000.0 ** (-np.arange(nf, dtype=np.float32) / nf)).astype(np.float32)
    pos = np.arange(2048)
    ar = (pos // 64).astype(np.float32)[:, None] * freqs
    ac = (pos % 64).astype(np.float32)[:, None] * freqs
    cr, sr, cc, sc = np.cos(ar), np.sin(ar), np.cos(ac), np.sin(ac)
    cosf = np.concatenate([cr, cr, cc, cc], 1).astype(np.float32)
    sinf = np.concatenate([-sr, sr, -sc, sc], 1).astype(np.float32)
    rope = np.stack([cosf, sinf], 1).reshape(16, 128, 2, 64).transpose(1, 0, 2, 3)
    cm = np.zeros((128, 2), np.float32)
    cm[:64, 0] = 1
    cm[64:, 1] = 1
    return np.ascontiguousarray(m), np.ascontiguousarray(mb), np.ascontiguousarray(rope), cm


def build(n_layers=2, stop=None, dbg=()):
    nc = bass.Bass("TRN2", target_bir_lowering=False)
    P = Prog(nc)
    ins = {}

    def din(name, shape, dt=F32):
        ins[name] = nc.dram_tensor(name, list(shape), dt, kind="ExternalInput").ap()
        return ins[name]

    x_d = din("x", [2048, D])
    ctx_d = din("ctx", [256, D])
    cc_d = din("cc", [128, 8, 2])
    wmod_d = din("w_mod", [2, D, 3072])
    bmod_d = din("b_mod", [2, 3072])
    win_d = din("w_in", [2, D, 3100])
    wout_d = din("w_out", [2, D, D])
    normwc_d = din("normwc", [128, 2, 8])
    bmodc_d = din("bmodc", [128, 2, 16])
    convwc_d = din("convwc", [128, 2, 11, 3])
    convbc_d = din("convbc", [128, 2, 11])
    rows_d = din("rows", [2, 1024])
    constm_d = din("constm", [128, NF_, 128])
    constb_d = din("constb", [128, NB_, 128])
    rope_d = din("rope", [128, 16, 2, 64])
    cm_d = din("cm", [128, 2])
    out_d = nc.dram_tensor("out", [2048, D], F32, kind="ExternalOutput").ap()
    x1s_d = nc.dram_tensor("x1s", [2048, D], F32, kind="Internal").ap()
    dbg_out = {}

    from contextlib import ExitStack
    with ExitStack() as es:
        def sb(name, shape, dt):
            h = es.enter_context(nc.sbuf_tensor(name, list(shape), dt))
            P.addr[name] = int(nc.lookup_mloc(h).addr)
            return h

        PS = es.enter_context(nc.psum_tensor("PS", [128, 8, 512], F32))
        bank_ctr = [0]

        NROT = [8]

        def nb():
            b = bank_ctr[0] % NROT[0]
            bank_ctr[0] += 1
            return PS[:, b, :]

        HT = sb("HT", [128, 8, T], BF16)
        MIX = sb("MIX", [128, NT, 1024], BF16)
        XC = sb("XC", [128, 2, D], F32)
        CMF = sb("CMF", [128, NF_, 128], F32)
        CMB = sb("CMB", [128, NB_, 128], BF16)
        WS = [sb("WS0", [128, 8, 512], BF16), sb("WS1", [128, 8, 512], BF16)]
        SM = sb("SM", [128, 512], F32)
        DTR = sb("DTR", [128, NT, 28], F32)
        SDT = sb("SDT", [128, NT, 12], F32)
        SDA = sb("SDA", [128, NT, 12], F32)
        GG = sb("GG", [128, NT, 8], F32)
        GB = sb("GB", [128, NT, 8], F32)
        SDAH = sb("SDAH", [128, NT, 12], BF16)
        SDAL = sb("SDAL", [128, NT, 12], BF16)
        GGH = sb("GGH", [128, NT, 8], BF16)
        GGL = sb("GGL", [128, NT, 8], BF16)
        ROWS = sb("ROWS", [128, 1024], F32)
        MODC = sb("MODC", [128, 16, 2], F32)
        GC = sb("GC", [128, 8, 2], F32)
        SILC = sb("SILC", [128, 8, 2], BF16)
        SILB = sb("SILB", [128, 2, 8, 128], BF16)
        CCF = sb("CCF", [128, 8, 2], F32)
        NWC = sb("NWC", [128, 2, 8], F32)
        BMC = sb("BMC", [128, 2, 16], F32)
        CVW = sb("CVW", [128, 2, 11, 3], F32)
        CVB = sb("CVB", [128, 2, 11], F32)
        CMK = sb("CMK", [128, 2], F32)
        ARENA = sb("ARENA", [128, 22080], F32)
        arena_off = [0]

        def arena_reset():
            arena_off[0] = 0

        def al(shape, dt):
            n = 1
            for s in shape[1:]:
                n *= s
            nbytes = n * ISZ[dt]
            nwords = (nbytes + 3) // 4
            o = arena_off[0]
            assert o + nwords <= 22080, ("arena overflow", o, nwords)
            arena_off[0] = o + nwords
            v = ARENA[:, o:o + nwords]
            if dt != F32:
                v = v.bitcast(dt)[:, 0:n]
            if len(shape) == 2:
                return v
            names = ' '.join('a%d' % i for i in range(len(shape) - 1))
            kw = {'a%d' % i: shape[i + 1] for i in range(len(shape) - 2)}
            return v.rearrange("p (%s) -> p %s" % (names, names), **kw)

        def cmf(i):
            return CMF[:, i, :]

        def cmb(i):
            return CMB[:, i, :]

        IDB = cmb(B_ID)
        IDF = cmf(F_ID)
        ONESB = cmb(B_ONES)
        NONESB = cmb(B_NONES)

        def dump(name, ap):
            if name not in dbg:
                return
            d = nc.dram_tensor("dbg_" + name, list(ap.shape), ap.dtype, kind="ExternalOutput").ap()
            dbg_out[name] = d
            P.dma(d, ap)

        P.dma(CMF[:], constm_d)
        P.dma(CMB[:], constb_d, q='pool')
        P.dma(CCF[:], cc_d)
        P.dma(NWC[:], normwc_d)
        P.dma(BMC[:], bmodc_d)
        P.dma(CVW[:], convwc_d)
        P.dma(CVB[:], convbc_d)
        P.dma(CMK[:], cm_d)
        P.dma(XC[:], ctx_d.rearrange("(t p) d -> p t d", p=128))
        EPSC = SM[:, 0:1]
        P.memset('dve', EPSC, EPS)
        P.fence_scr = SM[:, 208:216]
        P.act(SILC[:], CCF[:], AF.Silu)
        for v in range(2):
            P.copy('dve', SILB[:, v, :, :], bcm(SILC[:, :, v], 128))

        def softplus(out, in_, tmp1, tmp2, eng='dve'):
            P.act(tmp1, in_, AF.Abs)
            P.act(tmp1, tmp1, AF.Exp, scale=-1.0)
            P.ts(eng, tmp1, tmp1, 1.0, None, ALU.add)
            P.act(tmp1, tmp1, AF.Ln)
            P.ts(eng, tmp2, in_, 0.0, None, ALU.max)
            P.tt(eng, out, tmp1, tmp2, ALU.add)

        import os as _os
        stop_in = stop
        stop_l = int(_os.environ.get('STOP_L', '0'))
        for l in range(n_layers):
            stop = stop_in if l == stop_l else None
            last = (l == n_layers - 1)
            xsrc = x_d if l == 0 else x1s_d
            xdst = out_d if last else x1s_d
            arena_reset()
            P.dma(ROWS[:], rows_d[l].partition_broadcast(128))
            R_DTB = ROWS[:, 0:12]
            R_AL = ROWS[:, 12:24]
            R_D = ROWS[:, 24:30]
            R_GAL = ROWS[:, 32:40]
            R_GDB = ROWS[:, 40:48]
            R_GNW = ROWS[:, 64:128]
            R_QNW = ROWS[:, 128:192]
            R_KNW = ROWS[:, 192:256]
            R_SNW = ROWS[:, 256:640]
            RA = SM[:, 8:20]
            RGA = SM[:, 20:28]
            P.act(RA, R_AL, AF.Exp)
            P.ts('dve', RA, RA, -1.0, None, ALU.mult)
            P.act(RGA, R_GAL, AF.Exp)
            P.ts('dve', RGA, RGA, -1.0, None, ALU.mult)

            GATE = al([128, 2, 1024], F32)
            mark_gate = arena_off[0]
            BG = al([128, 1024], F32)
            P.dma(BG, bmod_d[l, 2048:3072].partition_broadcast(128))
            wmod_v = wmod_d[l].rearrange("(k p) n -> p k n", p=128)
            for blk in range(6):
                ws = WS[blk % 2]
                P.dma(ws[:, :, :], wmod_v[:, :, blk * 512:(blk + 1) * 512], q='pool')
                if blk < 4:
                    pb = nb()
                    for j in range(4):
                        for k in range(8):
                            P.mm(pb[:, j * 2:(j + 1) * 2], ws[:, k, j * 128:(j + 1) * 128], SILC[:, k, :],
                                 start=(k == 0), stop=(k == 7))
                    P.tt('dve', MODC[:, blk * 4:(blk + 1) * 4, :],
                         pb[:, 0:8].rearrange("p (j v) -> p j v", v=2),
                         bcm(BMC[:, l, blk * 4:(blk + 1) * 4], 2), ALU.add)
                else:
                    for v in range(2):
                        pb = nb()
                        for k in range(8):
                            P.mm(pb[:, :], SILB[:, v, k, :], ws[:, k, :], start=(k == 0), stop=(k == 7))
                        P.tt('dve', GATE[:, v, (blk - 4) * 512:(blk - 3) * 512], pb[:, :],
                             BG[:, (blk - 4) * 512:(blk - 3) * 512], ALU.add)
            P.ts('dve', GC[:], MODC[:, 8:16, :], 1.0, None, ALU.add)
            P.tt('dve', GC[:], GC[:], bcm(NWC[:, l, :], 2), ALU.mult)
            dump("L%d_gc" % l, GC[:])
            dump("L%d_modc" % l, MODC[:])
            dump("L%d_gate" % l, GATE)

            arena_off[0] = mark_gate
            mark = arena_off[0]
            XIN = [al([128, 4, D], F32), al([128, 4, D], F32)]
            XN = [al([128, 4, D], BF16), al([128, 4, D], BF16)]
            JUNK = al([128, D], BF16)
            SS = SM[:, 32:50]
            RS = SM[:, 64:82]
            groups = [([0, 1], 1)] + [([2 + 4 * g + j for j in range(4)], 0) for g in range(4)]
            for gi, (tiles, v) in enumerate(groups):
                n = len(tiles)
                t0 = tiles[0]
                slot = gi % 2
                if v == 1:
                    xin = XC
                else:
                    xin = XIN[slot]
                    r0 = (t0 - 2) * 128
                    P.dma(xin[:, :, :], xsrc[r0:r0 + 512, :].rearrange("(t p) d -> p t d", p=128))
                for j, t in enumerate(tiles):
                    P.act(JUNK, xin[:, j, :], AF.Square, accum_out=SS[:, t:t + 1])
                P.ts('dve', RS[:, t0:t0 + n], SS[:, t0:t0 + n], 1.0 / D, EPS, ALU.mult, ALU.add)
                P.act(RS[:, t0:t0 + n], RS[:, t0:t0 + n], AF.Sqrt)
                P.recip(RS[:, t0:t0 + n], RS[:, t0:t0 + n])
                for j, t in enumerate(tiles):
                    P.ts('dve' if j % 2 == 0 else 'pool', XN[slot][:, j, :], xin[:, j, :], RS[:, t:t + 1], None, ALU.mult)
                for k in range(8):
                    pb = nb()
                    for j in range(n):
                        P.mm(pb[:, j * 128:(j + 1) * 128], XN[slot][:, j, k * 128:(k + 1) * 128], IDB)
                    dst = HT[:, k, t0 * 128:(t0 + n) * 128]
                    if k % 2 == 0:
                        P.ts('dve', dst, pb[:, 0:n * 128], GC[:, k, v:v + 1], MODC[:, k, v:v + 1], ALU.mult, ALU.add)
                    else:
                        P.act(dst, pb[:, 0:n * 128], AF.Identity, bias=MODC[:, k, v:v + 1], scale=GC[:, k, v:v + 1])
            dump("L%d_ht" % l, HT[:])
            arena_off[0] = mark
            if stop == 'p1':
                break

            win_v = win_d[l].rearrange("(k p) n -> p k n", p=128)
            WSM = al([128, 8, 28], BF16)
            P.dma(WSM[:, :, 0:12], win_v[:, :, C_SSDDT:C_SSDDT + 12], q='pool')
            P.dma(WSM[:, :, 12:28], win_v[:, :, C_GDNA:C_GDNA + 16], q='pool')
            for g0 in (0, 16):
                tiles = list(range(g0, min(g0 + 16, NT)))
                pb = nb()
                for j, t in enumerate(tiles):
                    for k in range(8):
                        P.mm(pb[:, j * 28:(j + 1) * 28], HT[:, k, t * 128:(t + 1) * 128], WSM[:, k, :],
                             start=(k == 0), stop=(k == 7))
                n = len(tiles)
                P.copy('dve', DTR[:, g0:g0 + n, :], pb[:, 0:n * 28].rearrange("p (t c) -> p t c", c=28))
            TMPA = al([128, NT, 12], F32)
            TMPB = al([128, NT, 12], F32)
            P.tt('dve', SDT[:], DTR[:, :, 0:12], bch(R_DTB, NT), ALU.add)
            softplus(SDT[:], SDT[:], TMPA, TMPB)
            P.tt('dve', SDA[:], SDT[:], bch(RA, NT), ALU.mult)
            P.copy('dve', SDAH[:], SDA[:])
            P.tt('dve', SDAL[:], SDA[:], SDAH[:], ALU.subtract)
            P.tt('dve', GG[:], DTR[:, :, 12:20], bch(R_GDB, NT), ALU.add)
            softplus(GG[:], GG[:], TMPA[:, :, 0:8], TMPB[:, :, 0:8])
            P.tt('dve', GG[:], GG[:], bch(RGA, NT), ALU.mult)
            P.act(GB[:], DTR[:, :, 20:28], AF.Sigmoid)
            P.copy('dve', GGH[:], GG[:])
            P.tt('dve', GGL[:], GG[:], GGH[:], ALU.subtract)
            dump("L%d_sdt" % l, SDT[:])
            dump("L%d_gg" % l, GG[:])
            dump("L%d_gb" % l, GB[:])
            dump("L%d_dtr" % l, DTR[:])
            if stop == 'dt':
                break

            mark_mix = arena_off[0]
            PREB = al([128, T + 4], BF16)
            DG = [al([128, 3, 128], BF16), al([128, 3, 128], BF16)]
            P.memset('pool', PREB[:, 0:1], 0.0)
            P.memset('pool', PREB[:, 257:259], 0.0)
            P.memset('pool', PREB[:, T + 3:T + 4], 0.0)
            TBK = [(0, 256)] + [(256 + 512 * i, 512) for i in range(4)]
            if stop == 'c0':
                dump("L%d_preb" % l, PREB)
                break
            cctr = [0]
            wsslot = [0]
            wsbase = [0]

            def conv_proj(ch, dest):
                dg = DG[cctr[0] % 2]
                cctr[0] += 1
                grp = {0: (0, 4), 4: (4, 5), 5: (5, 9), 9: (9, 11)}
                if ch in grp:
                    c0, c1 = grp[ch]
                    wsslot[0] = (wsslot[0] + 1) % 2
                    wsbase[0] = c0
                    P.dma(WS[wsslot[0]][:, :, 0:(c1 - c0) * 128], win_v[:, :, c0 * 128:c1 * 128], q='pool')
                ws = WS[wsslot[0]][:, :, (ch - wsbase[0]) * 128:(ch - wsbase[0] + 1) * 128]
                for k in range(3):
                    P.ts('pool', dg[:, k, :], IDF, CVW[:, l, ch, k:k + 1], None, ALU.mult)
                for bi, (t0, n) in enumerate(TBK):
                    pb = nb()
                    for k in range(8):
                        P.mm(pb[:, 0:n], ws[:, k, 0:128], HT[:, k, t0:t0 + n], start=(k == 0), stop=(k == 7))
                    po = t0 + 1 if t0 < 256 else t0 + 3
                    if bi % 2 == 0:
                        P.copy('dve', PREB[:, po:po + n], pb[:, 0:n])
                    else:
                        P.copy('act', PREB[:, po:po + n], pb[:, 0:n])
                if stop == 'c1':
                    return
                for bi, (t0, n) in enumerate(TBK):
                    po = t0 + 1 if t0 < 256 else t0 + 3
                    pb2 = nb()
                    for k in range(3):
                        P.mm(pb2[:, 0:n], dg[:, k, :], PREB[:, po - 1 + k:po - 1 + k + n], start=(k == 0), stop=(k == 2))
                    P.act(dest[:, t0:t0 + n], pb2[:, 0:n], AF.Silu, bias=CVB[:, l, ch:ch + 1])

            def to_tok(src, CTt, c0):
                for gi, g0 in enumerate(range(0, NT, 4)):
                    tiles = list(range(g0, min(g0 + 4, NT)))
                    n = len(tiles)
                    pb = nb()
                    for j, t in enumerate(tiles):
                        P.mm(pb[:, j * 128:(j + 1) * 128], src[:, t * 128:(t + 1) * 128], IDB)
                    P.copy('dve' if gi % 2 == 0 else 'act', CTt[:, g0:g0 + n, c0:c0 + 128],
                           pb[:, 0:n * 128].rearrange("p (t c) -> p t c", c=128))

            mark_ssd = arena_off[0]
            CTS = al([128, NT, 512], BF16)
            CFB = al([128, T], BF16)
            CFC = al([128, T], BF16)
            XF = [al([128, T], BF16), al([128, T], BF16)]
            if stop in ('c1', 'c2'):
                conv_proj(0, XF[0])
                dump("L%d_preb" % l, PREB)
                dump("L%d_xf" % l, XF[0])
                if stop == 'c2':
                    to_tok(XF[0], CTS, 0)
                    dump("L%d_cts" % l, CTS)
                break
            for ch in range(3):
                conv_proj(ch, XF[ch % 2])
                to_tok(XF[ch % 2], CTS, ch * 128)
            conv_proj(3, CFB)
            to_tok(CFB, CTS, 384)
            conv_proj(4, CFC)
            dump("L%d_cts" % l, CTS)
            dump("L%d_cfc" % l, CFC)
            if stop == 'conv':
                break
            RALH = [al([128, 6, 128], BF16), al([128, 6, 128], BF16)]
            RALL = [al([128, 6, 128], BF16), al([128, 6, 128], BF16)]
            EE = [al([128, 6, 128], F32), al([128, 6, 128], F32)]
            WTT = [al([128, 6, 128], BF16), al([128, 6, 128], BF16)]
            XS = [al([128, 6, 64], BF16), al([128, 6, 64], BF16)]
            TMPY = al([128, 6, 64], F32)
            TY2 = al([128, 384], F32)
            TY3 = al([128, 384], F32)
            SST = [al([128, 384], F32), al([128, 384], F32)]
            SBF = [al([128, 384], BF16), al([128, 384], BF16)]
            NCUM = SM[:, 96:102]
            ECUM = SM[:, 104:110]
            SD = SM[:, 112:118]
            CD = SM[:, 120:126]
            DFULL = al([128, 6, 64], F32)
            P.copy('dve', DFULL, bcm(R_D, 64))
            sctr = [0]
            for d in range(2):
                P.memset('dve', SST[d], 0.0)
                P.memset('pool', SBF[d], 0.0)
                order = list(range(NT)) if d == 0 else [1, 0] + list(range(NT - 1, 1, -1))
                import os
                order = order[:int(os.environ.get('SSD_N', '99'))]
                TRI = cmb(B_TRIF if d == 0 else B_TRIB)
                NMK = cmb(B_NMF if d == 0 else B_NMB)
                lastc = 127 if d == 0 else 0
                for t in order:
                    par = sctr[0] % 2
                    sctr[0] += 1
                    tok = slice(t * 128, (t + 1) * 128)
                    need_y = not (last and t < 2)
                    a6h = SDAH[:, t, d * 6:(d + 1) * 6]
                    a6l = SDAL[:, t, d * 6:(d + 1) * 6]
                    dt6 = SDT[:, t, d * 6:(d + 1) * 6]
                    rah, ral, ee, wt, xs = RALH[par], RALL[par], EE[par], WTT[par], XS[par]
                    P.tt('pool', rah, bch(TRI, 6), bcm(a6h, 128), ALU.mult)
                    P.tt('pool', ral, bch(TRI, 6), bcm(a6l, 128), ALU.mult)
                    pcol = nb()
                    P.mm(pcol[:, 0:6], TRI, a6h, start=True, stop=False)
                    P.mm(pcol[:, 0:6], TRI, a6l, start=False, stop=True)
                    P.ts('dve', NCUM, pcol[:, 0:6], -1.0, None, ALU.mult)
                    P.act(ECUM, pcol[:, 0:6], AF.Exp)
                    if stop == 's1':
                        break
                    pA = nb()
                    pB = nb()
                    dsts = []
                    for h in range(6):
                        dst = (pA if h < 4 else pB)[:, (h % 4) * 128:(h % 4 + 1) * 128]
                        dsts.append(dst)
                        P.mm(dst, ONESB, rah[:, h, :], start=True, stop=False)
                        P.mm(dst, ONESB, ral[:, h, :], start=False, stop=False)
                        P.mm(dst, IDB, NMK, start=False, stop=True)
                    for h in range(6):
                        P.act(ee[:, h, :], dsts[h], AF.Exp, bias=NCUM[:, h:h + 1])
                    if stop == 's2':
                        break
                    P.mm(pcol[:, 8:14], ONESB, a6h, start=True, stop=False)
                    P.mm(pcol[:, 8:14], ONESB, a6l, start=False, stop=True)
                    P.tt('dve', SD, pcol[:, 8:14], NCUM, ALU.add)
                    P.act(SD, SD, AF.Exp)
                    P.tt('dve', SD, SD, dt6, ALU.mult)
                    P.act(CD, pcol[:, 8:14], AF.Exp)
                    if stop == 's3b':
                        break
                    P.tt('pool', xs, CTS[:, t, 0:384].rearrange("p (h f) -> p h f", h=6), bcm(SD, 64), ALU.mult)
                    if stop == 's3':
                        break
                    if need_y:
                        psc = [nb(), nb()]
                        for g in range(2):
                            P.mm(psc[g][:, 0:128], CFB[g * 64:(g + 1) * 64, tok], CFC[g * 64:(g + 1) * 64, tok])
                        for h in range(6):
                            g = h // 3
                            P.stt(wt[:, h, :], psc[g][:, 0:128], dt6[:, h:h + 1], ee[:, h, :], ALU.mult, ALU.mult)
                        if stop == 'y1':
                            break
                        py = nb()
                        poff = [nb(), nb()]
                        for h in range(6):
                            P.mm(py[:, h * 64:(h + 1) * 64], wt[:, h, :], CTS[:, t, h * 64:(h + 1) * 64])
                        for g in range(2):
                            P.mm(poff[g][:, 0:192], CFC[g * 64:(g + 1) * 64, tok],
                                 SBF[d][g * 64:(g + 1) * 64, g * 192:(g + 1) * 192])
                        if stop == 'y2':
                            break
                        for g in range(2):
                            P.tt('dve', TMPY[:, 3 * g:3 * g + 3, :], poff[g][:, 0:192].rearrange("p (h f) -> p h f", h=3),
                                 bcm(ECUM[:, 3 * g:3 * g + 3], 64), ALU.mult)
                        P.tt('dve', TY2, py[:, 0:384], TMPY.rearrange("p h f -> p (h f)"), ALU.add)
                        if d == 0:
                            P.tt('pool', TY3, CTS[:, t, 0:384], DFULL.rearrange("p h f -> p (h f)"), ALU.mult)
                            P.tt('pool', MIX[:, t, 0:384], TY2, TY3, ALU.add)
                        else:
                            P.tt('pool', MIX[:, t, 0:384], TY2, MIX[:, t, 0:384], ALU.add)
                    if stop == 's4':
                        break
                    pst = nb()
                    P.mm(pst[:, 0:384], CTS[:, t, 384:512], xs.rearrange("p h f -> p (h f)"))
                    P.tt('pool', SST[d].rearrange("p (h f) -> p h f", h=6), SST[d].rearrange("p (h f) -> p h f", h=6),
                         bcm(CD, 64), ALU.mult)
                    P.tt('dve', SST[d], SST[d], pst[:, 0:384], ALU.add)
                    P.copy('pool', SBF[d], SST[d])
                    if stop == 's5':
                        break
            dump("L%d_mixs" % l, MIX[:, :, 0:384])
            dump("L%d_mix" % l, MIX[:])
            arena_off[0] = mark_ssd
            if stop in ('ssd', 's1', 's2', 's3', 's4', 's5', 's3a', 's3b', 'y1', 'y2'):
                break

            def nb2():
                if bank_ctr[0] % 2 == 1:
                    bank_ctr[0] += 1
                b = bank_ctr[0] % NROT[0]
                bank_ctr[0] += 2
                return PS[:, b:b + 2, :]

            hpi = lambda h: (h % 2) * 2 + h // 2
            GBP = al([128, NT, 8], F32)
            GHP = al([128, NT, 8], BF16)
            GLP = al([128, NT, 8], BF16)
            for dd in range(2):
                for h in range(4):
                    P.copy('dve', GBP[:, :, dd * 4 + hpi(h)], GB[:, :, dd * 4 + h])
                    P.copy('dve', GHP[:, :, dd * 4 + hpi(h)], GGH[:, :, dd * 4 + h])
                    P.copy('dve', GLP[:, :, dd * 4 + hpi(h)], GGL[:, :, dd * 4 + h])
            CFQK = al([128, 4, T], BF16)
            CTG = al([128, NT, 512], BF16)
            mark_gcore = arena_off[0]
            XFG = [al([128, T], BF16), al([128, T], BF16)]
            SQ = al([128, 512], BF16)
            RN = al([128, 512], F32)
            for ci in range(4):
                xf = XFG[ci % 2]
                conv_proj(5 + ci, xf)
                for (t0, n) in TBK:
                    P.tt('dve', SQ[:, 0:n], xf[:, t0:t0 + n], xf[:, t0:t0 + n], ALU.mult)
                    pb = nb()
                    P.mm(pb[:, 0:n], cmb(B_BLK), SQ[:, 0:n])
                    P.act(RN[:, 0:n], pb[:, 0:n], AF.Sqrt, bias=EPSC)
                    P.recip(RN[:, 0:n], RN[:, 0:n])
                    if ci < 2:
                        P.stt(CFQK[:, ci, t0:t0 + n], xf[:, t0:t0 + n], 0.125, RN[:, 0:n], ALU.mult, ALU.mult)
                    else:
                        P.tt('dve', CFQK[:, ci, t0:t0 + n], xf[:, t0:t0 + n], RN[:, 0:n], ALU.mult)
                if ci >= 2:
                    to_tok(CFQK[:, ci, :], CTG, (ci - 2) * 128)
            for ci in range(2):
                xf = XFG[ci % 2]
                conv_proj(9 + ci, xf)
                to_tok(xf, CTG, 256 + ci * 128)
            dump("L%d_cfqk" % l, CFQK)
            dump("L%d_ctg" % l, CTG)
            if stop == 'gconv':
                break
            arena_off[0] = mark_gcore
            RGH = al([128, 4, 128], BF16)
            RGL = al([128, 4, 128], BF16)
            GMH = al([128, 4, 2], BF16)
            GML = al([128, 4, 2], BF16)
            EG = al([128, 4, 128], F32)
            ES = al([128, 4, 128], F32)
            T1 = al([128, 4, 128], F32)
            XX = [al([128, 4, 128], BF16), al([128, 4, 128], BF16)]
            XXT = [al([128, 4, 128], BF16), al([128, 4, 128], BF16)]
            WW = [al([128, 4, 128], BF16), al([128, 4, 128], BF16)]
            WWT = [al([128, 4, 128], BF16), al([128, 4, 128], BF16)]
            CTS_ = al([128, 4, 128], BF16)
            CS_ = al([128, 4, 128], BF16)
            YY = al([128, 4, 128], BF16)
            YYT = al([128, 4, 128], BF16)
            ITT = al([128, 4, 128], BF16)
            UU = al([128, 256], F32)
            KG = al([128, 4, 64], BF16)
            KD = al([128, 4, 64], BF16)
            WTG = al([128, 2, 128], BF16)
            VN = al([128, 256], BF16)
            OA = al([128, 256], F32)
            OA2 = al([128, 256], F32)
            SG = [al([128, 128], F32), al([128, 128], F32)]
            SGB = [al([128, 128], BF16), al([128, 128], BF16)]
            NGC = SM[:, 128:132]
            EGC = SM[:, 136:140]
            F1 = SM[:, 144:148]
            CDF = SM[:, 152:156].rearrange("p (c hh) -> p c hh", c=2)
            for d in range(2):
                P.memset('dve', SG[d], 0.0)
                P.memset('dve', SGB[d], 0.0)
                order = list(range(NT)) if d == 0 else [1, 0] + list(range(NT - 1, 1, -1))
                import os
                order = order[:int(os.environ.get('GDN_N', '99'))]
                TB = cmb(B_TBF if d == 0 else B_TBB)
                NMK = cmb(B_NBF if d == 0 else B_NBB)
                SMK = cmb(B_SBF if d == 0 else B_SBB)
                for t in order:
                    tok = slice(t * 128, (t + 1) * 128)
                    need_o = not (last and t < 2)
                    gh = GHP[:, t, d * 4:(d + 1) * 4]
                    gl = GLP[:, t, d * 4:(d + 1) * 4]
                    bp = GBP[:, t, d * 4:(d + 1) * 4]
                    P.tt('dve', RGH, bch(TB, 4), bcm(gh, 128), ALU.mult)
                    P.tt('dve', RGL, bch(TB, 4), bcm(gl, 128), ALU.mult)
                    P.tt('dve', GMH, bcm(gh, 2), bch(CMK[:], 4), ALU.mult)
                    P.tt('dve', GML, bcm(gl, 2), bch(CMK[:], 4), ALU.mult)
                    pcol = nb()
                    P.mm(pcol[:, 0:4], TB, gh, start=True, stop=False)
                    P.mm(pcol[:, 0:4], TB, gl, start=False, stop=True)
                    P.mm(pcol[:, 8:16], ONESB, GMH.rearrange("p h c -> p (h c)"), start=True, stop=False)
                    P.mm(pcol[:, 8:16], ONESB, GML.rearrange("p h c -> p (h c)"), start=False, stop=True)
                    P.ts('dve', NGC, pcol[:, 0:4], -1.0, None, ALU.mult)
                    P.act(EGC, pcol[:, 0:4], AF.Exp)
                    pcv = pcol[:, 8:16].rearrange("p (hl hh c) -> p hl hh c", hl=2, hh=2)
                    for hl in range(2):
                        for c in range(2):
                            P.act(CDF[hl * 64:(hl + 1) * 64, c, :], pcv[hl * 64:(hl + 1) * 64, hl, :, c], AF.Exp)
                    pd = nb()
                    for hp in range(4):
                        dst = pd[:, hp * 128:(hp + 1) * 128]
                        P.mm(dst, ONESB, RGH[:, hp, :], start=True, stop=False)
                        P.mm(dst, ONESB, RGL[:, hp, :], start=False, stop=False)
                        P.mm(dst, IDB, NMK, start=False, stop=True)
                    for hp in range(4):
                        P.act(EG[:, hp, :], pd[:, hp * 128:(hp + 1) * 128], AF.Exp, bias=NGC[:, hp:hp + 1])
                    pkk = nb2()
                    pqk = nb2()
                    for h in range(4):
                        hl, hh = h % 2, h // 2
                        kf = CFQK[hl * 64:(hl + 1) * 64, 2 + hh, tok]
                        qf = CFQK[hl * 64:(hl + 1) * 64, hh, tok]
                        P.mm(pkk[:, hl, hh * 128:(hh + 1) * 128], kf, kf)
                        P.mm(pqk[:, hl, hh * 128:(hh + 1) * 128], kf, qf)
                    P.tt('dve', ES, EG, bch(SMK, 4), ALU.mult)
                    for hl in range(2):
                        P.tt('dve', T1[:, hl * 2:(hl + 1) * 2, :], pkk[:, hl, 0:256].rearrange("p (b i) -> p b i", b=2),
                             bcm(bp[:, hl * 2:(hl + 1) * 2], 128), ALU.mult)
                    P.tt('dve', XX[0], T1, ES, ALU.mult)
                    for hl in range(2):
                        P.tt('dve', T1[:, hl * 2:(hl + 1) * 2, :], pqk[:, hl, 0:256].rearrange("p (b i) -> p b i", b=2),
                             bcm(bp[:, hl * 2:(hl + 1) * 2], 128), ALU.mult)
                    P.tt('dve', ITT, T1, EG, ALU.mult)
                    pt = nb()
                    for hp in range(4):
                        P.mm(pt[:, hp * 128:(hp + 1) * 128], XX[0][:, hp, :], IDB)
                    P.copy('act', XXT[0].rearrange("p h i -> p (h i)"), pt[:, :])
                    MPm, LPm = XX[0], XXT[0]
                    Wc = [WW[0], WW[1]]
                    Wtc = [WWT[0], WWT[1]]
                    m0 = cmb(B_LV)
                    P.tt('dve', T1, LPm, bch(m0, 4), ALU.mult)
                    P.stt(Wc[0], T1, -1.0, bch(IDB, 4), ALU.mult, ALU.add)
                    P.tt('dve', T1, MPm, bch(m0, 4), ALU.mult)
                    P.stt(Wtc[0], T1, -1.0, bch(IDB, 4), ALU.mult, ALU.add)
                    cur = 0
                    for lev in range(1, 6):
                        ml = cmb(B_LV + lev)
                        P.tt('dve', CTS_, MPm, bch(ml, 4), ALU.mult)
                        P.tt('dve', CS_, LPm, bch(ml, 4), ALU.mult)
                        p1 = nb()
                        for hp in range(4):
                            P.mm(p1[:, hp * 128:(hp + 1) * 128], CTS_[:, hp, :], Wc[cur][:, hp, :])
                        P.copy('act', YY.rearrange("p h i -> p (h i)"), p1[:, :])
                        p2 = nb()
                        for hp in range(4):
                            P.mm(p2[:, hp * 128:(hp + 1) * 128], CS_[:, hp, :], Wtc[cur][:, hp, :])
                        P.copy('act', YYT.rearrange("p h i -> p (h i)"), p2[:, :])
                        p3 = nb()
                        for hp in range(4):
                            P.mm(p3[:, hp * 128:(hp + 1) * 128], Wtc[cur][:, hp, :], YY[:, hp, :])
                        P.tt('dve', Wc[1 - cur], Wc[cur], p3.rearrange("p (h i) -> p h i", h=4), ALU.subtract)
                        p4 = nb()
                        for hp in range(4):
                            P.mm(p4[:, hp * 128:(hp + 1) * 128], Wc[cur][:, hp, :], YYT[:, hp, :])
                        P.tt('dve', Wtc[1 - cur], Wtc[cur], p4.rearrange("p (h i) -> p h i", h=4), ALU.subtract)
                        cur = 1 - cur
                    PM = Wtc[cur]
                    pu = nb()
                    for h in range(4):
                        hp = hpi(h)
                        P.mm(pu[:, hp * 64:(hp + 1) * 64], PM[:, hp, :], CTG[:, t, 256 + h * 64:256 + (h + 1) * 64])
                    P.copy('dve', UU, pu[:, 0:256])
                    kv = CTG[:, t, 0:256].rearrange("p (hh hl f) -> p hh hl f", hh=2, hl=2)
                    for hl in range(2):
                        P.tt('dve', KG[:, hl * 2:(hl + 1) * 2, :], kv[:, :, hl, :], bcm(EGC[:, hl * 2:(hl + 1) * 2], 64), ALU.mult)
                    pw = nb()
                    for hp in range(4):
                        hl, hh = hp // 2, hp % 2
                        P.mm(pw[hl * 64:(hl + 1) * 64, hh * 128:(hh + 1) * 128], KG[:, hp, :], PM[:, hp, :])
                    P.copy('act', WTG.rearrange("p h i -> p (h i)"), pw[:, 0:256])
                    for c in range(2):
                        rows = slice(c * 64, (c + 1) * 64)
                        lastcol = c * 64 + (63 if d == 0 else 0)
                        P.tt('dve', F1[rows, :], EG[rows, :, lastcol], bp[rows, :], ALU.mult)
                    for hl in range(2):
                        P.tt('dve', KD[:, hl * 2:(hl + 1) * 2, :], kv[:, :, hl, :], bcm(F1[:, hl * 2:(hl + 1) * 2], 64), ALU.mult)
                    for c in ([0, 1] if d == 0 else [1, 0]):
                        rows = slice(c * 64, (c + 1) * 64)
                        ctok = slice(t * 128 + c * 64, t * 128 + (c + 1) * 64)
                        pr = nb2()
                        for hp in range(4):
                            hl, hh = hp // 2, hp % 2
                            sblk = SGB[d][hl * 64:(hl + 1) * 64, hh * 64:(hh + 1) * 64]
                            P.mm(pr[rows, hl, hh * 64:(hh + 1) * 64], WTG[hl * 64:(hl + 1) * 64, hh, c * 64:(c + 1) * 64], sblk)
                            if need_o:
                                P.mm(pr[rows, hl, 128 + hh * 64:128 + (hh + 1) * 64], CFQK[hl * 64:(hl + 1) * 64, hh, ctok], sblk)
                        P.tt('dve', VN[rows, :].rearrange("p (a b) -> p a b", a=2), UU[rows, :].rearrange("p (a b) -> p a b", a=2),
                             pr[rows, :, 0:128], ALU.subtract)
                        if need_o:
                            for hl in range(2):
                                P.tt('dve', OA[rows, hl * 128:(hl + 1) * 128].rearrange("p (a b) -> p a b", a=2),
                                     pr[rows, hl, 128:256].rearrange("p (a b) -> p a b", a=2),
                                     bcm(EGC[rows, hl * 2:(hl + 1) * 2], 64), ALU.mult)
                            po = nb()
                            for hp in range(4):
                                P.mm(po[rows, hp * 64:(hp + 1) * 64], ITT[rows, hp, c * 64:(c + 1) * 64], VN[rows, hp * 64:(hp + 1) * 64])
                            P.tt('dve', OA2[rows, :], OA[rows, :], po[rows, 0:256], ALU.add)
                            mv = MIX[rows, t, 384:640].rearrange("p (hh hl f) -> p hh hl f", hh=2, hl=2)
                            for hl in range(2):
                                src = OA2[rows, hl * 128:(hl + 1) * 128].rearrange("p (a b) -> p a b", a=2)
                                if d == 0:
                                    P.copy('dve', mv[:, :, hl, :], src)
                                else:
                                    P.tt('dve', mv[:, :, hl, :], mv[:, :, hl, :], src, ALU.add)
                        psu = nb()
                        for hp in range(4):
                            hl, hh = hp // 2, hp % 2
                            P.mm(psu[hl * 64:(hl + 1) * 64, hh * 64:(hh + 1) * 64], KD[rows, hp, :], VN[rows, hp * 64:(hp + 1) * 64])
                        P.tt('dve', SG[d].rearrange("p (a b) -> p a b", a=2), SG[d].rearrange("p (a b) -> p a b", a=2),
                             bcm(CDF[:, c, :], 64), ALU.mult)
                        P.tt('dve', SG[d], SG[d], psu[:, 0:128], ALU.add)
                        P.copy('act', SGB[d], SG[d])
            dump("L%d_mixg" % l, MIX[:, :, 0:640])
            arena_off[0] = mark_ssd
            if stop == 'gdn':
                break

            arena_off[0] = mark_mix
            ROPE = al([128, 16, 2, 64], F32)
            P.dma(ROPE, rope_d)
            QKT = al([128, 4, T], BF16)
            VA = al([128, NT, 2, 65], BF16)
            NWB = al([128, 512], F32)
            SQF = al([128, 512], F32)
            QN = al([128, 512], F32)
            T1R = al([128, 512], F32)
            T2R = al([128, 512], F32)
            SINR = al([128, 512], F32)
            DST = al([128, 512], BF16)
            SSQ = SM[:, 160:168]
            P.dma(WS[0][:, :, 0:512], win_v[:, :, C_AQ:C_AQ + 512], q='pool')
            P.dma(WS[1][:, :, 0:128], win_v[:, :, C_AV:C_AV + 128], q='pool')
            P.ts('dve', NWB[:, 0:384].rearrange("p (h f) -> p h f", h=6), bch(R_QNW, 6), 0.125, None, ALU.mult)
            P.copy('dve', NWB[:, 384:512].rearrange("p (h f) -> p h f", h=2), bch(R_KNW, 2))
            P.memset('dve', VA[:, :, :, 64:65], 1.0)
            for t in range(NT):
                tok = slice(t * 128, (t + 1) * 128)
                pa = nb()
                pv = nb()
                for k in range(8):
                    P.mm(pa[:, 0:512], HT[:, k, tok], WS[0][:, k, 0:512], start=(k == 0), stop=(k == 7))
                for k in range(8):
                    P.mm(pv[:, 0:128], HT[:, k, tok], WS[1][:, k, 0:128], start=(k == 0), stop=(k == 7))
                P.copy('act', VA[:, t, :, 0:64], pv[:, 0:128].rearrange("p (g f) -> p g f", g=2))
                P.act(SQF, pa[:, 0:512], AF.Square)
                P.reduce('dve', SSQ, SQF.rearrange("p (h f) -> p h f", h=8))
                P.ts('dve', SSQ, SSQ, 1.0 / 64, EPS, ALU.mult, ALU.add)
                P.act(SSQ, SSQ, AF.Sqrt)
                P.recip(SSQ, SSQ)
                P.tt('dve', QN.rearrange("p (h f) -> p h f", h=8), pa[:, 0:512].rearrange("p (h f) -> p h f", h=8),
                     bcm(SSQ, 64), ALU.mult)
                P.tt('dve', QN, QN, NWB, ALU.mult)
                if t >= 2:
                    P.tt('dve', T1R.rearrange("p (h f) -> p h f", h=8), QN.rearrange("p (h f) -> p h f", h=8),
                         bch(ROPE[:, t - 2, 0, :], 8), ALU.mult)
                    P.copy('act', SINR.rearrange("p (h f) -> p h f", h=8), bch(ROPE[:, t - 2, 1, :], 8))
                    qv = QN.rearrange("p (ha s f) -> p ha s f", s=2, f=16)
                    sv = SINR.rearrange("p (ha s f) -> p ha s f", s=2, f=16)
                    tv = T2R.rearrange("p (ha s f) -> p ha s f", s=2, f=16)
                    P.tt('dve', tv[:, :, 0, :], qv[:, :, 1, :], sv[:, :, 0, :], ALU.mult)
                    P.tt('dve', tv[:, :, 1, :], qv[:, :, 0, :], sv[:, :, 1, :], ALU.mult)
                    P.tt('dve', T1R, T1R, T2R, ALU.add)
                    srcq = T1R
                else:
                    srcq = QN
                P.copy('act', DST[:, 0:128], srcq[:, 384:512])
                dq = DST[:, 128:512].rearrange("p (a g f) -> p a g f", a=3, g=2)
                for g in range(2):
                    P.copy('dve', dq[:, :, g, :], srcq[:, g * 192:(g + 1) * 192].rearrange("p (a f) -> p a f", a=3))
                ptq = nb()
                for j in range(4):
                    P.mm(ptq[:, j * 128:(j + 1) * 128], DST[:, j * 128:(j + 1) * 128], IDB)
                P.copy('act', QKT[:, :, tok], ptq.rearrange("p (j i) -> p j i", j=4))
            dump("L%d_qkt" % l, QKT)
            dump("L%d_va" % l, VA)
            if stop == 'aprep':
                break
            PT = [al([128, 512], BF16), al([128, 512], BF16), al([128, 512], BF16)]
            AO = al([128, 4, 384], F32)
            REC = SM[:, 176:180]
            qblocks = [(256 + 512 * i, 512, list(range(NT))) for i in range(4)]
            if not last:
                qblocks = [(0, 256, [0, 1])] + qblocks
            import os
            qblocks = qblocks[:int(os.environ.get('ATT_N', '99'))]
            pctr = 0
            NROT[0] = 6
            actr = 0
            for (q0, nq, ktiles) in qblocks:
                nj = nq // 128
                for g in range(2):
                    for a in range(3):
                        h = 3 * g + a
                        pacc = PS[:, 6 + actr % 2, :]
                        actr += 1
                        for kt in ktiles:
                            ps = nb()
                            P.mm(ps[:, 0:nq], QKT[g * 64:(g + 1) * 64, 0, kt * 128:(kt + 1) * 128],
                                 QKT[g * 64:(g + 1) * 64, 1 + a, q0:q0 + nq])
                            pt = PT[pctr % 3]
                            pctr += 1
                            P.act(pt[:, 0:nq], ps[:, 0:nq], AF.Exp)
                            for j in range(nj):
                                P.mm(pacc[:, j * 65:(j + 1) * 65], pt[:, j * 128:(j + 1) * 128], VA[:, kt, g, :],
                                     start=(kt == ktiles[0] and j == 0), stop=(kt == ktiles[-1] and j == nj - 1))
                        pv3 = pacc[:, 0:nj * 65].rearrange("p (j c) -> p j c", c=65)
                        P.recip(REC[:, 0:nj], pv3[:, :, 64])
                        P.tt('dve', AO[:, 0:nj, h * 64:(h + 1) * 64], pv3[:, :, 0:64], bcm(REC[:, 0:nj], 64), ALU.mult)
                tq = q0 // 128
                P.copy('act', MIX[:, tq:tq + nj, 640:1024], AO[:, 0:nj, :])
            NROT[0] = 8
            dump("L%d_mixa" % l, MIX[:])
            if stop == 'attn':
                break

            arena_off[0] = mark_gate
            WZ = al([128, 8, 1024], BF16)
            WO = al([128, 8, 1024], BF16)
            P.dma(WZ[:, :, 0:384], win_v[:, :, C_SSDZ:C_SSDZ + 384], q='pool')
            P.dma(WZ[:, :, 384:640], win_v[:, :, C_GDNZ:C_GDNZ + 256], q='pool')
            P.dma(WZ[:, :, 640:1024], win_v[:, :, C_AZ:C_AZ + 384], q='pool')
            P.dma(WO[:, :, :], wout_d[l].rearrange("(k p) n -> p k n", p=128), q='pool')
            XTL = [al([128, D], F32), al([128, D], F32)]
            ZS = al([128, 1024], F32)
            G1 = al([128, 1024], F32)
            MB = al([128, 1024], BF16)
            MT = al([128, 8, 128], BF16)
            UPD = al([128, 1024], F32)
            JK = al([128, 384], F32)
            FS = SM[:, 192:200]
            ftiles = list(range(NT)) if not last else list(range(2, NT))
            for t in ftiles:
                tok = slice(t * 128, (t + 1) * 128)
                v = 1 if t < 2 else 0
                if t >= 2:
                    xt = XTL[t % 2]
                    P.dma(xt, xsrc[(t - 2) * 128:(t - 1) * 128, :])
                else:
                    xt = XC[:, t, :]
                pz = [nb(), nb()]
                for nbk in range(2):
                    for k in range(8):
                        P.mm(pz[nbk][:, :], HT[:, k, tok], WZ[:, k, nbk * 512:(nbk + 1) * 512], start=(k == 0), stop=(k == 7))
                P.act(ZS[:, 0:512], pz[0][:, :], AF.Silu)
                P.act(ZS[:, 512:1024], pz[1][:, :], AF.Silu)
                P.tt('dve', G1[:, 0:384], MIX[:, t, 0:384], ZS[:, 0:384], ALU.mult)
                P.act(JK, G1[:, 0:384], AF.Square, accum_out=FS[:, 0:1])
                P.ts('dve', FS[:, 1:2], FS[:, 0:1], 1.0 / 384, EPS, ALU.mult, ALU.add)
                P.act(FS[:, 1:2], FS[:, 1:2], AF.Sqrt)
                P.recip(FS[:, 1:2], FS[:, 1:2])
                P.stt(MB[:, 0:384], G1[:, 0:384], FS[:, 1:2], R_SNW, ALU.mult, ALU.mult)
                P.act(JK[:, 0:256], MIX[:, t, 384:640], AF.Square)
                P.reduce('dve', FS[:, 4:8], JK[:, 0:256].rearrange("p (h f) -> p h f", h=4))
                P.ts('dve', FS[:, 4:8], FS[:, 4:8], 1.0 / 64, EPS, ALU.mult, ALU.add)
                P.act(FS[:, 4:8], FS[:, 4:8], AF.Sqrt)
                P.recip(FS[:, 4:8], FS[:, 4:8])
                g3 = G1[:, 384:640].rearrange("p (h f) -> p h f", h=4)
                P.tt('dve', g3, MIX[:, t, 384:640].rearrange("p (h f) -> p h f", h=4), bcm(FS[:, 4:8], 64), ALU.mult)
                P.tt('dve', g3, g3, bch(R_GNW, 4), ALU.mult)
                P.tt('dve', MB[:, 384:640], G1[:, 384:640], ZS[:, 384:640], ALU.mult)
                P.tt('dve', MB[:, 640:1024], MIX[:, t, 640:1024], ZS[:, 640:1024], ALU.mult)
                for half in range(2):
                    ptm = nb()
                    for j in range(4):
                        kc = half * 4 + j
                        P.mm(ptm[:, j * 128:(j + 1) * 128], MB[:, kc * 128:(kc + 1) * 128], IDB)
                    P.copy('act' if half else 'dve', MT[:, half * 4:(half + 1) * 4, :], ptm.rearrange("p (j i) -> p j i", j=4))
                for nbk in range(2):
                    po = nb()
                    for kc in range(8):
                        P.mm(po[:, :], MT[:, kc, :], WO[:, kc, nbk * 512:(nbk + 1) * 512], start=(kc == 0), stop=(kc == 7))
                    P.tt('dve', UPD[:, nbk * 512:(nbk + 1) * 512], po[:, :], GATE[:, v, nbk * 512:(nbk + 1) * 512], ALU.mult)
                if t >= 2:
                    P.tt('dve', xt, xt, UPD, ALU.add)
                    P.dma(xdst[(t - 2) * 128:(t - 1) * 128, :], xt)
                else:
                    P.tt('dve', XC[:, t, :], XC[:, t, :], UPD, ALU.add)
            dump("L%d_wz" % l, WZ)
            dump("L%d_wo" % l, WO)
            if stop == 'L0':
                dump("L%d_xc" % l, XC[:])
                break

        dump("final_mix", MIX[:])
        P.finish([out_d] + list(dbg_out.values()))
        P.emit()
    return nc, dbg_out


def prep_inputs(inputs):
    f = lambda a: np.ascontiguousarray(np.asarray(a, dtype=np.float32))
    constm, constb, rope, cm = host_consts()
    c = f(inputs['c'])
    c_ctx = f(inputs['c_ctx'])
    norm_w = f(inputs['norm_w'])
    b_mod = f(inputs['b_mod'])
    conv_w = f(inputs['conv_w'])
    conv_b = f(inputs['conv_b'])
    rows = np.zeros((2, 1024), np.float32)
    rows[:, 0:12] = f(inputs['ssd_dt_bias']).reshape(2, 12)
    rows[:, 12:24] = f(inputs['ssd_A_log']).reshape(2, 12)
    rows[:, 24:30] = f(inputs['ssd_D'])
    rows[:, 32:40] = f(inputs['gdn_A_log']).reshape(2, 8)
    rows[:, 40:48] = f(inputs['gdn_dt_bias']).reshape(2, 8)
    rows[:, 64:128] = f(inputs['gdn_norm_w'])
    rows[:, 128:192] = f(inputs['q_norm_w'])
    rows[:, 192:256] = f(inputs['k_norm_w'])
    rows[:, 256:640] = f(inputs['ssd_norm_w'])
    shared = {
        'w_mod': f(inputs['w_mod']), 'b_mod': b_mod, 'w_in': f(inputs['w_in']), 'w_out': f(inputs['w_out']),
        'normwc': np.ascontiguousarray(norm_w.reshape(2, 8, 128).transpose(2, 0, 1)),
        'bmodc': np.ascontiguousarray(b_mod[:, 0:2048].reshape(2, 16, 128).transpose(2, 0, 1)),
        'convwc': np.ascontiguousarray(conv_w.reshape(2, 3, 11, 128).transpose(3, 0, 2, 1)),
        'convbc': np.ascontiguousarray(conv_b.reshape(2, 11, 128).transpose(2, 0, 1)),
        'rows': rows, 'constm': constm, 'constb': constb, 'rope': rope, 'cm': cm,
    }
    x = f(inputs['x'])
    ctx = f(inputs['ctx'])
    maps = []
    for b in range(x.shape[0]):
        m = dict(shared)
        m['x'] = x[b]
        m['ctx'] = ctx[b]
        cc = np.stack([c[b].reshape(8, 128).T, c_ctx.reshape(8, 128).T], axis=-1)
        m['cc'] = np.ascontiguousarray(cc)
        maps.append(m)
    return maps


def kernel(**inputs):
    maps = prep_inputs(inputs)
    nc, _ = build()
    res = run_bass_kernel_spmd(nc, maps, core_ids=list(range(len(maps))))
    return np.stack([np.asarray(r["out"], dtype=np.float32) for r in res.results], axis=0)
```

```python
import math
import numpy as np
import ml_dtypes
import concourse.bass as bass
import concourse.mybir as mybir
from concourse.bass_utils import run_bass_kernel_spmd

F32 = mybir.dt.float32
BF16 = mybir.dt.bfloat16
AF = mybir.ActivationFunctionType
ALU = mybir.AluOpType
AX = mybir.AxisListType
ENGS = ['pe', 'act', 'dve', 'pool', 'sp']
ISZ = {F32: 4, BF16: 2}

T = 2304
NT = 18
D = 1024
EPS = 1e-6
BIG = 30000.0


class Op:
    __slots__ = ('eng', 'fn', 'is_dma', 'eidx', 'gidx', 'waits', 'signal', 'sem', 'val', 'prevdma')


class Prog:
    NDMA = 24

    def __init__(self, nc):
        self.nc = nc
        self.ops = []
        self.eops = {e: [] for e in ENGS}
        self.track = {}
        self.waited = {e: {} for e in ENGS}
        self.dma_waited = {e: set() for e in ENGS}
        self.ndma = 0
        self.dma_last = {}
        self.addr = {}
        self.pool_ok = False
        self.fence_scr = None
        self.nsw = 0
        self.strict = False

    def region(self, ap):
        t = ap.tensor
        name = ap.name
        pairs = ap.ap
        off = int(ap.offset)
        sp = str(ap.space)
        if sp in ('SB', 'PSUM'):
            shp = tuple(t.shape)
            fsz = 1
            for s in shp[1:]:
                fsz *= s
            p0 = off // fsz
            f0 = off % fsz
            pst, pc = pairs[0]
            p1 = p0 + (pc if pst > 0 else 1)
            ext = 0
            for st, c in pairs[1:]:
                ext += abs(st) * (c - 1)
            isz = ISZ.get(ap.dtype, 4)
            if sp == 'SB':
                base = self.addr[name]
                return ('SB', p0, p1, base + f0 * isz, base + (f0 + ext + 1) * isz)
            b0 = (f0 * isz) // 2048
            b1 = ((f0 + ext + 1) * isz + 2047) // 2048
            return (name, (p0 // 32) * 32, ((p1 + 31) // 32) * 32, b0 * 2048, b1 * 2048)
        ext = 0
        for st, c in pairs:
            ext += abs(st) * (c - 1)
        return (name, 0, 1, off, off + ext + 1)

    def add(self, eng, fn, reads, writes, is_dma=False):
        if eng == 'pool' and not is_dma and not self.pool_ok:
            eng = 'dve'
        op = Op()
        op.eng = eng
        op.fn = fn
        op.is_dma = is_dma
        op.signal = False
        op.gidx = len(self.ops)
        op.eidx = len(self.eops[eng])
        op.sem = None
        op.val = 0
        op.prevdma = None
        deps = []
        rregs = [self.region(a) for a in reads]
        wregs = [self.region(a) for a in writes]

        def ov(a, b):
            return a[1] < b[2] and b[1] < a[2] and a[3] < b[4] and b[3] < a[4]

        def cov(a, b):
            return a[1] <= b[1] and a[2] >= b[2] and a[3] <= b[3] and a[4] >= b[4]

        for r in rregs:
            tr = self.track.get(r[0])
            if tr is None:
                continue
            for (wr, wop) in tr[0]:
                if ov(r, wr):
                    deps.append((wop, 'RAW'))
        for w in wregs:
            tr = self.track.get(w[0])
            if tr is None:
                continue
            for (wr, wop) in tr[0]:
                if ov(w, wr):
                    deps.append((wop, 'WAW'))
            for (rr, rop) in tr[1]:
                if ov(w, rr):
                    deps.append((rop, 'WAR'))
        for w in wregs:
            tr = self.track.setdefault(w[0], [[], []])
            tr[0] = [(wr, wop) for (wr, wop) in tr[0] if not cov(w, wr)]
            tr[1] = [(rr, rop) for (rr, rop) in tr[1] if not cov(w, rr)]
            tr[0].append((w, op))
        for r in rregs:
            tr = self.track.setdefault(r[0], [[], []])
            if not is_dma:
                tr[1] = [(rr, rop) for (rr, rop) in tr[1]
                         if not (rop.eng == eng and not rop.is_dma and cov(r, rr))]
            tr[1].append((r, op))
        need = {}
        for (p, kind) in deps:
            if p is op:
                continue
            if p.is_dma:
                if p.gidx in self.dma_waited[eng]:
                    continue
                need[('dma', p.gidx)] = p
            else:
                if p.eng == eng:
                    if eng == 'pe':
                        continue
                    if kind != 'RAW' and not is_dma and not self.strict:
                        continue
                if eng != p.eng and kind == 'RAW' and p.eng in ('dve', 'act') and self.fence_scr is not None:
                    lst = self.eops[p.eng]
                    if p.eidx + 1 >= len(lst):
                        self._fence(p.eng)
                    p = lst[p.eidx + 1]
                if p.eidx <= self.waited[eng].get(p.eng, -1):
                    continue
                cur = need.get(p.eng)
                if cur is None or p.eidx > cur.eidx:
                    need[p.eng] = p
        for k, p in need.items():
            p.signal = True
            if p.is_dma:
                self.dma_waited[eng].add(p.gidx)
            else:
                self.waited[eng][p.eng] = p.eidx
        op.waits = list(need.values())
        if is_dma and eng == 'pool':
            op.sem = ('sw', self.nsw)
            self.nsw += 1
            op.signal = True
        elif is_dma:
            slot = self.ndma % self.NDMA
            self.ndma += 1
            prev = self.dma_last.get(slot)
            if prev is not None and prev.gidx not in self.dma_waited[eng]:
                op.prevdma = prev
                self.dma_waited[eng].add(prev.gidx)
            self.dma_last[slot] = op
            op.sem = slot
            op.signal = True
        op.gidx = len(self.ops)
        op.eidx = len(self.eops[eng])
        self.ops.append(op)
        self.eops[eng].append(op)
        return op

    def _fence(self, eng):
        scr = self.fence_scr
        if eng == 'dve':
            return self.add('dve', lambda e: e.memset(scr[:, 0:1], 0.0), [], [scr[:, 0:1]])
        return self.add('act', lambda e: e.memzero(scr[:, 2:3]), [], [scr[:, 2:3]])

    def mm(self, out, lhsT, rhs, start=True, stop=True):
        rd = [lhsT, rhs] + ([] if start else [out])
        return self.add('pe', lambda e: e.matmul(out, lhsT, rhs, start=start, stop=stop), rd, [out])

    def act(self, out, in_, func, bias=None, scale=1.0, accum_out=None):
        rd = [in_]
        kw = {}
        if bias is not None:
            kw['bias'] = bias
            if not isinstance(bias, (int, float)):
                rd.append(bias)
        if not isinstance(scale, (int, float)):
            rd.append(scale)
        wr = [out]
        if accum_out is not None:
            kw['accum_out'] = accum_out
            wr.append(accum_out)
        return self.add('act', lambda e: e.activation(out, in_, func, scale=scale, **kw), rd, wr)

    def tt(self, eng, out, in0, in1, op):
        return self.add(eng, lambda e: e.tensor_tensor(out, in0, in1, op), [in0, in1], [out])

    def ts(self, eng, out, in0, s1, s2, op0, op1=None):
        rd = [in0]
        if not isinstance(s1, (int, float)):
            rd.append(s1)
        if s2 is not None and not isinstance(s2, (int, float)):
            rd.append(s2)
        if op1 is None:
            return self.add(eng, lambda e: e.tensor_scalar(out, in0, s1, None, op0), rd, [out])
        return self.add(eng, lambda e: e.tensor_scalar(out, in0, s1, s2, op0, op1), rd, [out])

    def stt(self, out, in0, scalar, in1, op0, op1):
        rd = [in0, in1]
        if not isinstance(scalar, (int, float)):
            rd.append(scalar)
        return self.add('dve', lambda e: e.scalar_tensor_tensor(out, in0, scalar, in1, op0, op1), rd, [out])

    def copy(self, eng, out, in_):
        if eng == 'act':
            return self.add(eng, lambda e: e.copy(out, in_), [in_], [out])
        return self.add(eng, lambda e: e.tensor_copy(out, in_), [in_], [out])

    def memset(self, eng, out, val):
        return self.add(eng, lambda e: e.memset(out, val), [], [out])

    def reduce(self, eng, out, in_, op=None):
        op = ALU.add if op is None else op
        return self.add(eng, lambda e: e.tensor_reduce(out, in_, AX.X, op), [in_], [out])

    def recip(self, out, in_):
        return self.add('dve', lambda e: e.reciprocal(out, in_), [in_], [out])

    def dma(self, out, in_, q='sp'):
        return self.add(q, lambda e: e.dma_start(out=out, in_=in_), [in_], [out], is_dma=True)

    def finish(self, aps, q='sp'):
        return self.add(q, None, list(aps), [])

    def emit(self):
        nc = self.nc
        cnt = {e: 0 for e in ENGS}
        dcnt = {}
        for op in self.ops:
            if op.is_dma and isinstance(op.sem, tuple):
                op.val = 16
            elif op.is_dma:
                dcnt[op.sem] = dcnt.get(op.sem, 0) + 16
                op.val = dcnt[op.sem]
            elif op.signal:
                cnt[op.eng] += 1
                op.val = cnt[op.eng]
        self.stats = {e: (len(self.eops[e]), cnt[e]) for e in ENGS}
        esem = {e: nc.alloc_semaphore('s_' + e) for e in ENGS}
        dsem = [nc.alloc_semaphore('s_dma%d' % i) for i in range(self.NDMA)]
        swsem = [nc.alloc_semaphore('s_sw%d' % i) for i in range(self.nsw)]

        def semof(p):
            if p.is_dma:
                return swsem[p.sem[1]] if isinstance(p.sem, tuple) else dsem[p.sem]
            return esem[p.eng]

        def run(ename, e):
            for op in self.eops[ename]:
                if op.prevdma is not None:
                    e.wait_ge(dsem[op.prevdma.sem], op.prevdma.val)
                for p in op.waits:
                    e.wait_ge(semof(p), p.val)
                if op.fn is None:
                    continue
                ins = op.fn(e)
                if op.is_dma:
                    ins.then_inc(semof(op), 16)
                elif op.signal:
                    ins.then_inc(esem[op.eng], 1)

        with nc.Block() as block:
            @block.tensor
            def _(e):
                run('pe', e)

            @block.scalar
            def _(e):
                run('act', e)

            @block.vector
            def _(e):
                run('dve', e)

            @block.gpsimd
            def _(e):
                run('pool', e)

            @block.sync
            def _(e):
                run('sp', e)
                for slot, v in dcnt.items():
                    e.wait_ge(dsem[slot], v)
                for sm in swsem:
                    e.wait_ge(sm, 16)


def bcm(ap, n):
    s = list(ap.shape)
    return ap.unsqueeze(len(s)).broadcast_to(s + [n])


def bch(ap, n):
    s = list(ap.shape)
    return ap.unsqueeze(1).broadcast_to([s[0], n] + s[1:])


C_SSDZ, C_SSDDT, C_GDNZ, C_GDNA, C_GDNB = 1408, 1792, 1804, 2060, 2068
C_AQ, C_AK, C_AV, C_AZ = 2076, 2460, 2588, 2716

(F_ID, F_TRIF, F_TRIB, F_TBF, F_TBB, F_ONES, F_NONES) = range(7)
NF_ = 7
(B_ID, B_NMF, B_NMB, B_NBF, B_NBB, B_SBF, B_SBB, B_BLK, B_TRIF, B_TRIB, B_TBF, B_TBB, B_ONES, B_NONES) = range(14)
B_LV = 14
NB_ = 20


def host_consts():
    t = np.arange(128)
    tri_f = (t[:, None] <= t[None, :]).astype(np.float32)
    tri_b = (t[:, None] >= t[None, :]).astype(np.float32)
    blk = ((t[:, None] // 64) == (t[None, :] // 64)).astype(np.float32)
    m = np.zeros((128, NF_, 128), np.float32)
    m[:, F_ID] = np.eye(128)
    m[:, F_TRIF] = tri_f
    m[:, F_TRIB] = tri_b
    m[:, F_TBF] = tri_f * blk
    m[:, F_TBB] = tri_b * blk
    m[:, F_ONES] = 1.0
    m[:, F_NONES] = -1.0
    mb = np.zeros((128, NB_, 128), np.float32)
    mb[:, B_ID] = np.eye(128)
    mb[:, B_NMF] = (tri_f - 1) * BIG
    mb[:, B_NMB] = (tri_b - 1) * BIG
    mb[:, B_NBF] = (tri_f * blk - 1) * BIG
    mb[:, B_NBB] = (tri_b * blk - 1) * BIG
    mb[:, B_SBF] = (t[:, None] < t[None, :]) * blk
    mb[:, B_SBB] = (t[:, None] > t[None, :]) * blk
    mb[:, B_BLK] = blk
    mb[:, B_TRIF] = tri_f
    mb[:, B_TRIB] = tri_b
    mb[:, B_TBF] = tri_f * blk
    mb[:, B_TBB] = tri_b * blk
    mb[:, B_ONES] = 1.0
    mb[:, B_NONES] = -1.0
    for sl in range(6):
        mb[:, B_LV + sl] = ((t[:, None] >> (sl + 1)) == (t[None, :] >> (sl + 1))) & ((t[:, None] >> sl) != (t[None, :] >> sl))
    nf = 16
    freqs = (10000.0 ** (-np.arange(nf, dtype=np.float32) / nf)).astype(np.float32)
    pos = np.arange(2048)
    ar = (pos // 64).astype(np.float32)[:, None] * freqs
    ac = (pos % 64).astype(np.float32)[:, None] * freqs
    cr, sr, cc, sc = np.cos(ar), np.sin(ar), np.cos(ac), np.sin(ac)
    cosf = np.concatenate([cr, cr, cc, cc], 1).astype(np.float32)
    sinf = np.concatenate([-sr, sr, -sc, sc], 1).astype(np.float32)
    rope = np.stack([cosf, sinf], 1).reshape(16, 128, 2, 64).transpose(1, 0, 2, 3)
    cm = np.zeros((128, 2), np.float32)
    cm[:64, 0] = 1
    cm[64:, 1] = 1
    return np.ascontiguousarray(m), np.ascontiguousarray(mb), np.ascontiguousarray(rope), cm


def build(n_layers=2, stop=None, dbg=()):
    nc = bass.Bass("TRN2", target_bir_lowering=False)
    P = Prog(nc)
    ins = {}

    def din(name, shape, dt=F32):
        ins[name] = nc.dram_tensor(name, list(shape), dt, kind="ExternalInput").ap()
        return ins[name]

    x_d = din("x", [2048, D])
    ctx_d = din("ctx", [256, D])
    cc_d = din("cc", [128, 8, 2])
    wmod_d = din("w_mod", [2, D, 3072])
    bmod_d = din("b_mod", [2, 3072])
    win_d = din("w_in", [2, D, 3100])
    wout_d = din("w_out", [2, D, D])
    normwc_d = din("normwc", [128, 2, 8])
    bmodc_d = din("bmodc", [128, 2, 16])
    convwc_d = din("convwc", [128, 2, 11, 3])
    convbc_d = din("convbc", [128, 2, 11])
    rows_d = din("rows", [2, 1024])
    constm_d = din("constm", [128, NF_, 128])
    constb_d = din("constb", [128, NB_, 128])
    rope_d = din("rope", [128, 16, 2, 64])
    cm_d = din("cm", [128, 2])
    out_d = nc.dram_tensor("out", [2048, D], F32, kind="ExternalOutput").ap()
    x1s_d = nc.dram_tensor("x1s", [2048, D], F32, kind="Internal").ap()
    dbg_out = {}

    from contextlib import ExitStack
    with ExitStack() as es:
        def sb(name, shape, dt):
            h = es.enter_context(nc.sbuf_tensor(name, list(shape), dt))
            P.addr[name] = int(nc.lookup_mloc(h).addr)
            return h

        PS = es.enter_context(nc.psum_tensor("PS", [128, 8, 512], F32))
        bank_ctr = [0]

        NROT = [8]

        def nb():
            b = bank_ctr[0] % NROT[0]
            bank_ctr[0] += 1
            return PS[:, b, :]

        HT = sb("HT", [128, 8, T], BF16)
        MIX = sb("MIX", [128, NT, 1024], BF16)
        XC = sb("XC", [128, 2, D], F32)
        CMF = sb("CMF", [128, NF_, 128], F32)
        CMB = sb("CMB", [128, NB_, 128], BF16)
        WS = [sb("WS0", [128, 8, 512], BF16), sb("WS1", [128, 8, 512], BF16)]
        SM = sb("SM", [128, 512], F32)
        DTR = sb("DTR", [128, NT, 28], F32)
        SDT = sb("SDT", [128, NT, 12], F32)
        SDA = sb("SDA", [128, NT, 12], F32)
        GG = sb("GG", [128, NT, 8], F32)
        GB = sb("GB", [128, NT, 8], F32)
        SDAH = sb("SDAH", [128, NT, 12], BF16)
        SDAL = sb("SDAL", [128, NT, 12], BF16)
        GGH = sb("GGH", [128, NT, 8], BF16)
        GGL = sb("GGL", [128, NT, 8], BF16)
        ROWS = sb("ROWS", [128, 1024], F32)
        MODC = sb("MODC", [128, 16, 2], F32)
        GC = sb("GC", [128, 8, 2], F32)
        SILC = sb("SILC", [128, 8, 2], BF16)
        SILB = sb("SILB", [128, 2, 8, 128], BF16)
        CCF = sb("CCF", [128, 8, 2], F32)
        NWC = sb("NWC", [128, 2, 8], F32)
        BMC = sb("BMC", [128, 2, 16], F32)
        CVW = sb("CVW", [128, 2, 11, 3], F32)
        CVB = sb("CVB", [128, 2, 11], F32)
        CMK = sb("CMK", [128, 2], F32)
        ARENA = sb("ARENA", [128, 22080], F32)
        arena_off = [0]

        def arena_reset():
            arena_off[0] = 0

        def al(shape, dt):
            n = 1
            for s in shape[1:]:
                n *= s
            nbytes = n * ISZ[dt]
            nwords = (nbytes + 3) // 4
            o = arena_off[0]
            assert o + nwords <= 22080, ("arena overflow", o, nwords)
            arena_off[0] = o + nwords
            v = ARENA[:, o:o + nwords]
            if dt != F32:
                v = v.bitcast(dt)[:, 0:n]
            if len(shape) == 2:
                return v
            names = ' '.join('a%d' % i for i in range(len(shape) - 1))
            kw = {'a%d' % i: shape[i + 1] for i in range(len(shape) - 2)}
            return v.rearrange("p (%s) -> p %s" % (names, names), **kw)

        def cmf(i):
            return CMF[:, i, :]

        def cmb(i):
            return CMB[:, i, :]

        IDB = cmb(B_ID)
        IDF = cmf(F_ID)
        ONESB = cmb(B_ONES)
        NONESB = cmb(B_NONES)

        def dump(name, ap):
            if name not in dbg:
                return
            d = nc.dram_tensor("dbg_" + name, list(ap.shape), ap.dtype, kind="ExternalOutput").ap()
            dbg_out[name] = d
            P.dma(d, ap)

        P.dma(CMF[:], constm_d)
        P.dma(CMB[:], constb_d, q='pool')
        P.dma(CCF[:], cc_d)
        P.dma(NWC[:], normwc_d)
        P.dma(BMC[:], bmodc_d)
        P.dma(CVW[:], convwc_d)
        P.dma(CVB[:], convbc_d)
        P.dma(CMK[:], cm_d)
        P.dma(XC[:], ctx_d.rearrange("(t p) d -> p t d", p=128))
        EPSC = SM[:, 0:1]
        P.memset('dve', EPSC, EPS)
        P.fence_scr = SM[:, 208:216]
        P.act(SILC[:], CCF[:], AF.Silu)
        for v in range(2):
            P.copy('dve', SILB[:, v, :, :], bcm(SILC[:, :, v], 128))

        def softplus(out, in_, tmp1, tmp2, eng='dve'):
            P.act(tmp1, in_, AF.Abs)
            P.act(tmp1, tmp1, AF.Exp, scale=-1.0)
            P.ts(eng, tmp1, tmp1, 1.0, None, ALU.add)
            P.act(tmp1, tmp1, AF.Ln)
            P.ts(eng, tmp2, in_, 0.0, None, ALU.max)
            P.tt(eng, out, tmp1, tmp2, ALU.add)

        import os as _os
        stop_in = stop
        stop_l = int(_os.environ.get('STOP_L', '0'))
        for l in range(n_layers):
            stop = stop_in if l == stop_l else None
            last = (l == n_layers - 1)
            xsrc = x_d if l == 0 else x1s_d
            xdst = out_d if last else x1s_d
            arena_reset()
            P.dma(ROWS[:], rows_d[l].partition_broadcast(128))
            R_DTB = ROWS[:, 0:12]
            R_AL = ROWS[:, 12:24]
            R_D = ROWS[:, 24:30]
            R_GAL = ROWS[:, 32:40]
            R_GDB = ROWS[:, 40:48]
            R_GNW = ROWS[:, 64:128]
            R_QNW = ROWS[:, 128:192]
            R_KNW = ROWS[:, 192:256]
            R_SNW = ROWS[:, 256:640]
            RA = SM[:, 8:20]
            RGA = SM[:, 20:28]
            P.act(RA, R_AL, AF.Exp)
            P.ts('dve', RA, RA, -1.0, None, ALU.mult)
            P.act(RGA, R_GAL, AF.Exp)
            P.ts('dve', RGA, RGA, -1.0, None, ALU.mult)

            GATE = al([128, 2, 1024], F32)
            mark_gate = arena_off[0]
            BG = al([128, 1024], F32)
            P.dma(BG, bmod_d[l, 2048:3072].partition_broadcast(128))
            wmod_v = wmod_d[l].rearrange("(k p) n -> p k n", p=128)
            for blk in range(6):
                ws = WS[blk % 2]
                P.dma(ws[:, :, :], wmod_v[:, :, blk * 512:(blk + 1) * 512], q='pool')
                if blk < 4:
                    pb = nb()
                    for j in range(4):
                        for k in range(8):
                            P.mm(pb[:, j * 2:(j + 1) * 2], ws[:, k, j * 128:(j + 1) * 128], SILC[:, k, :],
                                 start=(k == 0), stop=(k == 7))
                    P.tt('dve', MODC[:, blk * 4:(blk + 1) * 4, :],
                         pb[:, 0:8].rearrange("p (j v) -> p j v", v=2),
                         bcm(BMC[:, l, blk * 4:(blk + 1) * 4], 2), ALU.add)
                else:
                    for v in range(2):
                        pb = nb()
                        for k in range(8):
                            P.mm(pb[:, :], SILB[:, v, k, :], ws[:, k, :], start=(k == 0), stop=(k == 7))
                        P.tt('dve', GATE[:, v, (blk - 4) * 512:(blk - 3) * 512], pb[:, :],
                             BG[:, (blk - 4) * 512:(blk - 3) * 512], ALU.add)
            P.ts('dve', GC[:], MODC[:, 8:16, :], 1.0, None, ALU.add)
            P.tt('dve', GC[:], GC[:], bcm(NWC[:, l, :], 2), ALU.mult)
            dump("L%d_gc" % l, GC[:])
            dump("L%d_modc" % l, MODC[:])
            dump("L%d_gate" % l, GATE)

            arena_off[0] = mark_gate
            mark = arena_off[0]
            XIN = [al([128, 4, D], F32), al([128, 4, D], F32)]
            XN = [al([128, 4, D], BF16), al([128, 4, D], BF16)]
            JUNK = al([128, D], BF16)
            SS = SM[:, 32:50]
            RS = SM[:, 64:82]
            groups = [([0, 1], 1)] + [([2 + 4 * g + j for j in range(4)], 0) for g in range(4)]
            for gi, (tiles, v) in enumerate(groups):
                n = len(tiles)
                t0 = tiles[0]
                slot = gi % 2
                if v == 1:
                    xin = XC
                else:
                    xin = XIN[slot]
                    r0 = (t0 - 2) * 128
                    P.dma(xin[:, :, :], xsrc[r0:r0 + 512, :].rearrange("(t p) d -> p t d", p=128))
                for j, t in enumerate(tiles):
                    P.act(JUNK, xin[:, j, :], AF.Square, accum_out=SS[:, t:t + 1])
                P.ts('dve', RS[:, t0:t0 + n], SS[:, t0:t0 + n], 1.0 / D, EPS, ALU.mult, ALU.add)
                P.act(RS[:, t0:t0 + n], RS[:, t0:t0 + n], AF.Sqrt)
                P.recip(RS[:, t0:t0 + n], RS[:, t0:t0 + n])
                for j, t in enumerate(tiles):
                    P.ts('dve' if j % 2 == 0 else 'pool', XN[slot][:, j, :], xin[:, j, :], RS[:, t:t + 1], None, ALU.mult)
                for k in range(8):
                    pb = nb()
                    for j in range(n):
                        P.mm(pb[:, j * 128:(j + 1) * 128], XN[slot][:, j, k * 128:(k + 1) * 128], IDB)
                    dst = HT[:, k, t0 * 128:(t0 + n) * 128]
                    if k % 2 == 0:
                        P.ts('dve', dst, pb[:, 0:n * 128], GC[:, k, v:v + 1], MODC[:, k, v:v + 1], ALU.mult, ALU.add)
                    else:
                        P.act(dst, pb[:, 0:n * 128], AF.Identity, bias=MODC[:, k, v:v + 1], scale=GC[:, k, v:v + 1])
            dump("L%d_ht" % l, HT[:])
            arena_off[0] = mark
            if stop == 'p1':
                break

            win_v = win_d[l].rearrange("(k p) n -> p k n", p=128)
            WSM = al([128, 8, 28], BF16)
            P.dma(WSM[:, :, 0:12], win_v[:, :, C_SSDDT:C_SSDDT + 12], q='pool')
            P.dma(WSM[:, :, 12:28], win_v[:, :, C_GDNA:C_GDNA + 16], q='pool')
            for g0 in (0, 16):
                tiles = list(range(g0, min(g0 + 16, NT)))
                pb = nb()
                for j, t in enumerate(tiles):
                    for k in range(8):
                        P.mm(pb[:, j * 28:(j + 1) * 28], HT[:, k, t * 128:(t + 1) * 128], WSM[:, k, :],
                             start=(k == 0), stop=(k == 7))
                n = len(tiles)
                P.copy('dve', DTR[:, g0:g0 + n, :], pb[:, 0:n * 28].rearrange("p (t c) -> p t c", c=28))
            TMPA = al([128, NT, 12], F32)
            TMPB = al([128, NT, 12], F32)
            P.tt('dve', SDT[:], DTR[:, :, 0:12], bch(R_DTB, NT), ALU.add)
            softplus(SDT[:], SDT[:], TMPA, TMPB)
            P.tt('dve', SDA[:], SDT[:], bch(RA, NT), ALU.mult)
            P.copy('dve', SDAH[:], SDA[:])
            P.tt('dve', SDAL[:], SDA[:], SDAH[:], ALU.subtract)
            P.tt('dve', GG[:], DTR[:, :, 12:20], bch(R_GDB, NT), ALU.add)
            softplus(GG[:], GG[:], TMPA[:, :, 0:8], TMPB[:, :, 0:8])
            P.tt('dve', GG[:], GG[:], bch(RGA, NT), ALU.mult)
            P.act(GB[:], DTR[:, :, 20:28], AF.Sigmoid)
            P.copy('dve', GGH[:], GG[:])
            P.tt('dve', GGL[:], GG[:], GGH[:], ALU.subtract)
            dump("L%d_sdt" % l, SDT[:])
            dump("L%d_gg" % l, GG[:])
            dump("L%d_gb" % l, GB[:])
            dump("L%d_dtr" % l, DTR[:])
            if stop == 'dt':
                break

            mark_mix = arena_off[0]
            PREB = al([128, T + 4], BF16)
            DG = [al([128, 3, 128], BF16), al([128, 3, 128], BF16)]
            P.memset('pool', PREB[:, 0:1], 0.0)
            P.memset('pool', PREB[:, 257:259], 0.0)
            P.memset('pool', PREB[:, T + 3:T + 4], 0.0)
            TBK = [(0, 256)] + [(256 + 512 * i, 512) for i in range(4)]
            if stop == 'c0':
                dump("L%d_preb" % l, PREB)
                break
            cctr = [0]
            wsslot = [0]
            wsbase = [0]

            def conv_proj(ch, dest):
                dg = DG[cctr[0] % 2]
                cctr[0] += 1
                grp = {0: (0, 4), 4: (4, 5), 5: (5, 9), 9: (9, 11)}
                if ch in grp:
                    c0, c1 = grp[ch]
                    wsslot[0] = (wsslot[0] + 1) % 2
                    wsbase[0] = c0
                    P.dma(WS[wsslot[0]][:, :, 0:(c1 - c0) * 128], win_v[:, :, c0 * 128:c1 * 128], q='pool')
                ws = WS[wsslot[0]][:, :, (ch - wsbase[0]) * 128:(ch - wsbase[0] + 1) * 128]
                for k in range(3):
                    P.ts('pool', dg[:, k, :], IDF, CVW[:, l, ch, k:k + 1], None, ALU.mult)
                for bi, (t0, n) in enumerate(TBK):
                    pb = nb()
                    for k in range(8):
                        P.mm(pb[:, 0:n], ws[:, k, 0:128], HT[:, k, t0:t0 + n], start=(k == 0), stop=(k == 7))
                    po = t0 + 1 if t0 < 256 else t0 + 3
                    if bi % 2 == 0:
                        P.copy('dve', PREB[:, po:po + n], pb[:, 0:n])
                    else:
                        P.copy('act', PREB[:, po:po + n], pb[:, 0:n])
                if stop == 'c1':
                    return
                for bi, (t0, n) in enumerate(TBK):
                    po = t0 + 1 if t0 < 256 else t0 + 3
                    pb2 = nb()
                    for k in range(3):
                        P.mm(pb2[:, 0:n], dg[:, k, :], PREB[:, po - 1 + k:po - 1 + k + n], start=(k == 0), stop=(k == 2))
                    P.act(dest[:, t0:t0 + n], pb2[:, 0:n], AF.Silu, bias=CVB[:, l, ch:ch + 1])

            def to_tok(src, CTt, c0):
                for gi, g0 in enumerate(range(0, NT, 4)):
                    tiles = list(range(g0, min(g0 + 4, NT)))
                    n = len(tiles)
                    pb = nb()
                    for j, t in enumerate(tiles):
                        P.mm(pb[:, j * 128:(j + 1) * 128], src[:, t * 128:(t + 1) * 128], IDB)
                    P.copy('dve' if gi % 2 == 0 else 'act', CTt[:, g0:g0 + n, c0:c0 + 128],
                           pb[:, 0:n * 128].rearrange("p (t c) -> p t c", c=128))

            mark_ssd = arena_off[0]
            CTS = al([128, NT, 512], BF16)
            CFB = al([128, T], BF16)
            CFC = al([128, T], BF16)
            XF = [al([128, T], BF16), al([128, T], BF16)]
            if stop in ('c1', 'c2'):
                conv_proj(0, XF[0])
                dump("L%d_preb" % l, PREB)
                dump("L%d_xf" % l, XF[0])
                if stop == 'c2':
                    to_tok(XF[0], CTS, 0)
                    dump("L%d_cts" % l, CTS)
                break
            for ch in range(3):
                conv_proj(ch, XF[ch % 2])
                to_tok(XF[ch % 2], CTS, ch * 128)
            conv_proj(3, CFB)
            to_tok(CFB, CTS, 384)
            conv_proj(4, CFC)
            dump("L%d_cts" % l, CTS)
            dump("L%d_cfc" % l, CFC)
            if stop == 'conv':
                break
            RALH = [al([128, 6, 128], BF16), al([128, 6, 128], BF16)]
            RALL = [al([128, 6, 128], BF16), al([128, 6, 128], BF16)]
            EE = [al([128, 6, 128], F32), al([128, 6, 128], F32)]
            WTT = [al([128, 6, 128], BF16), al([128, 6, 128], BF16)]
            XS = [al([128, 6, 64], BF16), al([128, 6, 64], BF16)]
            TMPY = al([128, 6, 64], F32)
            TY2 = al([128, 384], F32)
            TY3 = al([128, 384], F32)
            SST = [al([128, 384], F32), al([128, 384], F32)]
            SBF = [al([128, 384], BF16), al([128, 384], BF16)]
            NCUM = SM[:, 96:102]
            ECUM = SM[:, 104:110]
            SD = SM[:, 112:118]
            CD = SM[:, 120:126]
            DFULL = al([128, 6, 64], F32)
            P.copy('dve', DFULL, bcm(R_D, 64))
            sctr = [0]
            for d in range(2):
                P.memset('dve', SST[d], 0.0)
                P.memset('pool', SBF[d], 0.0)
                order = list(range(NT)) if d == 0 else [1, 0] + list(range(NT - 1, 1, -1))
                import os
                order = order[:int(os.environ.get('SSD_N', '99'))]
                TRI = cmb(B_TRIF if d == 0 else B_TRIB)
                NMK = cmb(B_NMF if d == 0 else B_NMB)
                lastc = 127 if d == 0 else 0
                for t in order:
                    par = sctr[0] % 2
                    sctr[0] += 1
                    tok = slice(t * 128, (t + 1) * 128)
                    need_y = not (last and t < 2)
                    a6h = SDAH[:, t, d * 6:(d + 1) * 6]
                    a6l = SDAL[:, t, d * 6:(d + 1) * 6]
                    dt6 = SDT[:, t, d * 6:(d + 1) * 6]
                    rah, ral, ee, wt, xs = RALH[par], RALL[par], EE[par], WTT[par], XS[par]
                    P.tt('pool', rah, bch(TRI, 6), bcm(a6h, 128), ALU.mult)
                    P.tt('pool', ral, bch(TRI, 6), bcm(a6l, 128), ALU.mult)
                    pcol = nb()
                    P.mm(pcol[:, 0:6], TRI, a6h, start=True, stop=False)
                    P.mm(pcol[:, 0:6], TRI, a6l, start=False, stop=True)
                    P.ts('dve', NCUM, pcol[:, 0:6], -1.0, None, ALU.mult)
                    P.act(ECUM, pcol[:, 0:6], AF.Exp)
                    if stop == 's1':
                        break
                    pA = nb()
                    pB = nb()
                    dsts = []
                    for h in range(6):
                        dst = (pA if h < 4 else pB)[:, (h % 4) * 128:(h % 4 + 1) * 128]
                        dsts.append(dst)
                        P.mm(dst, ONESB, rah[:, h, :], start=True, stop=False)
                        P.mm(dst, ONESB, ral[:, h, :], start=False, stop=False)
                        P.mm(dst, IDB, NMK, start=False, stop=True)
                    for h in range(6):
                        P.act(ee[:, h, :], dsts[h], AF.Exp, bias=NCUM[:, h:h + 1])
                    if stop == 's2':
                        break
                    P.mm(pcol[:, 8:14], ONESB, a6h, start=True, stop=False)
                    P.mm(pcol[:, 8:14], ONESB, a6l, start=False, stop=True)
                    P.tt('dve', SD, pcol[:, 8:14], NCUM, ALU.add)
                    P.act(SD, SD, AF.Exp)
                    P.tt('dve', SD, SD, dt6, ALU.mult)
                    P.act(CD, pcol[:, 8:14], AF.Exp)
                    if stop == 's3b':
                        break
                    P.tt('pool', xs, CTS[:, t, 0:384].rearrange("p (h f) -> p h f", h=6), bcm(SD, 64), ALU.mult)
                    if stop == 's3':
                        break
                    if need_y:
                        psc = [nb(), nb()]
                        for g in range(2):
                            P.mm(psc[g][:, 0:128], CFB[g * 64:(g + 1) * 64, tok], CFC[g * 64:(g + 1) * 64, tok])
                        for h in range(6):
                            g = h // 3
                            P.stt(wt[:, h, :], psc[g][:, 0:128], dt6[:, h:h + 1], ee[:, h, :], ALU.mult, ALU.mult)
                        if stop == 'y1':
                            break
                        py = nb()
                        poff = [nb(), nb()]
                        for h in range(6):
                            P.mm(py[:, h * 64:(h + 1) * 64], wt[:, h, :], CTS[:, t, h * 64:(h + 1) * 64])
                        for g in range(2):
                            P.mm(poff[g][:, 0:192], CFC[g * 64:(g + 1) * 64, tok],
                                 SBF[d][g * 64:(g + 1) * 64, g * 192:(g + 1) * 192])
                        if stop == 'y2':
                            break
                        for g in range(2):
                            P.tt('dve', TMPY[:, 3 * g:3 * g + 3, :], poff[g][:, 0:192].rearrange("p (h f) -> p h f", h=3),
                                 bcm(ECUM[:, 3 * g:3 * g + 3], 64), ALU.mult)
                        P.tt('dve', TY2, py[:, 0:384], TMPY.rearrange("p h f -> p (h f)"), ALU.add)
                        if d == 0:
                            P.tt('pool', TY3, CTS[:, t, 0:384], DFULL.rearrange("p h f -> p (h f)"), ALU.mult)
                            P.tt('pool', MIX[:, t, 0:384], TY2, TY3, ALU.add)
                        else:
                            P.tt('pool', MIX[:, t, 0:384], TY2, MIX[:, t, 0:384], ALU.add)
                    if stop == 's4':
                        break
                    pst = nb()
                    P.mm(pst[:, 0:384], CTS[:, t, 384:512], xs.rearrange("p h f -> p (h f)"))
                    P.tt('pool', SST[d].rearrange("p (h f) -> p h f", h=6), SST[d].rearrange("p (h f) -> p h f", h=6),
                         bcm(CD, 64), ALU.mult)
                    P.tt('dve', SST[d], SST[d], pst[:, 0:384], ALU.add)
                    P.copy('pool', SBF[d], SST[d])
                    if stop == 's5':
                        break
            dump("L%d_mixs" % l, MIX[:, :, 0:384])
            dump("L%d_mix" % l, MIX[:])
            arena_off[0] = mark_ssd
            if stop in ('ssd', 's1', 's2', 's3', 's4', 's5', 's3a', 's3b', 'y1', 'y2'):
                break

            def nb2():
                if bank_ctr[0] % 2 == 1:
                    bank_ctr[0] += 1
                b = bank_ctr[0] % NROT[0]
                bank_ctr[0] += 2
                return PS[:, b:b + 2, :]

            hpi = lambda h: (h % 2) * 2 + h // 2
            GBP = al([128, NT, 8], F32)
            GHP = al([128, NT, 8], BF16)
            GLP = al([128, NT, 8], BF16)
            for dd in range(2):
                for h in range(4):
                    P.copy('dve', GBP[:, :, dd * 4 + hpi(h)], GB[:, :, dd * 4 + h])
                    P.copy('dve', GHP[:, :, dd * 4 + hpi(h)], GGH[:, :, dd * 4 + h])
                    P.copy('dve', GLP[:, :, dd * 4 + hpi(h)], GGL[:, :, dd * 4 + h])
            CFQK = al([128, 4, T], BF16)
            CTG = al([128, NT, 512], BF16)
            mark_gcore = arena_off[0]
            XFG = [al([128, T], BF16), al([128, T], BF16)]
            SQ = al([128, 512], BF16)
            RN = al([128, 512], F32)
            for ci in range(4):
                xf = XFG[ci % 2]
                conv_proj(5 + ci, xf)
                for (t0, n) in TBK:
                    P.tt('dve', SQ[:, 0:n], xf[:, t0:t0 + n], xf[:, t0:t0 + n], ALU.mult)
                    pb = nb()
                    P.mm(pb[:, 0:n], cmb(B_BLK), SQ[:, 0:n])
                    P.act(RN[:, 0:n], pb[:, 0:n], AF.Sqrt, bias=EPSC)
                    P.recip(RN[:, 0:n], RN[:, 0:n])
                    if ci < 2:
                        P.stt(CFQK[:, ci, t0:t0 + n], xf[:, t0:t0 + n], 0.125, RN[:, 0:n], ALU.mult, ALU.mult)
                    else:
                        P.tt('dve', CFQK[:, ci, t0:t0 + n], xf[:, t0:t0 + n], RN[:, 0:n], ALU.mult)
                if ci >= 2:
                    to_tok(CFQK[:, ci, :], CTG, (ci - 2) * 128)
            for ci in range(2):
                xf = XFG[ci % 2]
                conv_proj(9 + ci, xf)
                to_tok(xf, CTG, 256 + ci * 128)
            dump("L%d_cfqk" % l, CFQK)
            dump("L%d_ctg" % l, CTG)
            if stop == 'gconv':
                break
            arena_off[0] = mark_gcore
            RGH = al([128, 4, 128], BF16)
            RGL = al([128, 4, 128], BF16)
            GMH = al([128, 4, 2], BF16)
            GML = al([128, 4, 2], BF16)
            EG = al([128, 4, 128], F32)
            ES = al([128, 4, 128], F32)
            T1 = al([128, 4, 128], F32)
            XX = [al([128, 4, 128], BF16), al([128, 4, 128], BF16)]
            XXT = [al([128, 4, 128], BF16), al([128, 4, 128], BF16)]
            WW = [al([128, 4, 128], BF16), al([128, 4, 128], BF16)]
            WWT = [al([128, 4, 128], BF16), al([128, 4, 128], BF16)]
            CTS_ = al([128, 4, 128], BF16)
            CS_ = al([128, 4, 128], BF16)
            YY = al([128, 4, 128], BF16)
            YYT = al([128, 4, 128], BF16)
            ITT = al([128, 4, 128], BF16)
            UU = al([128, 256], F32)
            KG = al([128, 4, 64], BF16)
            KD = al([128, 4, 64], BF16)
            WTG = al([128, 2, 128], BF16)
            VN = al([128, 256], BF16)
            OA = al([128, 256], F32)
            OA2 = al([128, 256], F32)
            SG = [al([128, 128], F32), al([128, 128], F32)]
            SGB = [al([128, 128], BF16), al([128, 128], BF16)]
            NGC = SM[:, 128:132]
            EGC = SM[:, 136:140]
            F1 = SM[:, 144:148]
            CDF = SM[:, 152:156].rearrange("p (c hh) -> p c hh", c=2)
            for d in range(2):
                P.memset('dve', SG[d], 0.0)
                P.memset('dve', SGB[d], 0.0)
                order = list(range(NT)) if d == 0 else [1, 0] + list(range(NT - 1, 1, -1))
                import os
                order = order[:int(os.environ.get('GDN_N', '99'))]
                TB = cmb(B_TBF if d == 0 else B_TBB)
                NMK = cmb(B_NBF if d == 0 else B_NBB)
                SMK = cmb(B_SBF if d == 0 else B_SBB)
                for t in order:
                    tok = slice(t * 128, (t + 1) * 128)
                    need_o = not (last and t < 2)
                    gh = GHP[:, t, d * 4:(d + 1) * 4]
                    gl = GLP[:, t, d * 4:(d + 1) * 4]
                    bp = GBP[:, t, d * 4:(d + 1) * 4]
                    P.tt('dve', RGH, bch(TB, 4), bcm(gh, 128), ALU.mult)
                    P.tt('dve', RGL, bch(TB, 4), bcm(gl, 128), ALU.mult)
                    P.tt('dve', GMH, bcm(gh, 2), bch(CMK[:], 4), ALU.mult)
                    P.tt('dve', GML, bcm(gl, 2), bch(CMK[:], 4), ALU.mult)
                    pcol = nb()
                    P.mm(pcol[:, 0:4], TB, gh, start=True, stop=False)
                    P.mm(pcol[:, 0:4], TB, gl, start=False, stop=True)
                    P.mm(pcol[:, 8:16], ONESB, GMH.rearrange("p h c -> p (h c)"), start=True, stop=False)
                    P.mm(pcol[:, 8:16], ONESB, GML.rearrange("p h c -> p (h c)"), start=False, stop=True)
                    P.ts('dve', NGC, pcol[:, 0:4], -1.0, None, ALU.mult)
                    P.act(EGC, pcol[:, 0:4], AF.Exp)
                    pcv = pcol[:, 8:16].rearrange("p (hl hh c) -> p hl hh c", hl=2, hh=2)
                    for hl in range(2):
                        for c in range(2):
                            P.act(CDF[hl * 64:(hl + 1) * 64, c, :], pcv[hl * 64:(hl + 1) * 64, hl, :, c], AF.Exp)
                    pd = nb()
                    for hp in range(4):
                        dst = pd[:, hp * 128:(hp + 1) * 128]
                        P.mm(dst, ONESB, RGH[:, hp, :], start=True, stop=False)
                        P.mm(dst, ONESB, RGL[:, hp, :], start=False, stop=False)
                        P.mm(dst, IDB, NMK, start=False, stop=True)
                    for hp in range(4):
                        P.act(EG[:, hp, :], pd[:, hp * 128:(hp + 1) * 128], AF.Exp, bias=NGC[:, hp:hp + 1])
                    pkk = nb2()
                    pqk = nb2()
                    for h in range(4):
                        hl, hh = h % 2, h // 2
                        kf = CFQK[hl * 64:(hl + 1) * 64, 2 + hh, tok]
                        qf = CFQK[hl * 64:(hl + 1) * 64, hh, tok]
                        P.mm(pkk[:, hl, hh * 128:(hh + 1) * 128], kf, kf)
                        P.mm(pqk[:, hl, hh * 128:(hh + 1) * 128], kf, qf)
                    P.tt('dve', ES, EG, bch(SMK, 4), ALU.mult)
                    for hl in range(2):
                        P.tt('dve', T1[:, hl * 2:(hl + 1) * 2, :], pkk[:, hl, 0:256].rearrange("p (b i) -> p b i", b=2),
                             bcm(bp[:, hl * 2:(hl + 1) * 2], 128), ALU.mult)
                    P.tt('dve', XX[0], T1, ES, ALU.mult)
                    for hl in range(2):
                        P.tt('dve', T1[:, hl * 2:(hl + 1) * 2, :], pqk[:, hl, 0:256].rearrange("p (b i) -> p b i", b=2),
                             bcm(bp[:, hl * 2:(hl + 1) * 2], 128), ALU.mult)
                    P.tt('dve', ITT, T1, EG, ALU.mult)
                    pt = nb()
                    for hp in range(4):
                        P.mm(pt[:, hp * 128:(hp + 1) * 128], XX[0][:, hp, :], IDB)
                    P.copy('act', XXT[0].rearrange("p h i -> p (h i)"), pt[:, :])
                    MPm, LPm = XX[0], XXT[0]
                    Wc = [WW[0], WW[1]]
                    Wtc = [WWT[0], WWT[1]]
                    m0 = cmb(B_LV)
                    P.tt('dve', T1, LPm, bch(m0, 4), ALU.mult)
                    P.stt(Wc[0], T1, -1.0, bch(IDB, 4), ALU.mult, ALU.add)
                    P.tt('dve', T1, MPm, bch(m0, 4), ALU.mult)
                    P.stt(Wtc[0], T1, -1.0, bch(IDB, 4), ALU.mult, ALU.add)
                    cur = 0
                    for lev in range(1, 6):
                        ml = cmb(B_LV + lev)
                        P.tt('dve', CTS_, MPm, bch(ml, 4), ALU.mult)
                        P.tt('dve', CS_, LPm, bch(ml, 4), ALU.mult)
                        p1 = nb()
                        for hp in range(4):
                            P.mm(p1[:, hp * 128:(hp + 1) * 128], CTS_[:, hp, :], Wc[cur][:, hp, :])
                        p2 = nb()
                        for hp in range(4):
                            P.mm(p2[:, hp * 128:(hp + 1) * 128], CS_[:, hp, :], Wtc[cur][:, hp, :])
                        P.copy('act', YY.rearrange("p h i -> p (h i)"), p1[:, :])
                        P.copy('dve', YYT.rearrange("p h i -> p (h i)"), p2[:, :])
                        p3 = nb()
                        for hp in range(4):
                            P.mm(p3[:, hp * 128:(hp + 1) * 128], Wtc[cur][:, hp, :], YY[:, hp, :])
                        p4 = nb()
                        for hp in range(4):
                            P.mm(p4[:, hp * 128:(hp + 1) * 128], Wc[cur][:, hp, :], YYT[:, hp, :])
                        P.tt('dve', Wc[1 - cur], Wc[cur], p3.rearrange("p (h i) -> p h i", h=4), ALU.subtract)
                        P.tt('dve', Wtc[1 - cur], Wtc[cur], p4.rearrange("p (h i) -> p h i", h=4), ALU.subtract)
                        cur = 1 - cur
                    PM = Wtc[cur]
                    pu = nb()
                    for h in range(4):
                        hp = hpi(h)
                        P.mm(pu[:, hp * 64:(hp + 1) * 64], PM[:, hp, :], CTG[:, t, 256 + h * 64:256 + (h + 1) * 64])
                    P.copy('dve', UU, pu[:, 0:256])
                    kv = CTG[:, t, 0:256].rearrange("p (hh hl f) -> p hh hl f", hh=2, hl=2)
                    for hl in range(2):
                        P.tt('dve', KG[:, hl * 2:(hl + 1) * 2, :], kv[:, :, hl, :], bcm(EGC[:, hl * 2:(hl + 1) * 2], 64), ALU.mult)
                    pw = nb()
                    for hp in range(4):
                        hl, hh = hp // 2, hp % 2
                        P.mm(pw[hl * 64:(hl + 1) * 64, hh * 128:(hh + 1) * 128], KG[:, hp, :], PM[:, hp, :])
                    P.copy('act', WTG.rearrange("p h i -> p (h i)"), pw[:, 0:256])
                    for c in range(2):
                        rows = slice(c * 64, (c + 1) * 64)
                        lastcol = c * 64 + (63 if d == 0 else 0)
                        P.tt('dve', F1[rows, :], EG[rows, :, lastcol], bp[rows, :], ALU.mult)
                    for hl in range(2):
                        P.tt('dve', KD[:, hl * 2:(hl + 1) * 2, :], kv[:, :, hl, :], bcm(F1[:, hl * 2:(hl + 1) * 2], 64), ALU.mult)
                    for c in ([0, 1] if d == 0 else [1, 0]):
                        rows = slice(c * 64, (c + 1) * 64)
                        ctok = slice(t * 128 + c * 64, t * 128 + (c + 1) * 64)
                        pr = nb2()
                        for hp in range(4):
                            hl, hh = hp // 2, hp % 2
                            sblk = SGB[d][hl * 64:(hl + 1) * 64, hh * 64:(hh + 1) * 64]
                            P.mm(pr[rows, hl, hh * 64:(hh + 1) * 64], WTG[hl * 64:(hl + 1) * 64, hh, c * 64:(c + 1) * 64], sblk)
                            if need_o:
                                P.mm(pr[rows, hl, 128 + hh * 64:128 + (hh + 1) * 64], CFQK[hl * 64:(hl + 1) * 64, hh, ctok], sblk)
                        P.tt('dve', VN[rows, :].rearrange("p (a b) -> p a b", a=2), UU[rows, :].rearrange("p (a b) -> p a b", a=2),
                             pr[rows, :, 0:128], ALU.subtract)
                        if need_o:
                            for hl in range(2):
                                P.tt('dve', OA[rows, hl * 128:(hl + 1) * 128].rearrange("p (a b) -> p a b", a=2),
                                     pr[rows, hl, 128:256].rearrange("p (a b) -> p a b", a=2),
                                     bcm(EGC[rows, hl * 2:(hl + 1) * 2], 64), ALU.mult)
                            po = nb()
                            for hp in range(4):
                                P.mm(po[rows, hp * 64:(hp + 1) * 64], ITT[rows, hp, c * 64:(c + 1) * 64], VN[rows, hp * 64:(hp + 1) * 64])
                            P.tt('dve', OA2[rows, :], OA[rows, :], po[rows, 0:256], ALU.add)
                            mv = MIX[rows, t, 384:640].rearrange("p (hh hl f) -> p hh hl f", hh=2, hl=2)
                            for hl in range(2):
                                src = OA2[rows, hl * 128:(hl + 1) * 128].rearrange("p (a b) -> p a b", a=2)
                                if d == 0:
                                    P.copy('dve', mv[:, :, hl, :], src)
                                else:
                                    P.tt('dve', mv[:, :, hl, :], mv[:, :, hl, :], src, ALU.add)
                        psu = nb()
                        for hp in range(4):
                            hl, hh = hp // 2, hp % 2
                            P.mm(psu[hl * 64:(hl + 1) * 64, hh * 64:(hh + 1) * 64], KD[rows, hp, :], VN[rows, hp * 64:(hp + 1) * 64])
                        P.tt('dve', SG[d].rearrange("p (a b) -> p a b", a=2), SG[d].rearrange("p (a b) -> p a b", a=2),
                             bcm(CDF[:, c, :], 64), ALU.mult)
                        P.tt('dve', SG[d], SG[d], psu[:, 0:128], ALU.add)
                        P.copy('act', SGB[d], SG[d])
            dump("L%d_mixg" % l, MIX[:, :, 0:640])
            arena_off[0] = mark_ssd
            if stop == 'gdn':
                break

            arena_off[0] = mark_mix
            ROPE = al([128, 16, 2, 64], F32)
            P.dma(ROPE, rope_d)
            QKT = al([128, 4, T], BF16)
            VA = al([128, NT, 2, 65], BF16)
            NWB = al([128, 512], F32)
            SQF = al([128, 512], F32)
            QN = al([128, 512], F32)
            T1R = al([128, 512], F32)
            T2R = al([128, 512], F32)
            SINR = al([128, 512], F32)
            DST = al([128, 512], BF16)
            SSQ = SM[:, 160:168]
            P.dma(WS[0][:, :, 0:512], win_v[:, :, C_AQ:C_AQ + 512], q='pool')
            P.dma(WS[1][:, :, 0:128], win_v[:, :, C_AV:C_AV + 128], q='pool')
            P.ts('dve', NWB[:, 0:384].rearrange("p (h f) -> p h f", h=6), bch(R_QNW, 6), 0.125, None, ALU.mult)
            P.copy('dve', NWB[:, 384:512].rearrange("p (h f) -> p h f", h=2), bch(R_KNW, 2))
            P.memset('dve', VA[:, :, :, 64:65], 1.0)
            for t in range(NT):
                tok = slice(t * 128, (t + 1) * 128)
                pa = nb()
                pv = nb()
                for k in range(8):
                    P.mm(pa[:, 0:512], HT[:, k, tok], WS[0][:, k, 0:512], start=(k == 0), stop=(k == 7))
                for k in range(8):
                    P.mm(pv[:, 0:128], HT[:, k, tok], WS[1][:, k, 0:128], start=(k == 0), stop=(k == 7))
                P.copy('act', VA[:, t, :, 0:64], pv[:, 0:128].rearrange("p (g f) -> p g f", g=2))
                P.act(SQF, pa[:, 0:512], AF.Square)
                P.reduce('dve', SSQ, SQF.rearrange("p (h f) -> p h f", h=8))
                P.ts('dve', SSQ, SSQ, 1.0 / 64, EPS, ALU.mult, ALU.add)
                P.act(SSQ, SSQ, AF.Sqrt)
                P.recip(SSQ, SSQ)
                P.tt('dve', QN.rearrange("p (h f) -> p h f", h=8), pa[:, 0:512].rearrange("p (h f) -> p h f", h=8),
                     bcm(SSQ, 64), ALU.mult)
                P.tt('dve', QN, QN, NWB, ALU.mult)
                if t >= 2:
                    P.tt('dve', T1R.rearrange("p (h f) -> p h f", h=8), QN.rearrange("p (h f) -> p h f", h=8),
                         bch(ROPE[:, t - 2, 0, :], 8), ALU.mult)
                    P.copy('act', SINR.rearrange("p (h f) -> p h f", h=8), bch(ROPE[:, t - 2, 1, :], 8))
                    qv = QN.rearrange("p (ha s f) -> p ha s f", s=2, f=16)
                    sv = SINR.rearrange("p (ha s f) -> p ha s f", s=2, f=16)
                    tv = T2R.rearrange("p (ha s f) -> p ha s f", s=2, f=16)
                    P.tt('dve', tv[:, :, 0, :], qv[:, :, 1, :], sv[:, :, 0, :], ALU.mult)
                    P.tt('dve', tv[:, :, 1, :], qv[:, :, 0, :], sv[:, :, 1, :], ALU.mult)
                    P.tt('dve', T1R, T1R, T2R, ALU.add)
                    srcq = T1R
                else:
                    srcq = QN
                P.copy('act', DST[:, 0:128], srcq[:, 384:512])
                dq = DST[:, 128:512].rearrange("p (a g f) -> p a g f", a=3, g=2)
                for g in range(2):
                    P.copy('dve', dq[:, :, g, :], srcq[:, g * 192:(g + 1) * 192].rearrange("p (a f) -> p a f", a=3))
                ptq = nb()
                for j in range(4):
                    P.mm(ptq[:, j * 128:(j + 1) * 128], DST[:, j * 128:(j + 1) * 128], IDB)
                P.copy('act', QKT[:, :, tok], ptq.rearrange("p (j i) -> p j i", j=4))
            dump("L%d_qkt" % l, QKT)
            dump("L%d_va" % l, VA)
            if stop == 'aprep':
                break
            PT = [al([128, 512], BF16), al([128, 512], BF16), al([128, 512], BF16)]
            AO = al([128, 4, 384], F32)
            REC = SM[:, 176:180]
            qblocks = [(256 + 512 * i, 512, list(range(NT))) for i in range(4)]
            if not last:
                qblocks = [(0, 256, [0, 1])] + qblocks
            import os
            qblocks = qblocks[:int(os.environ.get('ATT_N', '99'))]
            pctr = 0
            NROT[0] = 6
            actr = 0
            for (q0, nq, ktiles) in qblocks:
                nj = nq // 128
                for g in range(2):
                    for a in range(3):
                        h = 3 * g + a
                        pacc = PS[:, 6 + actr % 2, :]
                        actr += 1
                        for kt in ktiles:
                            ps = nb()
                            P.mm(ps[:, 0:nq], QKT[g * 64:(g + 1) * 64, 0, kt * 128:(kt + 1) * 128],
                                 QKT[g * 64:(g + 1) * 64, 1 + a, q0:q0 + nq])
                            pt = PT[pctr % 3]
                            pctr += 1
                            P.act(pt[:, 0:nq], ps[:, 0:nq], AF.Exp)
                            for j in range(nj):
                                P.mm(pacc[:, j * 65:(j + 1) * 65], pt[:, j * 128:(j + 1) * 128], VA[:, kt, g, :],
                                     start=(kt == ktiles[0] and j == 0), stop=(kt == ktiles[-1] and j == nj - 1))
                        pv3 = pacc[:, 0:nj * 65].rearrange("p (j c) -> p j c", c=65)
                        P.recip(REC[:, 0:nj], pv3[:, :, 64])
                        P.tt('dve', AO[:, 0:nj, h * 64:(h + 1) * 64], pv3[:, :, 0:64], bcm(REC[:, 0:nj], 64), ALU.mult)
                tq = q0 // 128
                P.copy('act', MIX[:, tq:tq + nj, 640:1024], AO[:, 0:nj, :])
            NROT[0] = 8
            dump("L%d_mixa" % l, MIX[:])
            if stop == 'attn':
                break

            arena_off[0] = mark_gate
            WZ = al([128, 8, 1024], BF16)
            WO = al([128, 8, 1024], BF16)
            P.dma(WZ[:, :, 0:384], win_v[:, :, C_SSDZ:C_SSDZ + 384], q='pool')
            P.dma(WZ[:, :, 384:640], win_v[:, :, C_GDNZ:C_GDNZ + 256], q='pool')
            P.dma(WZ[:, :, 640:1024], win_v[:, :, C_AZ:C_AZ + 384], q='pool')
            P.dma(WO[:, :, :], wout_d[l].rearrange("(k p) n -> p k n", p=128), q='pool')
            XTL = [al([128, D], F32), al([128, D], F32)]
            ZS = al([128, 1024], F32)
            G1 = al([128, 1024], F32)
            MB = al([128, 1024], BF16)
            MT = al([128, 8, 128], BF16)
            UPD = al([128, 1024], F32)
            JK = al([128, 384], F32)
            FS = SM[:, 192:200]
            ftiles = list(range(NT)) if not last else list(range(2, NT))
            for t in ftiles:
                tok = slice(t * 128, (t + 1) * 128)
                v = 1 if t < 2 else 0
                if t >= 2:
                    xt = XTL[t % 2]
                    P.dma(xt, xsrc[(t - 2) * 128:(t - 1) * 128, :])
                else:
                    xt = XC[:, t, :]
                pz = [nb(), nb()]
                for nbk in range(2):
                    for k in range(8):
                        P.mm(pz[nbk][:, :], HT[:, k, tok], WZ[:, k, nbk * 512:(nbk + 1) * 512], start=(k == 0), stop=(k == 7))
                P.act(ZS[:, 0:512], pz[0][:, :], AF.Silu)
                P.act(ZS[:, 512:1024], pz[1][:, :], AF.Silu)
                P.tt('dve', G1[:, 0:384], MIX[:, t, 0:384], ZS[:, 0:384], ALU.mult)
                P.act(JK, G1[:, 0:384], AF.Square, accum_out=FS[:, 0:1])
                P.ts('dve', FS[:, 1:2], FS[:, 0:1], 1.0 / 384, EPS, ALU.mult, ALU.add)
                P.act(FS[:, 1:2], FS[:, 1:2], AF.Sqrt)
                P.recip(FS[:, 1:2], FS[:, 1:2])
                P.stt(MB[:, 0:384], G1[:, 0:384], FS[:, 1:2], R_SNW, ALU.mult, ALU.mult)
                P.act(JK[:, 0:256], MIX[:, t, 384:640], AF.Square)
                P.reduce('dve', FS[:, 4:8], JK[:, 0:256].rearrange("p (h f) -> p h f", h=4))
                P.ts('dve', FS[:, 4:8], FS[:, 4:8], 1.0 / 64, EPS, ALU.mult, ALU.add)
                P.act(FS[:, 4:8], FS[:, 4:8], AF.Sqrt)
                P.recip(FS[:, 4:8], FS[:, 4:8])
                g3 = G1[:, 384:640].rearrange("p (h f) -> p h f", h=4)
                P.tt('dve', g3, MIX[:, t, 384:640].rearrange("p (h f) -> p h f", h=4), bcm(FS[:, 4:8], 64), ALU.mult)
                P.tt('dve', g3, g3, bch(R_GNW, 4), ALU.mult)
                P.tt('dve', MB[:, 384:640], G1[:, 384:640], ZS[:, 384:640], ALU.mult)
                P.tt('dve', MB[:, 640:1024], MIX[:, t, 640:1024], ZS[:, 640:1024], ALU.mult)
                for half in range(2):
                    ptm = nb()
                    for j in range(4):
                        kc = half * 4 + j
                        P.mm(ptm[:, j * 128:(j + 1) * 128], MB[:, kc * 128:(kc + 1) * 128], IDB)
                    P.copy('act' if half else 'dve', MT[:, half * 4:(half + 1) * 4, :], ptm.rearrange("p (j i) -> p j i", j=4))
                for nbk in range(2):
                    po = nb()
                    for kc in range(8):
                        P.mm(po[:, :], MT[:, kc, :], WO[:, kc, nbk * 512:(nbk + 1) * 512], start=(kc == 0), stop=(kc == 7))
                    P.tt('dve', UPD[:, nbk * 512:(nbk + 1) * 512], po[:, :], GATE[:, v, nbk * 512:(nbk + 1) * 512], ALU.mult)
                if t >= 2:
                    P.tt('dve', xt, xt, UPD, ALU.add)
                    P.dma(xdst[(t - 2) * 128:(t - 1) * 128, :], xt)
                else:
                    P.tt('dve', XC[:, t, :], XC[:, t, :], UPD, ALU.add)
            dump("L%d_wz" % l, WZ)
            dump("L%d_wo" % l, WO)
            if stop == 'L0':
                dump("L%d_xc" % l, XC[:])
                break

        dump("final_mix", MIX[:])
        P.finish([out_d] + list(dbg_out.values()))
        P.emit()
    return nc, dbg_out


def prep_inputs(inputs):
    f = lambda a: np.ascontiguousarray(np.asarray(a, dtype=np.float32))
    constm, constb, rope, cm = host_consts()
    c = f(inputs['c'])
    c_ctx = f(inputs['c_ctx'])
    norm_w = f(inputs['norm_w'])
    b_mod = f(inputs['b_mod'])
    conv_w = f(inputs['conv_w'])
    conv_b = f(inputs['conv_b'])
    rows = np.zeros((2, 1024), np.float32)
    rows[:, 0:12] = f(inputs['ssd_dt_bias']).reshape(2, 12)
    rows[:, 12:24] = f(inputs['ssd_A_log']).reshape(2, 12)
    rows[:, 24:30] = f(inputs['ssd_D'])
    rows[:, 32:40] = f(inputs['gdn_A_log']).reshape(2, 8)
    rows[:, 40:48] = f(inputs['gdn_dt_bias']).reshape(2, 8)
    rows[:, 64:128] = f(inputs['gdn_norm_w'])
    rows[:, 128:192] = f(inputs['q_norm_w'])
    rows[:, 192:256] = f(inputs['k_norm_w'])
    rows[:, 256:640] = f(inputs['ssd_norm_w'])
    shared = {
        'w_mod': f(inputs['w_mod']), 'b_mod': b_mod, 'w_in': f(inputs['w_in']), 'w_out': f(inputs['w_out']),
        'normwc': np.ascontiguousarray(norm_w.reshape(2, 8, 128).transpose(2, 0, 1)),
        'bmodc': np.ascontiguousarray(b_mod[:, 0:2048].reshape(2, 16, 128).transpose(2, 0, 1)),
        'convwc': np.ascontiguousarray(conv_w.reshape(2, 3, 11, 128).transpose(3, 0, 2, 1)),
        'convbc': np.ascontiguousarray(conv_b.reshape(2, 11, 128).transpose(2, 0, 1)),
        'rows': rows, 'constm': constm, 'constb': constb, 'rope': rope, 'cm': cm,
    }
    x = f(inputs['x'])
    ctx = f(inputs['ctx'])
    maps = []
    for b in range(x.shape[0]):
        m = dict(shared)
        m['x'] = x[b]
        m['ctx'] = ctx[b]
        cc = np.stack([c[b].reshape(8, 128).T, c_ctx.reshape(8, 128).T], axis=-1)
        m['cc'] = np.ascontiguousarray(cc)
        maps.append(m)
    return maps


def kernel(**inputs):
    maps = prep_inputs(inputs)
    nc, _ = build()
    res = run_bass_kernel_spmd(nc, maps, core_ids=list(range(len(maps))))
    return np.stack([np.asarray(r["out"], dtype=np.float32) for r in res.results], axis=0)
```

```python
import math
import numpy as np
import ml_dtypes
import concourse.bass as bass
import concourse.mybir as mybir
from concourse.bass_utils import run_bass_kernel_spmd

F32 = mybir.dt.float32
BF16 = mybir.dt.bfloat16
AF = mybir.ActivationFunctionType
ALU = mybir.AluOpType
AX = mybir.AxisListType
ENGS = ['pe', 'act', 'dve', 'pool', 'sp']
ISZ = {F32: 4, BF16: 2}

T = 2304
NT = 18
D = 1024
EPS = 1e-6
BIG = 30000.0


class Op:
    __slots__ = ('eng', 'fn', 'is_dma', 'eidx', 'gidx', 'waits', 'signal', 'sem', 'val', 'prevdma')


class Prog:
    NDMA = 24

    def __init__(self, nc):
        self.nc = nc
        self.ops = []
        self.eops = {e: [] for e in ENGS}
        self.track = {}
        self.waited = {e: {} for e in ENGS}
        self.dma_waited = {e: set() for e in ENGS}
        self.ndma = 0
        self.dma_last = {}
        self.addr = {}
        self.pool_ok = False
        self.fence_scr = None
        self.nsw = 0
        self.strict = False

    def region(self, ap):
        t = ap.tensor
        name = ap.name
        pairs = ap.ap
        off = int(ap.offset)
        sp = str(ap.space)
        if sp in ('SB', 'PSUM'):
            shp = tuple(t.shape)
            fsz = 1
            for s in shp[1:]:
                fsz *= s
            p0 = off // fsz
            f0 = off % fsz
            pst, pc = pairs[0]
            p1 = p0 + (pc if pst > 0 else 1)
            ext = 0
            for st, c in pairs[1:]:
                ext += abs(st) * (c - 1)
            isz = ISZ.get(ap.dtype, 4)
            if sp == 'SB':
                base = self.addr[name]
                return ('SB', p0, p1, base + f0 * isz, base + (f0 + ext + 1) * isz)
            b0 = (f0 * isz) // 2048
            b1 = ((f0 + ext + 1) * isz + 2047) // 2048
            return (name, (p0 // 32) * 32, ((p1 + 31) // 32) * 32, b0 * 2048, b1 * 2048)
        ext = 0
        for st, c in pairs:
            ext += abs(st) * (c - 1)
        return (name, 0, 1, off, off + ext + 1)

    def add(self, eng, fn, reads, writes, is_dma=False):
        if eng == 'pool' and not is_dma and not self.pool_ok:
            eng = 'dve'
        op = Op()
        op.eng = eng
        op.fn = fn
        op.is_dma = is_dma
        op.signal = False
        op.gidx = len(self.ops)
        op.eidx = len(self.eops[eng])
        op.sem = None
        op.val = 0
        op.prevdma = None
        deps = []
        rregs = [self.region(a) for a in reads]
        wregs = [self.region(a) for a in writes]

        def ov(a, b):
            return a[1] < b[2] and b[1] < a[2] and a[3] < b[4] and b[3] < a[4]

        def cov(a, b):
            return a[1] <= b[1] and a[2] >= b[2] and a[3] <= b[3] and a[4] >= b[4]

        for r in rregs:
            tr = self.track.get(r[0])
            if tr is None:
                continue
            for (wr, wop) in tr[0]:
                if ov(r, wr):
                    deps.append((wop, 'RAW'))
        for w in wregs:
            tr = self.track.get(w[0])
            if tr is None:
                continue
            for (wr, wop) in tr[0]:
                if ov(w, wr):
                    deps.append((wop, 'WAW'))
            for (rr, rop) in tr[1]:
                if ov(w, rr):
                    deps.append((rop, 'WAR'))
        for w in wregs:
            tr = self.track.setdefault(w[0], [[], []])
            tr[0] = [(wr, wop) for (wr, wop) in tr[0] if not cov(w, wr)]
            tr[1] = [(rr, rop) for (rr, rop) in tr[1] if not cov(w, rr)]
            tr[0].append((w, op))
        for r in rregs:
            tr = self.track.setdefault(r[0], [[], []])
            if not is_dma:
                tr[1] = [(rr, rop) for (rr, rop) in tr[1]
                         if not (rop.eng == eng and not rop.is_dma and cov(r, rr))]
            tr[1].append((r, op))
        need = {}
        for (p, kind) in deps:
            if p is op:
                continue
            if p.is_dma:
                if p.gidx in self.dma_waited[eng]:
                    continue
                need[('dma', p.gidx)] = p
            else:
                if p.eng == eng:
                    if eng == 'pe':
                        continue
                    if kind != 'RAW' and not is_dma and not self.strict:
                        continue
                if eng != p.eng and kind == 'RAW' and p.eng in ('dve', 'act') and self.fence_scr is not None:
                    lst = self.eops[p.eng]
                    if p.eidx + 1 >= len(lst):
                        self._fence(p.eng)
                    p = lst[p.eidx + 1]
                if p.eidx <= self.waited[eng].get(p.eng, -1):
                    continue
                cur = need.get(p.eng)
                if cur is None or p.eidx > cur.eidx:
                    need[p.eng] = p
        for k, p in need.items():
            p.signal = True
            if p.is_dma:
                self.dma_waited[eng].add(p.gidx)
            else:
                self.waited[eng][p.eng] = p.eidx
        op.waits = list(need.values())
        if is_dma and eng == 'pool':
            op.sem = ('sw', self.nsw)
            self.nsw += 1
            op.signal = True
        elif is_dma:
            slot = self.ndma % self.NDMA
            self.ndma += 1
            prev = self.dma_last.get(slot)
            if prev is not None and prev.gidx not in self.dma_waited[eng]:
                op.prevdma = prev
                self.dma_waited[eng].add(prev.gidx)
            self.dma_last[slot] = op
            op.sem = slot
            op.signal = True
        op.gidx = len(self.ops)
        op.eidx = len(self.eops[eng])
        self.ops.append(op)
        self.eops[eng].append(op)
        return op

    def _fence(self, eng):
        scr = self.fence_scr
        if eng == 'dve':
            return self.add('dve', lambda e: e.memset(scr[:, 0:1], 0.0), [], [scr[:, 0:1]])
        return self.add('act', lambda e: e.memzero(scr[:, 2:3]), [], [scr[:, 2:3]])

    def mm(self, out, lhsT, rhs, start=True, stop=True):
        rd = [lhsT, rhs] + ([] if start else [out])
        return self.add('pe', lambda e: e.matmul(out, lhsT, rhs, start=start, stop=stop), rd, [out])

    def act(self, out, in_, func, bias=None, scale=1.0, accum_out=None):
        rd = [in_]
        kw = {}
        if bias is not None:
            kw['bias'] = bias
            if not isinstance(bias, (int, float)):
                rd.append(bias)
        if not isinstance(scale, (int, float)):
            rd.append(scale)
        wr = [out]
        if accum_out is not None:
            kw['accum_out'] = accum_out
            wr.append(accum_out)
        return self.add('act', lambda e: e.activation(out, in_, func, scale=scale, **kw), rd, wr)

    def tt(self, eng, out, in0, in1, op):
        return self.add(eng, lambda e: e.tensor_tensor(out, in0, in1, op), [in0, in1], [out])

    def ts(self, eng, out, in0, s1, s2, op0, op1=None):
        rd = [in0]
        if not isinstance(s1, (int, float)):
            rd.append(s1)
        if s2 is not None and not isinstance(s2, (int, float)):
            rd.append(s2)
        if op1 is None:
            return self.add(eng, lambda e: e.tensor_scalar(out, in0, s1, None, op0), rd, [out])
        return self.add(eng, lambda e: e.tensor_scalar(out, in0, s1, s2, op0, op1), rd, [out])

    def stt(self, out, in0, scalar, in1, op0, op1):
        rd = [in0, in1]
        if not isinstance(scalar, (int, float)):
            rd.append(scalar)
        return self.add('dve', lambda e: e.scalar_tensor_tensor(out, in0, scalar, in1, op0, op1), rd, [out])

    def copy(self, eng, out, in_):
        if eng == 'act':
            return self.add(eng, lambda e: e.copy(out, in_), [in_], [out])
        return self.add(eng, lambda e: e.tensor_copy(out, in_), [in_], [out])

    def memset(self, eng, out, val):
        return self.add(eng, lambda e: e.memset(out, val), [], [out])

    def reduce(self, eng, out, in_, op=None):
        op = ALU.add if op is None else op
        return self.add(eng, lambda e: e.tensor_reduce(out, in_, AX.X, op), [in_], [out])

    def recip(self, out, in_):
        return self.add('dve', lambda e: e.reciprocal(out, in_), [in_], [out])

    def dma(self, out, in_, q='sp'):
        return self.add(q, lambda e: e.dma_start(out=out, in_=in_), [in_], [out], is_dma=True)

    def finish(self, aps, q='sp'):
        return self.add(q, None, list(aps), [])

    def emit(self):
        nc = self.nc
        cnt = {e: 0 for e in ENGS}
        dcnt = {}
        for op in self.ops:
            if op.is_dma and isinstance(op.sem, tuple):
                op.val = 16
            elif op.is_dma:
                dcnt[op.sem] = dcnt.get(op.sem, 0) + 16
                op.val = dcnt[op.sem]
            elif op.signal:
                cnt[op.eng] += 1
                op.val = cnt[op.eng]
        self.stats = {e: (len(self.eops[e]), cnt[e]) for e in ENGS}
        esem = {e: nc.alloc_semaphore('s_' + e) for e in ENGS}
        dsem = [nc.alloc_semaphore('s_dma%d' % i) for i in range(self.NDMA)]
        swsem = [nc.alloc_semaphore('s_sw%d' % i) for i in range(self.nsw)]

        def semof(p):
            if p.is_dma:
                return swsem[p.sem[1]] if isinstance(p.sem, tuple) else dsem[p.sem]
            return esem[p.eng]

        def run(ename, e):
            for op in self.eops[ename]:
                if op.prevdma is not None:
                    e.wait_ge(dsem[op.prevdma.sem], op.prevdma.val)
                for p in op.waits:
                    e.wait_ge(semof(p), p.val)
                if op.fn is None:
                    continue
                ins = op.fn(e)
                if op.is_dma:
                    ins.then_inc(semof(op), 16)
                elif op.signal:
                    ins.then_inc(esem[op.eng], 1)

        with nc.Block() as block:
            @block.tensor
            def _(e):
                run('pe', e)

            @block.scalar
            def _(e):
                run('act', e)

            @block.vector
            def _(e):
                run('dve', e)

            @block.gpsimd
            def _(e):
                run('pool', e)

            @block.sync
            def _(e):
                run('sp', e)
                for slot, v in dcnt.items():
                    e.wait_ge(dsem[slot], v)
                for sm in swsem:
                    e.wait_ge(sm, 16)


def bcm(ap, n):
    s = list(ap.shape)
    return ap.unsqueeze(len(s)).broadcast_to(s + [n])


def bch(ap, n):
    s = list(ap.shape)
    return ap.unsqueeze(1).broadcast_to([s[0], n] + s[1:])


C_SSDZ, C_SSDDT, C_GDNZ, C_GDNA, C_GDNB = 1408, 1792, 1804, 2060, 2068
C_AQ, C_AK, C_AV, C_AZ = 2076, 2460, 2588, 2716

(F_ID, F_TRIF, F_TRIB, F_TBF, F_TBB, F_ONES, F_NONES) = range(7)
NF_ = 7
(B_ID, B_NMF, B_NMB, B_NBF, B_NBB, B_SBF, B_SBB, B_BLK, B_TRIF, B_TRIB, B_TBF, B_TBB, B_ONES, B_NONES) = range(14)
B_LV = 14
NB_ = 20


def host_consts():
    t = np.arange(128)
    tri_f = (t[:, None] <= t[None, :]).astype(np.float32)
    tri_b = (t[:, None] >= t[None, :]).astype(np.float32)
    blk = ((t[:, None] // 64) == (t[None, :] // 64)).astype(np.float32)
    m = np.zeros((128, NF_, 128), np.float32)
    m[:, F_ID] = np.eye(128)
    m[:, F_TRIF] = tri_f
    m[:, F_TRIB] = tri_b
    m[:, F_TBF] = tri_f * blk
    m[:, F_TBB] = tri_b * blk
    m[:, F_ONES] = 1.0
    m[:, F_NONES] = -1.0
    mb = np.zeros((128, NB_, 128), np.float32)
    mb[:, B_ID] = np.eye(128)
    mb[:, B_NMF] = (tri_f - 1) * BIG
    mb[:, B_NMB] = (tri_b - 1) * BIG
    mb[:, B_NBF] = (tri_f * blk - 1) * BIG
    mb[:, B_NBB] = (tri_b * blk - 1) * BIG
    mb[:, B_SBF] = (t[:, None] < t[None, :]) * blk
    mb[:, B_SBB] = (t[:, None] > t[None, :]) * blk
    mb[:, B_BLK] = blk
    mb[:, B_TRIF] = tri_f
    mb[:, B_TRIB] = tri_b
    mb[:, B_TBF] = tri_f * blk
    mb[:, B_TBB] = tri_b * blk
    mb[:, B_ONES] = 1.0
    mb[:, B_NONES] = -1.0
    for sl in range(6):
        mb[:, B_LV + sl] = ((t[:, None] >> (sl + 1)) == (t[None, :] >> (sl + 1))) & ((t[:, None] >> sl) != (t[None, :] >> sl))
    nf = 16
    freqs = (10000.0 ** (-np.arange(nf, dtype=np.float32) / nf)).astype(np.float32)
    pos = np.arange(2048)
    ar = (pos // 64).astype(np.float32)[:, None] * freqs
    ac = (pos % 64).astype(np.float32)[:, None] * freqs
    cr, sr, cc, sc = np.cos(ar), np.sin(ar), np.cos(ac), np.sin(ac)
    cosf = np.concatenate([cr, cr, cc, cc], 1).astype(np.float32)
    sinf = np.concatenate([-sr, sr, -sc, sc], 1).astype(np.float32)
    rope = np.stack([cosf, sinf], 1).reshape(16, 128, 2, 64).transpose(1, 0, 2, 3)
    cm = np.zeros((128, 2), np.float32)
    cm[:64, 0] = 1
    cm[64:, 1] = 1
    return np.ascontiguousarray(m), np.ascontiguousarray(mb), np.ascontiguousarray(rope), cm


def build(n_layers=2, stop=None, dbg=()):
    nc = bass.Bass("TRN2", target_bir_lowering=False)
    P = Prog(nc)
    ins = {}

    def din(name, shape, dt=F32):
        ins[name] = nc.dram_tensor(name, list(shape), dt, kind="ExternalInput").ap()
        return ins[name]

    x_d = din("x", [2048, D])
    ctx_d = din("ctx", [256, D])
    cc_d = din("cc", [128, 8, 2])
    wmod_d = din("w_mod", [2, D, 3072])
    bmod_d = din("b_mod", [2, 3072])
    win_d = din("w_in", [2, D, 3100])
    wout_d = din("w_out", [2, D, D])
    normwc_d = din("normwc", [128, 2, 8])
    bmodc_d = din("bmodc", [128, 2, 16])
    convwc_d = din("convwc", [128, 2, 11, 3])
    convbc_d = din("convbc", [128, 2, 11])
    rows_d = din("rows", [2, 1024])
    constm_d = din("constm", [128, NF_, 128])
    constb_d = din("constb", [128, NB_, 128])
    rope_d = din("rope", [128, 16, 2, 64])
    cm_d = din("cm", [128, 2])
    out_d = nc.dram_tensor("out", [2048, D], F32, kind="ExternalOutput").ap()
    x1s_d = nc.dram_tensor("x1s", [2048, D], F32, kind="Internal").ap()
    dbg_out = {}

    from contextlib import ExitStack
    with ExitStack() as es:
        def sb(name, shape, dt):
            h = es.enter_context(nc.sbuf_tensor(name, list(shape), dt))
            P.addr[name] = int(nc.lookup_mloc(h).addr)
            return h

        PS = es.enter_context(nc.psum_tensor("PS", [128, 8, 512], F32))
        bank_ctr = [0]

        NROT = [8]

        def nb():
            b = bank_ctr[0] % NROT[0]
            bank_ctr[0] += 1
            return PS[:, b, :]

        HT = sb("HT", [128, 8, T], BF16)
        MIX = sb("MIX", [128, NT, 1024], BF16)
        XC = sb("XC", [128, 2, D], F32)
        CMF = sb("CMF", [128, NF_, 128], F32)
        CMB = sb("CMB", [128, NB_, 128], BF16)
        WS = [sb("WS0", [128, 8, 512], BF16), sb("WS1", [128, 8, 512], BF16)]
        SM = sb("SM", [128, 512], F32)
        DTR = sb("DTR", [128, NT, 28], F32)
        SDT = sb("SDT", [128, NT, 12], F32)
        SDA = sb("SDA", [128, NT, 12], F32)
        GG = sb("GG", [128, NT, 8], F32)
        GB = sb("GB", [128, NT, 8], F32)
        SDAH = sb("SDAH", [128, NT, 12], BF16)
        SDAL = sb("SDAL", [128, NT, 12], BF16)
        GGH = sb("GGH", [128, NT, 8], BF16)
        GGL = sb("GGL", [128, NT, 8], BF16)
        ROWS = sb("ROWS", [128, 1024], F32)
        MODC = sb("MODC", [128, 16, 2], F32)
        GC = sb("GC", [128, 8, 2], F32)
        SILC = sb("SILC", [128, 8, 2], BF16)
        SILB = sb("SILB", [128, 2, 8, 128], BF16)
        CCF = sb("CCF", [128, 8, 2], F32)
        NWC = sb("NWC", [128, 2, 8], F32)
        BMC = sb("BMC", [128, 2, 16], F32)
        CVW = sb("CVW", [128, 2, 11, 3], F32)
        CVB = sb("CVB", [128, 2, 11], F32)
        CMK = sb("CMK", [128, 2], F32)
        ARENA = sb("ARENA", [128, 22080], F32)
        arena_off = [0]

        def arena_reset():
            arena_off[0] = 0

        def al(shape, dt):
            n = 1
            for s in shape[1:]:
                n *= s
            nbytes = n * ISZ[dt]
            nwords = (nbytes + 3) // 4
            o = arena_off[0]
            assert o + nwords <= 22080, ("arena overflow", o, nwords)
            arena_off[0] = o + nwords
            v = ARENA[:, o:o + nwords]
            if dt != F32:
                v = v.bitcast(dt)[:, 0:n]
            if len(shape) == 2:
                return v
            names = ' '.join('a%d' % i for i in range(len(shape) - 1))
            kw = {'a%d' % i: shape[i + 1] for i in range(len(shape) - 2)}
            return v.rearrange("p (%s) -> p %s" % (names, names), **kw)

        def cmf(i):
            return CMF[:, i, :]

        def cmb(i):
            return CMB[:, i, :]

        IDB = cmb(B_ID)
        IDF = cmf(F_ID)
        ONESB = cmb(B_ONES)
        NONESB = cmb(B_NONES)

        def dump(name, ap):
            if name not in dbg:
                return
            d = nc.dram_tensor("dbg_" + name, list(ap.shape), ap.dtype, kind="ExternalOutput").ap()
            dbg_out[name] = d
            P.dma(d, ap)

        P.dma(CMF[:], constm_d)
        P.dma(CMB[:], constb_d, q='pool')
        P.dma(CCF[:], cc_d)
        P.dma(NWC[:], normwc_d)
        P.dma(BMC[:], bmodc_d)
        P.dma(CVW[:], convwc_d)
        P.dma(CVB[:], convbc_d)
        P.dma(CMK[:], cm_d)
        P.dma(XC[:], ctx_d.rearrange("(t p) d -> p t d", p=128))
        EPSC = SM[:, 0:1]
        P.memset('dve', EPSC, EPS)
        P.fence_scr = SM[:, 208:216]
        P.act(SILC[:], CCF[:], AF.Silu)
        for v in range(2):
            P.copy('dve', SILB[:, v, :, :], bcm(SILC[:, :, v], 128))

        def softplus(out, in_, tmp1, tmp2, eng='dve'):
            P.act(tmp1, in_, AF.Abs)
            P.act(tmp1, tmp1, AF.Exp, scale=-1.0)
            P.ts(eng, tmp1, tmp1, 1.0, None, ALU.add)
            P.act(tmp1, tmp1, AF.Ln)
            P.ts(eng, tmp2, in_, 0.0, None, ALU.max)
            P.tt(eng, out, tmp1, tmp2, ALU.add)

        import os as _os
        stop_in = stop
        stop_l = int(_os.environ.get('STOP_L', '0'))
        for l in range(n_layers):
            stop = stop_in if l == stop_l else None
            last = (l == n_layers - 1)
            xsrc = x_d if l == 0 else x1s_d
            xdst = out_d if last else x1s_d
            arena_reset()
            P.dma(ROWS[:], rows_d[l].partition_broadcast(128))
            R_DTB = ROWS[:, 0:12]
            R_AL = ROWS[:, 12:24]
            R_D = ROWS[:, 24:30]
            R_GAL = ROWS[:, 32:40]
            R_GDB = ROWS[:, 40:48]
            R_GNW = ROWS[:, 64:128]
            R_QNW = ROWS[:, 128:192]
            R_KNW = ROWS[:, 192:256]
            R_SNW = ROWS[:, 256:640]
            RA = SM[:, 8:20]
            RGA = SM[:, 20:28]
            P.act(RA, R_AL, AF.Exp)
            P.ts('dve', RA, RA, -1.0, None, ALU.mult)
            P.act(RGA, R_GAL, AF.Exp)
            P.ts('dve', RGA, RGA, -1.0, None, ALU.mult)

            GATE = al([128, 2, 1024], F32)
            mark_gate = arena_off[0]
            BG = al([128, 1024], F32)
            P.dma(BG, bmod_d[l, 2048:3072].partition_broadcast(128))
            wmod_v = wmod_d[l].rearrange("(k p) n -> p k n", p=128)
            for blk in range(6):
                ws = WS[blk % 2]
                P.dma(ws[:, :, :], wmod_v[:, :, blk * 512:(blk + 1) * 512], q='pool')
                if blk < 4:
                    pb = nb()
                    for j in range(4):
                        for k in range(8):
                            P.mm(pb[:, j * 2:(j + 1) * 2], ws[:, k, j * 128:(j + 1) * 128], SILC[:, k, :],
                                 start=(k == 0), stop=(k == 7))
                    P.tt('dve', MODC[:, blk * 4:(blk + 1) * 4, :],
                         pb[:, 0:8].rearrange("p (j v) -> p j v", v=2),
                         bcm(BMC[:, l, blk * 4:(blk + 1) * 4], 2), ALU.add)
                else:
                    for v in range(2):
                        pb = nb()
                        for k in range(8):
                            P.mm(pb[:, :], SILB[:, v, k, :], ws[:, k, :], start=(k == 0), stop=(k == 7))
                        P.tt('dve', GATE[:, v, (blk - 4) * 512:(blk - 3) * 512], pb[:, :],
                             BG[:, (blk - 4) * 512:(blk - 3) * 512], ALU.add)
            P.ts('dve', GC[:], MODC[:, 8:16, :], 1.0, None, ALU.add)
            P.tt('dve', GC[:], GC[:], bcm(NWC[:, l, :], 2), ALU.mult)
            dump("L%d_gc" % l, GC[:])
            dump("L%d_modc" % l, MODC[:])
            dump("L%d_gate" % l, GATE)

            arena_off[0] = mark_gate
            mark = arena_off[0]
            XIN = [al([128, 4, D], F32), al([128, 4, D], F32)]
            XN = [al([128, 4, D], BF16), al([128, 4, D], BF16)]
            JUNK = al([128, D], BF16)
            SS = SM[:, 32:50]
            RS = SM[:, 64:82]
            groups = [([0, 1], 1)] + [([2 + 4 * g + j for j in range(4)], 0) for g in range(4)]
            for gi, (tiles, v) in enumerate(groups):
                n = len(tiles)
                t0 = tiles[0]
                slot = gi % 2
                if v == 1:
                    xin = XC
                else:
                    xin = XIN[slot]
                    r0 = (t0 - 2) * 128
                    P.dma(xin[:, :, :], xsrc[r0:r0 + 512, :].rearrange("(t p) d -> p t d", p=128))
                for j, t in enumerate(tiles):
                    P.act(JUNK, xin[:, j, :], AF.Square, accum_out=SS[:, t:t + 1])
                P.ts('dve', RS[:, t0:t0 + n], SS[:, t0:t0 + n], 1.0 / D, EPS, ALU.mult, ALU.add)
                P.act(RS[:, t0:t0 + n], RS[:, t0:t0 + n], AF.Sqrt)
                P.recip(RS[:, t0:t0 + n], RS[:, t0:t0 + n])
                for j, t in enumerate(tiles):
                    P.ts('dve' if j % 2 == 0 else 'pool', XN[slot][:, j, :], xin[:, j, :], RS[:, t:t + 1], None, ALU.mult)
                for k in range(8):
                    pb = nb()
                    for j in range(n):
                        P.mm(pb[:, j * 128:(j + 1) * 128], XN[slot][:, j, k * 128:(k + 1) * 128], IDB)
                    dst = HT[:, k, t0 * 128:(t0 + n) * 128]
                    if k % 2 == 0:
                        P.ts('dve', dst, pb[:, 0:n * 128], GC[:, k, v:v + 1], MODC[:, k, v:v + 1], ALU.mult, ALU.add)
                    else:
                        P.act(dst, pb[:, 0:n * 128], AF.Identity, bias=MODC[:, k, v:v + 1], scale=GC[:, k, v:v + 1])
            dump("L%d_ht" % l, HT[:])
            arena_off[0] = mark
            if stop == 'p1':
                break

            win_v = win_d[l].rearrange("(k p) n -> p k n", p=128)
            WSM = al([128, 8, 28], BF16)
            P.dma(WSM[:, :, 0:12], win_v[:, :, C_SSDDT:C_SSDDT + 12], q='pool')
            P.dma(WSM[:, :, 12:28], win_v[:, :, C_GDNA:C_GDNA + 16], q='pool')
            for g0 in (0, 16):
                tiles = list(range(g0, min(g0 + 16, NT)))
                pb = nb()
                for j, t in enumerate(tiles):
                    for k in range(8):
                        P.mm(pb[:, j * 28:(j + 1) * 28], HT[:, k, t * 128:(t + 1) * 128], WSM[:, k, :],
                             start=(k == 0), stop=(k == 7))
                n = len(tiles)
                P.copy('dve', DTR[:, g0:g0 + n, :], pb[:, 0:n * 28].rearrange("p (t c) -> p t c", c=28))
            TMPA = al([128, NT, 12], F32)
            TMPB = al([128, NT, 12], F32)
            P.tt('dve', SDT[:], DTR[:, :, 0:12], bch(R_DTB, NT), ALU.add)
            softplus(SDT[:], SDT[:], TMPA, TMPB)
            P.tt('dve', SDA[:], SDT[:], bch(RA, NT), ALU.mult)
            P.copy('dve', SDAH[:], SDA[:])
            P.tt('dve', SDAL[:], SDA[:], SDAH[:], ALU.subtract)
            P.tt('dve', GG[:], DTR[:, :, 12:20], bch(R_GDB, NT), ALU.add)
            softplus(GG[:], GG[:], TMPA[:, :, 0:8], TMPB[:, :, 0:8])
            P.tt('dve', GG[:], GG[:], bch(RGA, NT), ALU.mult)
            P.act(GB[:], DTR[:, :, 20:28], AF.Sigmoid)
            P.copy('dve', GGH[:], GG[:])
            P.tt('dve', GGL[:], GG[:], GGH[:], ALU.subtract)
            dump("L%d_sdt" % l, SDT[:])
            dump("L%d_gg" % l, GG[:])
            dump("L%d_gb" % l, GB[:])
            dump("L%d_dtr" % l, DTR[:])
            if stop == 'dt':
                break

            mark_mix = arena_off[0]
            PREB = al([128, T + 4], BF16)
            DG = [al([128, 3, 128], BF16), al([128, 3, 128], BF16)]
            P.memset('pool', PREB[:, 0:1], 0.0)
            P.memset('pool', PREB[:, 257:259], 0.0)
            P.memset('pool', PREB[:, T + 3:T + 4], 0.0)
            TBK = [(0, 256)] + [(256 + 512 * i, 512) for i in range(4)]
            if stop == 'c0':
                dump("L%d_preb" % l, PREB)
                break
            cctr = [0]
            wsslot = [0]
            wsbase = [0]

            def conv_proj(ch, dest):
                dg = DG[cctr[0] % 2]
                cctr[0] += 1
                grp = {0: (0, 4), 4: (4, 5), 5: (5, 9), 9: (9, 11)}
                if ch in grp:
                    c0, c1 = grp[ch]
                    wsslot[0] = (wsslot[0] + 1) % 2
                    wsbase[0] = c0
                    P.dma(WS[wsslot[0]][:, :, 0:(c1 - c0) * 128], win_v[:, :, c0 * 128:c1 * 128], q='pool')
                ws = WS[wsslot[0]][:, :, (ch - wsbase[0]) * 128:(ch - wsbase[0] + 1) * 128]
                for k in range(3):
                    P.ts('pool', dg[:, k, :], IDF, CVW[:, l, ch, k:k + 1], None, ALU.mult)
                for bi, (t0, n) in enumerate(TBK):
                    pb = nb()
                    for k in range(8):
                        P.mm(pb[:, 0:n], ws[:, k, 0:128], HT[:, k, t0:t0 + n], start=(k == 0), stop=(k == 7))
                    po = t0 + 1 if t0 < 256 else t0 + 3
                    if bi % 2 == 0:
                        P.copy('dve', PREB[:, po:po + n], pb[:, 0:n])
                    else:
                        P.copy('act', PREB[:, po:po + n], pb[:, 0:n])
                if stop == 'c1':
                    return
                for bi, (t0, n) in enumerate(TBK):
                    po = t0 + 1 if t0 < 256 else t0 + 3
                    pb2 = nb()
                    for k in range(3):
                        P.mm(pb2[:, 0:n], dg[:, k, :], PREB[:, po - 1 + k:po - 1 + k + n], start=(k == 0), stop=(k == 2))
                    P.act(dest[:, t0:t0 + n], pb2[:, 0:n], AF.Silu, bias=CVB[:, l, ch:ch + 1])

            def to_tok(src, CTt, c0):
                for gi, g0 in enumerate(range(0, NT, 4)):
                    tiles = list(range(g0, min(g0 + 4, NT)))
                    n = len(tiles)
                    pb = nb()
                    for j, t in enumerate(tiles):
                        P.mm(pb[:, j * 128:(j + 1) * 128], src[:, t * 128:(t + 1) * 128], IDB)
                    P.copy('dve' if gi % 2 == 0 else 'act', CTt[:, g0:g0 + n, c0:c0 + 128],
                           pb[:, 0:n * 128].rearrange("p (t c) -> p t c", c=128))

            mark_ssd = arena_off[0]
            CTS = al([128, NT, 512], BF16)
            CFB = al([128, T], BF16)
            CFC = al([128, T], BF16)
            XF = [al([128, T], BF16), al([128, T], BF16)]
            if stop in ('c1', 'c2'):
                conv_proj(0, XF[0])
                dump("L%d_preb" % l, PREB)
                dump("L%d_xf" % l, XF[0])
                if stop == 'c2':
                    to_tok(XF[0], CTS, 0)
                    dump("L%d_cts" % l, CTS)
                break
            for ch in range(3):
                conv_proj(ch, XF[ch % 2])
                to_tok(XF[ch % 2], CTS, ch * 128)
            conv_proj(3, CFB)
            to_tok(CFB, CTS, 384)
            conv_proj(4, CFC)
            dump("L%d_cts" % l, CTS)
            dump("L%d_cfc" % l, CFC)
            if stop == 'conv':
                break
            RALH = [al([128, 6, 128], BF16), al([128, 6, 128], BF16)]
            RALL = [al([128, 6, 128], BF16), al([128, 6, 128], BF16)]
            EE = [al([128, 6, 128], F32), al([128, 6, 128], F32)]
            WTT = [al([128, 6, 128], BF16), al([128, 6, 128], BF16)]
            XS = [al([128, 6, 64], BF16), al([128, 6, 64], BF16)]
            TMPY = al([128, 6, 64], F32)
            TY2 = al([128, 384], F32)
            TY3 = al([128, 384], F32)
            SST = [al([128, 384], F32), al([128, 384], F32)]
            SBF = [al([128, 384], BF16), al([128, 384], BF16)]
            NCUM = SM[:, 96:102]
            ECUM = SM[:, 104:110]
            SD = SM[:, 112:118]
            CD = SM[:, 120:126]
            DFULL = al([128, 6, 64], F32)
            P.copy('dve', DFULL, bcm(R_D, 64))
            sctr = [0]
            for d in range(2):
                P.memset('dve', SST[d], 0.0)
                P.memset('pool', SBF[d], 0.0)
                order = list(range(NT)) if d == 0 else [1, 0] + list(range(NT - 1, 1, -1))
                import os
                order = order[:int(os.environ.get('SSD_N', '99'))]
                TRI = cmb(B_TRIF if d == 0 else B_TRIB)
                NMK = cmb(B_NMF if d == 0 else B_NMB)
                lastc = 127 if d == 0 else 0
                for t in order:
                    par = sctr[0] % 2
                    sctr[0] += 1
                    tok = slice(t * 128, (t + 1) * 128)
                    need_y = not (last and t < 2)
                    a6h = SDAH[:, t, d * 6:(d + 1) * 6]
                    a6l = SDAL[:, t, d * 6:(d + 1) * 6]
                    dt6 = SDT[:, t, d * 6:(d + 1) * 6]
                    rah, ral, ee, wt, xs = RALH[par], RALL[par], EE[par], WTT[par], XS[par]
                    P.tt('pool', rah, bch(TRI, 6), bcm(a6h, 128), ALU.mult)
                    P.tt('pool', ral, bch(TRI, 6), bcm(a6l, 128), ALU.mult)
                    pcol = nb()
                    P.mm(pcol[:, 0:6], TRI, a6h, start=True, stop=False)
                    P.mm(pcol[:, 0:6], TRI, a6l, start=False, stop=True)
                    P.ts('dve', NCUM, pcol[:, 0:6], -1.0, None, ALU.mult)
                    P.act(ECUM, pcol[:, 0:6], AF.Exp)
                    if stop == 's1':
                        break
                    pA = nb()
                    pB = nb()
                    dsts = []
                    for h in range(6):
                        dst = (pA if h < 4 else pB)[:, (h % 4) * 128:(h % 4 + 1) * 128]
                        dsts.append(dst)
                        P.mm(dst, ONESB, rah[:, h, :], start=True, stop=False)
                        P.mm(dst, ONESB, ral[:, h, :], start=False, stop=False)
                        P.mm(dst, IDB, NMK, start=False, stop=True)
                    for h in range(6):
                        P.act(ee[:, h, :], dsts[h], AF.Exp, bias=NCUM[:, h:h + 1])
                    if stop == 's2':
                        break
                    P.mm(pcol[:, 8:14], ONESB, a6h, start=True, stop=False)
                    P.mm(pcol[:, 8:14], ONESB, a6l, start=False, stop=True)
                    P.tt('dve', SD, pcol[:, 8:14], NCUM, ALU.add)
                    P.act(SD, SD, AF.Exp)
                    P.tt('dve', SD, SD, dt6, ALU.mult)
                    P.act(CD, pcol[:, 8:14], AF.Exp)
                    if stop == 's3b':
                        break
                    P.tt('pool', xs, CTS[:, t, 0:384].rearrange("p (h f) -> p h f", h=6), bcm(SD, 64), ALU.mult)
                    if stop == 's3':
                        break
                    if need_y:
                        psc = [nb(), nb()]
                        for g in range(2):
                            P.mm(psc[g][:, 0:128], CFB[g * 64:(g + 1) * 64, tok], CFC[g * 64:(g + 1) * 64, tok])
                        for h in range(6):
                            g = h // 3
                            P.stt(wt[:, h, :], psc[g][:, 0:128], dt6[:, h:h + 1], ee[:, h, :], ALU.mult, ALU.mult)
                        if stop == 'y1':
                            break
                        py = nb()
                        poff = [nb(), nb()]
                        for h in range(6):
                            P.mm(py[:, h * 64:(h + 1) * 64], wt[:, h, :], CTS[:, t, h * 64:(h + 1) * 64])
                        for g in range(2):
                            P.mm(poff[g][:, 0:192], CFC[g * 64:(g + 1) * 64, tok],
                                 SBF[d][g * 64:(g + 1) * 64, g * 192:(g + 1) * 192])
                        if stop == 'y2':
                            break
                        for g in range(2):
                            P.tt('dve', TMPY[:, 3 * g:3 * g + 3, :], poff[g][:, 0:192].rearrange("p (h f) -> p h f", h=3),
                                 bcm(ECUM[:, 3 * g:3 * g + 3], 64), ALU.mult)
                        P.tt('dve', TY2, py[:, 0:384], TMPY.rearrange("p h f -> p (h f)"), ALU.add)
                        if d == 0:
                            P.tt('pool', TY3, CTS[:, t, 0:384], DFULL.rearrange("p h f -> p (h f)"), ALU.mult)
                            P.tt('pool', MIX[:, t, 0:384], TY2, TY3, ALU.add)
                        else:
                            P.tt('pool', MIX[:, t, 0:384], TY2, MIX[:, t, 0:384], ALU.add)
                    if stop == 's4':
                        break
                    pst = nb()
                    P.mm(pst[:, 0:384], CTS[:, t, 384:512], xs.rearrange("p h f -> p (h f)"))
                    P.tt('pool', SST[d].rearrange("p (h f) -> p h f", h=6), SST[d].rearrange("p (h f) -> p h f", h=6),
                         bcm(CD, 64), ALU.mult)
                    P.tt('dve', SST[d], SST[d], pst[:, 0:384], ALU.add)
                    P.copy('pool', SBF[d], SST[d])
                    if stop == 's5':
                        break
            dump("L%d_mixs" % l, MIX[:, :, 0:384])
            dump("L%d_mix" % l, MIX[:])
            arena_off[0] = mark_ssd
            if stop in ('ssd', 's1', 's2', 's3', 's4', 's5', 's3a', 's3b', 'y1', 'y2'):
                break

            def nb2():
                if bank_ctr[0] % 2 == 1:
                    bank_ctr[0] += 1
                b = bank_ctr[0] % NROT[0]
                bank_ctr[0] += 2
                return PS[:, b:b + 2, :]

            hpi = lambda h: (h % 2) * 2 + h // 2
            GBP = al([128, NT, 8], F32)
            GHP = al([128, NT, 8], BF16)
            GLP = al([128, NT, 8], BF16)
            for dd in range(2):
                for h in range(4):
                    P.copy('dve', GBP[:, :, dd * 4 + hpi(h)], GB[:, :, dd * 4 + h])
                    P.copy('dve', GHP[:, :, dd * 4 + hpi(h)], GGH[:, :, dd * 4 + h])
                    P.copy('dve', GLP[:, :, dd * 4 + hpi(h)], GGL[:, :, dd * 4 + h])
            CFQK = al([128, 4, T], BF16)
            CTG = al([128, NT, 512], BF16)
            mark_gcore = arena_off[0]
            XFG = [al([128, T], BF16), al([128, T], BF16)]
            SQ = al([128, 512], BF16)
            RN = al([128, 512], F32)
            for ci in range(4):
                xf = XFG[ci % 2]
                conv_proj(5 + ci, xf)
                for (t0, n) in TBK:
                    P.tt('dve', SQ[:, 0:n], xf[:, t0:t0 + n], xf[:, t0:t0 + n], ALU.mult)
                    pb = nb()
                    P.mm(pb[:, 0:n], cmb(B_BLK), SQ[:, 0:n])
                    P.act(RN[:, 0:n], pb[:, 0:n], AF.Sqrt, bias=EPSC)
                    P.recip(RN[:, 0:n], RN[:, 0:n])
                    if ci < 2:
                        P.stt(CFQK[:, ci, t0:t0 + n], xf[:, t0:t0 + n], 0.125, RN[:, 0:n], ALU.mult, ALU.mult)
                    else:
                        P.tt('dve', CFQK[:, ci, t0:t0 + n], xf[:, t0:t0 + n], RN[:, 0:n], ALU.mult)
                if ci >= 2:
                    to_tok(CFQK[:, ci, :], CTG, (ci - 2) * 128)
            for ci in range(2):
                xf = XFG[ci % 2]
                conv_proj(9 + ci, xf)
                to_tok(xf, CTG, 256 + ci * 128)
            dump("L%d_cfqk" % l, CFQK)
            dump("L%d_ctg" % l, CTG)
            if stop == 'gconv':
                break
            arena_off[0] = mark_gcore
            RGH = al([128, 4, 128], BF16)
            RGL = al([128, 4, 128], BF16)
            GMH = al([128, 4, 2], BF16)
            GML = al([128, 4, 2], BF16)
            EG = al([128, 4, 128], F32)
            ES = al([128, 4, 128], F32)
            T1 = al([128, 4, 128], F32)
            XX = [al([128, 4, 128], BF16), al([128, 4, 128], BF16)]
            XXT = [al([128, 4, 128], BF16), al([128, 4, 128], BF16)]
            WW = [al([128, 4, 128], BF16), al([128, 4, 128], BF16)]
            WWT = [al([128, 4, 128], BF16), al([128, 4, 128], BF16)]
            CTS_ = al([128, 4, 128], BF16)
            CS_ = al([128, 4, 128], BF16)
            YY = al([128, 4, 128], BF16)
            YYT = al([128, 4, 128], BF16)
            ITT = al([128, 4, 128], BF16)
            UU = al([128, 256], F32)
            KG = al([128, 4, 64], BF16)
            KD = al([128, 4, 64], BF16)
            WTG = al([128, 2, 128], BF16)
            VN = al([128, 256], BF16)
            OA = al([128, 256], F32)
            OA2 = al([128, 256], F32)
            SG = [al([128, 128], F32), al([128, 128], F32)]
            SGB = [al([128, 128], BF16), al([128, 128], BF16)]
            NGC = SM[:, 128:132]
            EGC = SM[:, 136:140]
            F1 = SM[:, 144:148]
            CDF = SM[:, 152:156].rearrange("p (c hh) -> p c hh", c=2)
            for d in range(2):
                P.memset('dve', SG[d], 0.0)
                P.memset('dve', SGB[d], 0.0)
                order = list(range(NT)) if d == 0 else [1, 0] + list(range(NT - 1, 1, -1))
                import os
                order = order[:int(os.environ.get('GDN_N', '99'))]
                TB = cmb(B_TBF if d == 0 else B_TBB)
                NMK = cmb(B_NBF if d == 0 else B_NBB)
                SMK = cmb(B_SBF if d == 0 else B_SBB)
                for t in order:
                    tok = slice(t * 128, (t + 1) * 128)
                    need_o = not (last and t < 2)
                    gh = GHP[:, t, d * 4:(d + 1) * 4]
                    gl = GLP[:, t, d * 4:(d + 1) * 4]
                    bp = GBP[:, t, d * 4:(d + 1) * 4]
                    P.tt('dve', RGH, bch(TB, 4), bcm(gh, 128), ALU.mult)
                    P.tt('dve', RGL, bch(TB, 4), bcm(gl, 128), ALU.mult)
                    P.tt('dve', GMH, bcm(gh, 2), bch(CMK[:], 4), ALU.mult)
                    P.tt('dve', GML, bcm(gl, 2), bch(CMK[:], 4), ALU.mult)
                    pcol = nb()
                    P.mm(pcol[:, 0:4], TB, gh, start=True, stop=False)
                    P.mm(pcol[:, 0:4], TB, gl, start=False, stop=True)
                    P.mm(pcol[:, 8:16], ONESB, GMH.rearrange("p h c -> p (h c)"), start=True, stop=False)
                    P.mm(pcol[:, 8:16], ONESB, GML.rearrange("p h c -> p (h c)"), start=False, stop=True)
                    P.ts('dve', NGC, pcol[:, 0:4], -1.0, None, ALU.mult)
                    P.act(EGC, pcol[:, 0:4], AF.Exp)
                    pcv = pcol[:, 8:16].rearrange("p (hl hh c) -> p hl hh c", hl=2, hh=2)
                    for hl in range(2):
                        for c in range(2):
                            P.act(CDF[hl * 64:(hl + 1) * 64, c, :], pcv[hl * 64:(hl + 1) * 64, hl, :, c], AF.Exp)
                    pd = nb()
                    for hp in range(4):
                        dst = pd[:, hp * 128:(hp + 1) * 128]
                        P.mm(dst, ONESB, RGH[:, hp, :], start=True, stop=False)
                        P.mm(dst, ONESB, RGL[:, hp, :], start=False, stop=False)
                        P.mm(dst, IDB, NMK, start=False, stop=True)
                    for hp in range(4):
                        P.act(EG[:, hp, :], pd[:, hp * 128:(hp + 1) * 128], AF.Exp, bias=NGC[:, hp:hp + 1])
                    pkk = nb2()
                    pqk = nb2()
                    for h in range(4):
                        hl, hh = h % 2, h // 2
                        kf = CFQK[hl * 64:(hl + 1) * 64, 2 + hh, tok]
                        qf = CFQK[hl * 64:(hl + 1) * 64, hh, tok]
                        P.mm(pkk[:, hl, hh * 128:(hh + 1) * 128], kf, kf)
                        P.mm(pqk[:, hl, hh * 128:(hh + 1) * 128], kf, qf)
                    P.tt('dve', ES, EG, bch(SMK, 4), ALU.mult)
                    for hl in range(2):
                        P.tt('dve', T1[:, hl * 2:(hl + 1) * 2, :], pkk[:, hl, 0:256].rearrange("p (b i) -> p b i", b=2),
                             bcm(bp[:, hl * 2:(hl + 1) * 2], 128), ALU.mult)
                    P.tt('dve', XX[0], T1, ES, ALU.mult)
                    for hl in range(2):
                        P.tt('dve', T1[:, hl * 2:(hl + 1) * 2, :], pqk[:, hl, 0:256].rearrange("p (b i) -> p b i", b=2),
                             bcm(bp[:, hl * 2:(hl + 1) * 2], 128), ALU.mult)
                    P.tt('dve', ITT, T1, EG, ALU.mult)
                    pt = nb()
                    for hp in range(4):
                        P.mm(pt[:, hp * 128:(hp + 1) * 128], XX[0][:, hp, :], IDB)
                    P.copy('act', XXT[0].rearrange("p h i -> p (h i)"), pt[:, :])
                    MPm, LPm = XX[0], XXT[0]
                    Wc = [WW[0], WW[1]]
                    Wtc = [WWT[0], WWT[1]]
                    m0 = cmb(B_LV)
                    P.tt('dve', T1, LPm, bch(m0, 4), ALU.mult)
                    P.stt(Wc[0], T1, -1.0, bch(IDB, 4), ALU.mult, ALU.add)
                    P.tt('dve', T1, MPm, bch(m0, 4), ALU.mult)
                    P.stt(Wtc[0], T1, -1.0, bch(IDB, 4), ALU.mult, ALU.add)
                    cur = 0
                    for lev in range(1, 6):
                        ml = cmb(B_LV + lev)
                        P.tt('dve', CTS_, MPm, bch(ml, 4), ALU.mult)
                        P.tt('dve', CS_, LPm, bch(ml, 4), ALU.mult)
                        p1 = nb()
                        for hp in range(4):
                            P.mm(p1[:, hp * 128:(hp + 1) * 128], CTS_[:, hp, :], Wc[cur][:, hp, :])
                        p2 = nb()
                        for hp in range(4):
                            P.mm(p2[:, hp * 128:(hp + 1) * 128], CS_[:, hp, :], Wtc[cur][:, hp, :])
                        P.copy('act', YY.rearrange("p h i -> p (h i)"), p1[:, :])
                        P.copy('dve', YYT.rearrange("p h i -> p (h i)"), p2[:, :])
                        p3 = nb()
                        for hp in range(4):
                            P.mm(p3[:, hp * 128:(hp + 1) * 128], Wtc[cur][:, hp, :], YY[:, hp, :])
                        p4 = nb()
                        for hp in range(4):
                            P.mm(p4[:, hp * 128:(hp + 1) * 128], Wc[cur][:, hp, :], YYT[:, hp, :])
                        P.tt('dve', Wc[1 - cur], Wc[cur], p3.rearrange("p (h i) -> p h i", h=4), ALU.subtract)
                        P.tt('dve', Wtc[1 - cur], Wtc[cur], p4.rearrange("p (h i) -> p h i", h=4), ALU.subtract)
                        cur = 1 - cur
                    PM = Wtc[cur]
                    pu = nb()
                    for h in range(4):
                        hp = hpi(h)
                        P.mm(pu[:, hp * 64:(hp + 1) * 64], PM[:, hp, :], CTG[:, t, 256 + h * 64:256 + (h + 1) * 64])
                    P.copy('dve', UU, pu[:, 0:256])
                    kv = CTG[:, t, 0:256].rearrange("p (hh hl f) -> p hh hl f", hh=2, hl=2)
                    for hl in range(2):
                        P.tt('dve', KG[:, hl * 2:(hl + 1) * 2, :], kv[:, :, hl, :], bcm(EGC[:, hl * 2:(hl + 1) * 2], 64), ALU.mult)
                    pw = nb()
                    for hp in range(4):
                        hl, hh = hp // 2, hp % 2
                        P.mm(pw[hl * 64:(hl + 1) * 64, hh * 128:(hh + 1) * 128], KG[:, hp, :], PM[:, hp, :])
                    P.copy('act', WTG.rearrange("p h i -> p (h i)"), pw[:, 0:256])
                    for c in range(2):
                        rows = slice(c * 64, (c + 1) * 64)
                        lastcol = c * 64 + (63 if d == 0 else 0)
                        P.tt('dve', F1[rows, :], EG[rows, :, lastcol], bp[rows, :], ALU.mult)
                    for hl in range(2):
                        P.tt('dve', KD[:, hl * 2:(hl + 1) * 2, :], kv[:, :, hl, :], bcm(F1[:, hl * 2:(hl + 1) * 2], 64), ALU.mult)
                    for c in ([0, 1] if d == 0 else [1, 0]):
                        rows = slice(c * 64, (c + 1) * 64)
                        ctok = slice(t * 128 + c * 64, t * 128 + (c + 1) * 64)
                        pr = nb2()
                        for hp in range(4):
                            hl, hh = hp // 2, hp % 2
                            sblk = SGB[d][hl * 64:(hl + 1) * 64, hh * 64:(hh + 1) * 64]
                            P.mm(pr[rows, hl, hh * 64:(hh + 1) * 64], WTG[hl * 64:(hl + 1) * 64, hh, c * 64:(c + 1) * 64], sblk)
                            if need_o:
                                P.mm(pr[rows, hl, 128 + hh * 64:128 + (hh + 1) * 64], CFQK[hl * 64:(hl + 1) * 64, hh, ctok], sblk)
                        P.tt('dve', VN[rows, :].rearrange("p (a b) -> p a b", a=2), UU[rows, :].rearrange("p (a b) -> p a b", a=2),
                             pr[rows, :, 0:128], ALU.subtract)
                        psu = nb()
                        for hp in range(4):
                            hl, hh = hp // 2, hp % 2
                            P.mm(psu[hl * 64:(hl + 1) * 64, hh * 64:(hh + 1) * 64], KD[rows, hp, :], VN[rows, hp * 64:(hp + 1) * 64])
                        P.tt('dve', SG[d].rearrange("p (a b) -> p a b", a=2), SG[d].rearrange("p (a b) -> p a b", a=2),
                             bcm(CDF[:, c, :], 64), ALU.mult)
                        P.tt('dve', SG[d], SG[d], psu[:, 0:128], ALU.add)
                        P.copy('act', SGB[d], SG[d])
                        if need_o:
                            for hl in range(2):
                                P.tt('dve', OA[rows, hl * 128:(hl + 1) * 128].rearrange("p (a b) -> p a b", a=2),
                                     pr[rows, hl, 128:256].rearrange("p (a b) -> p a b", a=2),
                                     bcm(EGC[rows, hl * 2:(hl + 1) * 2], 64), ALU.mult)
                            po = nb()
                            for hp in range(4):
                                P.mm(po[rows, hp * 64:(hp + 1) * 64], ITT[rows, hp, c * 64:(c + 1) * 64], VN[rows, hp * 64:(hp + 1) * 64])
                            P.tt('dve', OA2[rows, :], OA[rows, :], po[rows, 0:256], ALU.add)
                            mv = MIX[rows, t, 384:640].rearrange("p (hh hl f) -> p hh hl f", hh=2, hl=2)
                            for hl in range(2):
                                src = OA2[rows, hl * 128:(hl + 1) * 128].rearrange("p (a b) -> p a b", a=2)
                                if d == 0:
                                    P.copy('dve', mv[:, :, hl, :], src)
                                else:
                                    P.tt('dve', mv[:, :, hl, :], mv[:, :, hl, :], src, ALU.add)
            dump("L%d_mixg" % l, MIX[:, :, 0:640])
            arena_off[0] = mark_ssd
            if stop == 'gdn':
                break

            arena_off[0] = mark_mix
            ROPE = al([128, 16, 2, 64], F32)
            P.dma(ROPE, rope_d)
            QKT = al([128, 4, T], BF16)
            VA = al([128, NT, 2, 65], BF16)
            NWB = al([128, 512], F32)
            SQF = al([128, 512], F32)
            QN = al([128, 512], F32)
            T1R = al([128, 512], F32)
            T2R = al([128, 512], F32)
            SINR = al([128, 512], F32)
            DST = al([128, 512], BF16)
            SSQ = SM[:, 160:168]
            P.dma(WS[0][:, :, 0:512], win_v[:, :, C_AQ:C_AQ + 512], q='pool')
            P.dma(WS[1][:, :, 0:128], win_v[:, :, C_AV:C_AV + 128], q='pool')
            P.ts('dve', NWB[:, 0:384].rearrange("p (h f) -> p h f", h=6), bch(R_QNW, 6), 0.125, None, ALU.mult)
            P.copy('dve', NWB[:, 384:512].rearrange("p (h f) -> p h f", h=2), bch(R_KNW, 2))
            P.memset('dve', VA[:, :, :, 64:65], 1.0)
            for t in range(NT):
                tok = slice(t * 128, (t + 1) * 128)
                pa = nb()
                pv = nb()
                for k in range(8):
                    P.mm(pa[:, 0:512], HT[:, k, tok], WS[0][:, k, 0:512], start=(k == 0), stop=(k == 7))
                for k in range(8):
                    P.mm(pv[:, 0:128], HT[:, k, tok], WS[1][:, k, 0:128], start=(k == 0), stop=(k == 7))
                P.copy('act', VA[:, t, :, 0:64], pv[:, 0:128].rearrange("p (g f) -> p g f", g=2))
                P.act(SQF, pa[:, 0:512], AF.Square)
                P.reduce('dve', SSQ, SQF.rearrange("p (h f) -> p h f", h=8))
                P.ts('dve', SSQ, SSQ, 1.0 / 64, EPS, ALU.mult, ALU.add)
                P.act(SSQ, SSQ, AF.Sqrt)
                P.recip(SSQ, SSQ)
                P.tt('dve', QN.rearrange("p (h f) -> p h f", h=8), pa[:, 0:512].rearrange("p (h f) -> p h f", h=8),
                     bcm(SSQ, 64), ALU.mult)
                P.tt('dve', QN, QN, NWB, ALU.mult)
                if t >= 2:
                    P.tt('dve', T1R.rearrange("p (h f) -> p h f", h=8), QN.rearrange("p (h f) -> p h f", h=8),
                         bch(ROPE[:, t - 2, 0, :], 8), ALU.mult)
                    P.copy('act', SINR.rearrange("p (h f) -> p h f", h=8), bch(ROPE[:, t - 2, 1, :], 8))
                    qv = QN.rearrange("p (ha s f) -> p ha s f", s=2, f=16)
                    sv = SINR.rearrange("p (ha s f) -> p ha s f", s=2, f=16)
                    tv = T2R.rearrange("p (ha s f) -> p ha s f", s=2, f=16)
                    P.tt('dve', tv[:, :, 0, :], qv[:, :, 1, :], sv[:, :, 0, :], ALU.mult)
                    P.tt('dve', tv[:, :, 1, :], qv[:, :, 0, :], sv[:, :, 1, :], ALU.mult)
                    P.tt('dve', T1R, T1R, T2R, ALU.add)
                    srcq = T1R
                else:
                    srcq = QN
                P.copy('act', DST[:, 0:128], srcq[:, 384:512])
                dq = DST[:, 128:512].rearrange("p (a g f) -> p a g f", a=3, g=2)
                for g in range(2):
                    P.copy('dve', dq[:, :, g, :], srcq[:, g * 192:(g + 1) * 192].rearrange("p (a f) -> p a f", a=3))
                ptq = nb()
                for j in range(4):
                    P.mm(ptq[:, j * 128:(j + 1) * 128], DST[:, j * 128:(j + 1) * 128], IDB)
                P.copy('act', QKT[:, :, tok], ptq.rearrange("p (j i) -> p j i", j=4))
            dump("L%d_qkt" % l, QKT)
            dump("L%d_va" % l, VA)
            if stop == 'aprep':
                break
            PT = [al([128, 512], BF16), al([128, 512], BF16), al([128, 512], BF16)]
            AO = al([128, 4, 384], F32)
            REC = SM[:, 176:180]
            qblocks = [(256 + 512 * i, 512, list(range(NT))) for i in range(4)]
            if not last:
                qblocks = [(0, 256, [0, 1])] + qblocks
            import os
            qblocks = qblocks[:int(os.environ.get('ATT_N', '99'))]
            pctr = 0
            NROT[0] = 6
            actr = 0
            for (q0, nq, ktiles) in qblocks:
                nj = nq // 128
                for g in range(2):
                    for a in range(3):
                        h = 3 * g + a
                        pacc = PS[:, 6 + actr % 2, :]
                        actr += 1
                        for kt in ktiles:
                            ps = nb()
                            P.mm(ps[:, 0:nq], QKT[g * 64:(g + 1) * 64, 0, kt * 128:(kt + 1) * 128],
                                 QKT[g * 64:(g + 1) * 64, 1 + a, q0:q0 + nq])
                            pt = PT[pctr % 3]
                            pctr += 1
                            P.act(pt[:, 0:nq], ps[:, 0:nq], AF.Exp)
                            for j in range(nj):
                                P.mm(pacc[:, j * 65:(j + 1) * 65], pt[:, j * 128:(j + 1) * 128], VA[:, kt, g, :],
                                     start=(kt == ktiles[0] and j == 0), stop=(kt == ktiles[-1] and j == nj - 1))
                        pv3 = pacc[:, 0:nj * 65].rearrange("p (j c) -> p j c", c=65)
                        P.recip(REC[:, 0:nj], pv3[:, :, 64])
                        P.tt('dve', AO[:, 0:nj, h * 64:(h + 1) * 64], pv3[:, :, 0:64], bcm(REC[:, 0:nj], 64), ALU.mult)
                tq = q0 // 128
                P.copy('act', MIX[:, tq:tq + nj, 640:1024], AO[:, 0:nj, :])
            NROT[0] = 8
            dump("L%d_mixa" % l, MIX[:])
            if stop == 'attn':
                break

            arena_off[0] = mark_gate
            WZ = al([128, 8, 1024], BF16)
            WO = al([128, 8, 1024], BF16)
            P.dma(WZ[:, :, 0:384], win_v[:, :, C_SSDZ:C_SSDZ + 384], q='pool')
            P.dma(WZ[:, :, 384:640], win_v[:, :, C_GDNZ:C_GDNZ + 256], q='pool')
            P.dma(WZ[:, :, 640:1024], win_v[:, :, C_AZ:C_AZ + 384], q='pool')
            P.dma(WO[:, :, :], wout_d[l].rearrange("(k p) n -> p k n", p=128), q='pool')
            XTL = [al([128, D], F32), al([128, D], F32)]
            ZS = al([128, 1024], F32)
            G1 = al([128, 1024], F32)
            MB = al([128, 1024], BF16)
            MT = al([128, 8, 128], BF16)
            UPD = al([128, 1024], F32)
            JK = al([128, 384], F32)
            FS = SM[:, 192:200]
            ftiles = list(range(NT)) if not last else list(range(2, NT))
            for t in ftiles:
                tok = slice(t * 128, (t + 1) * 128)
                v = 1 if t < 2 else 0
                if t >= 2:
                    xt = XTL[t % 2]
                    P.dma(xt, xsrc[(t - 2) * 128:(t - 1) * 128, :])
                else:
                    xt = XC[:, t, :]
                pz = [nb(), nb()]
                for nbk in range(2):
                    for k in range(8):
                        P.mm(pz[nbk][:, :], HT[:, k, tok], WZ[:, k, nbk * 512:(nbk + 1) * 512], start=(k == 0), stop=(k == 7))
                P.act(ZS[:, 0:512], pz[0][:, :], AF.Silu)
                P.act(ZS[:, 512:1024], pz[1][:, :], AF.Silu)
                P.tt('dve', G1[:, 0:384], MIX[:, t, 0:384], ZS[:, 0:384], ALU.mult)
                P.act(JK, G1[:, 0:384], AF.Square, accum_out=FS[:, 0:1])
                P.ts('dve', FS[:, 1:2], FS[:, 0:1], 1.0 / 384, EPS, ALU.mult, ALU.add)
                P.act(FS[:, 1:2], FS[:, 1:2], AF.Sqrt)
                P.recip(FS[:, 1:2], FS[:, 1:2])
                P.stt(MB[:, 0:384], G1[:, 0:384], FS[:, 1:2], R_SNW, ALU.mult, ALU.mult)
                P.act(JK[:, 0:256], MIX[:, t, 384:640], AF.Square)
                P.reduce('dve', FS[:, 4:8], JK[:, 0:256].rearrange("p (h f) -> p h f", h=4))
                P.ts('dve', FS[:, 4:8], FS[:, 4:8], 1.0 / 64, EPS, ALU.mult, ALU.add)
                P.act(FS[:, 4:8], FS[:, 4:8], AF.Sqrt)
                P.recip(FS[:, 4:8], FS[:, 4:8])
                g3 = G1[:, 384:640].rearrange("p (h f) -> p h f", h=4)
                P.tt('dve', g3, MIX[:, t, 384:640].rearrange("p (h f) -> p h f", h=4), bcm(FS[:, 4:8], 64), ALU.mult)
                P.tt('dve', g3, g3, bch(R_GNW, 4), ALU.mult)
                P.tt('dve', MB[:, 384:640], G1[:, 384:640], ZS[:, 384:640], ALU.mult)
                P.tt('dve', MB[:, 640:1024], MIX[:, t, 640:1024], ZS[:, 640:1024], ALU.mult)
                for half in range(2):
                    ptm = nb()
                    for j in range(4):
                        kc = half * 4 + j
                        P.mm(ptm[:, j * 128:(j + 1) * 128], MB[:, kc * 128:(kc + 1) * 128], IDB)
                    P.copy('act' if half else 'dve', MT[:, half * 4:(half + 1) * 4, :], ptm.rearrange("p (j i) -> p j i", j=4))
                for nbk in range(2):
                    po = nb()
                    for kc in range(8):
                        P.mm(po[:, :], MT[:, kc, :], WO[:, kc, nbk * 512:(nbk + 1) * 512], start=(kc == 0), stop=(kc == 7))
                    P.tt('dve', UPD[:, nbk * 512:(nbk + 1) * 512], po[:, :], GATE[:, v, nbk * 512:(nbk + 1) * 512], ALU.mult)
                if t >= 2:
                    P.tt('dve', xt, xt, UPD, ALU.add)
                    P.dma(xdst[(t - 2) * 128:(t - 1) * 128, :], xt)
                else:
                    P.tt('dve', XC[:, t, :], XC[:, t, :], UPD, ALU.add)
            dump("L%d_wz" % l, WZ)
            dump("L%d_wo" % l, WO)
            if stop == 'L0':
                dump("L%d_xc" % l, XC[:])
                break

        dump("final_mix", MIX[:])
        P.finish([out_d] + list(dbg_out.values()))
        P.emit()
    return nc, dbg_out


def prep_inputs(inputs):
    f = lambda a: np.ascontiguousarray(np.asarray(a, dtype=np.float32))
    constm, constb, rope, cm = host_consts()
    c = f(inputs['c'])
    c_ctx = f(inputs['c_ctx'])
    norm_w = f(inputs['norm_w'])
    b_mod = f(inputs['b_mod'])
    conv_w = f(inputs['conv_w'])
    conv_b = f(inputs['conv_b'])
    rows = np.zeros((2, 1024), np.float32)
    rows[:, 0:12] = f(inputs['ssd_dt_bias']).reshape(2, 12)
    rows[:, 12:24] = f(inputs['ssd_A_log']).reshape(2, 12)
    rows[:, 24:30] = f(inputs['ssd_D'])
    rows[:, 32:40] = f(inputs['gdn_A_log']).reshape(2, 8)
    rows[:, 40:48] = f(inputs['gdn_dt_bias']).reshape(2, 8)
    rows[:, 64:128] = f(inputs['gdn_norm_w'])
    rows[:, 128:192] = f(inputs['q_norm_w'])
    rows[:, 192:256] = f(inputs['k_norm_w'])
    rows[:, 256:640] = f(inputs['ssd_norm_w'])
    shared = {
        'w_mod': f(inputs['w_mod']), 'b_mod': b_mod, 'w_in': f(inputs['w_in']), 'w_out': f(inputs['w_out']),
        'normwc': np.ascontiguousarray(norm_w.reshape(2, 8, 128).transpose(2, 0, 1)),
        'bmodc': np.ascontiguousarray(b_mod[:, 0:2048].reshape(2, 16, 128).transpose(2, 0, 1)),
        'convwc': np.ascontiguousarray(conv_w.reshape(2, 3, 11, 128).transpose(3, 0, 2, 1)),
        'convbc': np.ascontiguousarray(conv_b.reshape(2, 11, 128).transpose(2, 0, 1)),
        'rows': rows, 'constm': constm, 'constb': constb, 'rope': rope, 'cm': cm,
    }
    x = f(inputs['x'])
    ctx = f(inputs['ctx'])
    maps = []
    for b in range(x.shape[0]):
        m = dict(shared)
        m['x'] = x[b]
        m['ctx'] = ctx[b]
        cc = np.stack([c[b].reshape(8, 128).T, c_ctx.reshape(8, 128).T], axis=-1)
        m['cc'] = np.ascontiguousarray(cc)
        maps.append(m)
    return maps


def kernel(**inputs):
    maps = prep_inputs(inputs)
    nc, _ = build()
    res = run_bass_kernel_spmd(nc, maps, core_ids=list(range(len(maps))))
    return np.stack([np.asarray(r["out"], dtype=np.float32) for r in res.results], axis=0)
```
